# Optimizing a Trainium2 kernel written in Bass

```python
import math
import jax, jax.numpy as jnp
from jax import lax
import numpy as np

D_MODEL = 1024
BATCH = 16
SEQ = 2048
DEPTH = 4

N_EVEN = (DEPTH + 1) // 2
N_ODD = DEPTH // 2

CONV_WIDTH = D_MODEL // 2
CONV_KERNEL = 31
SSM_WIDTH = D_MODEL // 2
SSM_GROUP = 16
SSM_GROUPS = SSM_WIDTH // SSM_GROUP
SSM_STATE = 64
EVEN_IN = 3 * CONV_WIDTH + 2 * SSM_WIDTH
EVEN_MIX = CONV_WIDTH + SSM_WIDTH
ATTN_HEAD_DIM = 128
ATTN_PATTERNS = ((128, 1), (512, 4), (2048, 16))
N_PATTERNS = len(ATTN_PATTERNS)
HEADS_PER_PATTERN = D_MODEL // ATTN_HEAD_DIM
ATTN_QKV = N_PATTERNS * HEADS_PER_PATTERN * ATTN_HEAD_DIM
ATTN_WIDTH = HEADS_PER_PATTERN * ATTN_HEAD_DIM
ODD_IN = 3 * ATTN_QKV + ATTN_WIDTH
ATTN_SCALE = ATTN_HEAD_DIM ** -0.5
ROPE_THETA = 10000.0
EPS = 1e-6
NEG_INF = -1e30

kernel_name = "hybrid_conv_s5_dilated_attn_trunk"


def rms_norm(x, w):
    xf = x.astype(jnp.float32)
    y = xf * lax.rsqrt(jnp.mean(xf * xf, axis=-1, keepdims=True) + EPS)
    return (y * w.astype(jnp.float32)).astype(x.dtype)


def layer_norm(x, w, b):
    xf = x.astype(jnp.float32)
    mu = jnp.mean(xf, axis=-1, keepdims=True)
    xc = xf - mu
    y = xc * lax.rsqrt(jnp.mean(xc * xc, axis=-1, keepdims=True) + EPS)
    return (y * w.astype(jnp.float32) + b.astype(jnp.float32)).astype(x.dtype)


def rope_tables(positions):
    inv = ROPE_THETA ** (-jnp.arange(0, ATTN_HEAD_DIM, 2, dtype=jnp.float32) / ATTN_HEAD_DIM)
    ang = positions.astype(jnp.float32)[..., None] * inv
    return jnp.cos(ang), jnp.sin(ang)


def apply_rope(t, cos, sin):
    tf = t.astype(jnp.float32)
    half = ATTN_HEAD_DIM // 2
    t1, t2 = tf[..., :half], tf[..., half:]
    c = cos[:, :, None, None, :]
    s = sin[:, :, None, None, :]
    return jnp.concatenate([t1 * c - t2 * s, t2 * c + t1 * s], axis=-1).astype(t.dtype)


def conformer_conv(a_in, dw_w, dw_b, ln_w, ln_b):
    a = a_in[..., :CONV_WIDTH] * jax.nn.sigmoid(a_in[..., CONV_WIDTH:])
    y = lax.conv_general_dilated(
        a, dw_w[:, None, :], window_strides=(1,), padding=[(CONV_KERNEL - 1, 0)],
        dimension_numbers=('NWC', 'WIO', 'NWC'), feature_group_count=CONV_WIDTH) + dw_b
    return jax.nn.silu(layer_norm(y, ln_w, ln_b))


def _complex_affine_combine(e1, e2):
    a1r, a1i, b1r, b1i = e1
    a2r, a2i, b2r, b2i = e2
    ar = a2r * a1r - a2i * a1i
    ai = a2r * a1i + a2i * a1r
    br = a2r * b1r - a2i * b1i + b2r
    bi = a2r * b1i + a2i * b1r + b2i
    return (ar, ai, br, bi)


def s5_ssm(u, lam_re, lam_im, log_dt, b_re, b_im, c_re, c_im, d_skip):
    f32 = jnp.float32
    bsz, seq_len, _ = u.shape
    uf = u.astype(f32).reshape(bsz, seq_len, SSM_GROUPS, SSM_GROUP)
    lr, li = lam_re.astype(f32), lam_im.astype(f32)
    dt = jnp.exp(log_dt.astype(f32))[:, None]
    mag = jnp.exp(lr * dt)
    ar = mag * jnp.cos(li * dt)
    ai = mag * jnp.sin(li * dt)
    den = lr * lr + li * li
    nr = ar - 1.0
    kr = (nr * lr + ai * li) / den
    ki = (ai * lr - nr * li) / den
    br, bi = b_re.astype(f32), b_im.astype(f32)
    bbr = kr[..., None] * br - ki[..., None] * bi
    bbi = kr[..., None] * bi + ki[..., None] * br
    bu_r = jnp.einsum('blgh,gph->lbgp', uf, bbr)
    bu_i = jnp.einsum('blgh,gph->lbgp', uf, bbi)
    a_r = jnp.broadcast_to(ar[None, None], (seq_len, 1, SSM_GROUPS, SSM_STATE))
    a_i = jnp.broadcast_to(ai[None, None], (seq_len, 1, SSM_GROUPS, SSM_STATE))
    _, _, xr, xi = lax.associative_scan(_complex_affine_combine, (a_r, a_i, bu_r, bu_i), axis=0)
    y = (jnp.einsum('lbgp,ghp->blgh', xr, c_re.astype(f32))
         - jnp.einsum('lbgp,ghp->blgh', xi, c_im.astype(f32)))
    y = y.reshape(bsz, seq_len, SSM_WIDTH) + d_skip.astype(f32) * uf.reshape(bsz, seq_len, SSM_WIDTH)
    return y.astype(u.dtype)


def dilated_window_attention(q, k, v, window, dilation):
    bsz, seq_len, n_heads, hd = q.shape
    n_keys = window // dilation
    sub_len = seq_len // dilation
    n_blk = -(-sub_len // n_keys)
    pad_len = n_blk * n_keys

    def to_sub(t):
        t = t.reshape(bsz, sub_len, dilation, n_heads, hd).transpose(0, 2, 3, 1, 4)
        t = jnp.pad(t, ((0, 0), (0, 0), (0, 0), (0, pad_len - sub_len), (0, 0)))
        return t.reshape(bsz, dilation, n_heads, n_blk, n_keys, hd)

    def with_prev(t):
        prev = jnp.pad(t, ((0, 0), (0, 0), (0, 0), (1, 0), (0, 0), (0, 0)))[:, :, :, :-1]
        return jnp.concatenate([prev, t], axis=4)

    qb = to_sub(q)
    kk = with_prev(to_sub(k))
    vv = with_prev(to_sub(v))
    s = jnp.einsum('bdhnqe,bdhnke->bdhnqk', qb, kk).astype(jnp.float32) * ATTN_SCALE
    blk = jnp.arange(n_blk)[:, None, None]
    qi = jnp.arange(n_keys)[None, :, None]
    kj = jnp.arange(2 * n_keys)[None, None, :]
    valid = (kj >= qi) & (kj <= qi + n_keys) & ((blk > 0) | (kj >= n_keys))
    s = jnp.where(valid, s, NEG_INF)
    m = jnp.max(s, axis=-1)
    p = jnp.exp(s - m[..., None])
    den = jnp.sum(p, axis=-1)
    o = jnp.einsum('bdhnqk,bdhnke->bdhnqe', p, vv.astype(jnp.float32))

    def back(t):
        t = t.reshape((bsz, dilation, n_heads, pad_len) + t.shape[5:])[:, :, :, :sub_len]
        perm = (0, 3, 1, 2) + tuple(range(4, t.ndim))
        return t.transpose(perm).reshape((bsz, seq_len, n_heads) + t.shape[4:])

    return back(m), back(den), back(o)


def even_mixer(h, w_in, dw_w, dw_b, ln_w, ln_b, lam_re, lam_im, log_dt,
               b_re, b_im, c_re, c_im, d_skip, glu_w, glu_b, w_out):
    proj = h @ w_in
    a_in = proj[..., :2 * CONV_WIDTH]
    a_z = proj[..., 2 * CONV_WIDTH:3 * CONV_WIDTH]
    u = proj[..., 3 * CONV_WIDTH:3 * CONV_WIDTH + SSM_WIDTH]
    b_z = proj[..., 3 * CONV_WIDTH + SSM_WIDTH:]
    ya = conformer_conv(a_in, dw_w, dw_b, ln_w, ln_b) * jax.nn.silu(a_z)
    ys = jax.nn.gelu(s5_ssm(u, lam_re, lam_im, log_dt, b_re, b_im, c_re, c_im, d_skip), approximate=False)
    ys = ys * jax.nn.sigmoid(ys @ glu_w + glu_b)
    ys = ys * jax.nn.silu(b_z)
    return jnp.concatenate([ya, ys], axis=-1) @ w_out


def odd_mixer(h, cos, sin, w_in, q_norm_w, k_norm_w, w_out):
    bsz, seq_len, _ = h.shape
    proj = h @ w_in
    shp = (bsz, seq_len, N_PATTERNS, HEADS_PER_PATTERN, ATTN_HEAD_DIM)
    q = proj[..., :ATTN_QKV].reshape(shp)
    k = proj[..., ATTN_QKV:2 * ATTN_QKV].reshape(shp)
    v = proj[..., 2 * ATTN_QKV:3 * ATTN_QKV].reshape(shp)
    z = proj[..., 3 * ATTN_QKV:]
    q = apply_rope(rms_norm(q, q_norm_w), cos, sin)
    k = apply_rope(rms_norm(k, k_norm_w), cos, sin)
    stats = [dilated_window_attention(q[:, :, g], k[:, :, g], v[:, :, g], win, dil)
             for g, (win, dil) in enumerate(ATTN_PATTERNS)]
    m_all = jnp.stack([st[0] for st in stats])
    s_all = jnp.stack([st[1] for st in stats])
    o_all = jnp.stack([st[2] for st in stats])
    wgt = jnp.exp(m_all - jnp.max(m_all, axis=0, keepdims=True))
    den = jnp.sum(wgt * s_all, axis=0)
    o = jnp.sum(wgt[..., None] * o_all, axis=0) / den[..., None]
    o = o.reshape(bsz, seq_len, ATTN_WIDTH).astype(h.dtype) * jax.nn.silu(z)
    return o @ w_out


def setup_inputs(seed: int = 0) -> dict:
    key = jax.random.key(seed)
    ks = jax.random.split(key, 32)
    f32 = jnp.float32
    nrm = lambda k, shp, sc: jax.random.normal(k, shp, f32) * sc
    x = nrm(ks[0], (BATCH, SEQ, D_MODEL), 1.0)
    c = nrm(ks[1], (BATCH, D_MODEL), 1.0)
    offset = jax.random.randint(ks[2], (BATCH, 1), 0, 4096)
    positions = (offset + jnp.arange(SEQ)[None, :]).astype(jnp.int32)
    mod_w = nrm(ks[3], (DEPTH, D_MODEL, 3 * D_MODEL), 0.1 * D_MODEL ** -0.5)
    mod_b = jnp.concatenate([nrm(ks[4], (DEPTH, 2 * D_MODEL), 0.02),
                             1.0 + nrm(ks[5], (DEPTH, D_MODEL), 0.02)], axis=-1)
    norm_w = 1.0 + nrm(ks[6], (DEPTH, D_MODEL), 0.02)
    even_w_in = nrm(ks[7], (N_EVEN, D_MODEL, EVEN_IN), D_MODEL ** -0.5)
    conv_dw_w = nrm(ks[8], (N_EVEN, CONV_KERNEL, CONV_WIDTH), CONV_KERNEL ** -0.5)
    conv_dw_b = nrm(ks[9], (N_EVEN, CONV_WIDTH), 0.02)
    conv_ln_w = 1.0 + nrm(ks[10], (N_EVEN, CONV_WIDTH), 0.02)
    conv_ln_b = nrm(ks[11], (N_EVEN, CONV_WIDTH), 0.02)
    ssm_lam_re = -0.5 + nrm(ks[12], (N_EVEN, SSM_GROUPS, SSM_STATE), 0.01)
    ssm_lam_im = (jnp.pi * jnp.arange(SSM_STATE, dtype=f32))[None, None, :] + nrm(ks[13], (N_EVEN, SSM_GROUPS, SSM_STATE), 0.01)
    ssm_log_dt = jax.random.uniform(ks[14], (N_EVEN, SSM_GROUPS), f32, math.log(1e-3), math.log(1e-1))
    ssm_b_re = nrm(ks[15], (N_EVEN, SSM_GROUPS, SSM_STATE, SSM_GROUP), (2 * SSM_GROUP) ** -0.5)
    ssm_b_im = nrm(ks[16], (N_EVEN, SSM_GROUPS, SSM_STATE, SSM_GROUP), (2 * SSM_GROUP) ** -0.5)
    ssm_c_re = nrm(ks[17], (N_EVEN, SSM_GROUPS, SSM_GROUP, SSM_STATE), (2 * SSM_STATE) ** -0.5)
    ssm_c_im = nrm(ks[18], (N_EVEN, SSM_GROUPS, SSM_GROUP, SSM_STATE), (2 * SSM_STATE) ** -0.5)
    ssm_d = nrm(ks[19], (N_EVEN, SSM_WIDTH), 1.0)
    ssm_glu_w = nrm(ks[20], (N_EVEN, SSM_WIDTH, SSM_WIDTH), SSM_WIDTH ** -0.5)
    ssm_glu_b = nrm(ks[21], (N_EVEN, SSM_WIDTH), 0.02)
    even_w_out = nrm(ks[22], (N_EVEN, EVEN_MIX, D_MODEL), EVEN_MIX ** -0.5)
    attn_w_in = nrm(ks[23], (N_ODD, D_MODEL, ODD_IN), D_MODEL ** -0.5)
    attn_q_norm_w = 1.0 + nrm(ks[24], (N_ODD, ATTN_HEAD_DIM), 0.02)
    attn_k_norm_w = 1.0 + nrm(ks[25], (N_ODD, ATTN_HEAD_DIM), 0.02)
    attn_w_out = nrm(ks[26], (N_ODD, ATTN_WIDTH, D_MODEL), ATTN_WIDTH ** -0.5)
    return {"x": x, "c": c, "positions": positions, "mod_w": mod_w, "mod_b": mod_b,
            "norm_w": norm_w, "even_w_in": even_w_in, "conv_dw_w": conv_dw_w,
            "conv_dw_b": conv_dw_b, "conv_ln_w": conv_ln_w, "conv_ln_b": conv_ln_b,
            "ssm_lam_re": ssm_lam_re, "ssm_lam_im": ssm_lam_im, "ssm_log_dt": ssm_log_dt,
            "ssm_b_re": ssm_b_re, "ssm_b_im": ssm_b_im, "ssm_c_re": ssm_c_re,
            "ssm_c_im": ssm_c_im, "ssm_d": ssm_d, "ssm_glu_w": ssm_glu_w,
            "ssm_glu_b": ssm_glu_b, "even_w_out": even_w_out, "attn_w_in": attn_w_in,
            "attn_q_norm_w": attn_q_norm_w, "attn_k_norm_w": attn_k_norm_w,
            "attn_w_out": attn_w_out}


def reference(x, c, positions, mod_w, mod_b, norm_w, even_w_in, conv_dw_w, conv_dw_b,
              conv_ln_w, conv_ln_b, ssm_lam_re, ssm_lam_im, ssm_log_dt, ssm_b_re, ssm_b_im,
              ssm_c_re, ssm_c_im, ssm_d, ssm_glu_w, ssm_glu_b, even_w_out, attn_w_in,
              attn_q_norm_w, attn_k_norm_w, attn_w_out):
    cos, sin = rope_tables(positions)
    for layer in range(DEPTH):
        mod = c @ mod_w[layer] + mod_b[layer]
        shift = mod[:, None, :D_MODEL]
        scale = mod[:, None, D_MODEL:2 * D_MODEL]
        gate = mod[:, None, 2 * D_MODEL:]
        h = rms_norm(x, norm_w[layer]) * (1.0 + scale) + shift
        i = layer // 2
        if layer % 2 == 0:
            out = even_mixer(h, even_w_in[i], conv_dw_w[i], conv_dw_b[i], conv_ln_w[i], conv_ln_b[i],
                             ssm_lam_re[i], ssm_lam_im[i], ssm_log_dt[i], ssm_b_re[i], ssm_b_im[i],
                             ssm_c_re[i], ssm_c_im[i], ssm_d[i], ssm_glu_w[i], ssm_glu_b[i],
                             even_w_out[i])
        else:
            out = odd_mixer(h, cos, sin, attn_w_in[i], attn_q_norm_w[i], attn_k_norm_w[i],
                            attn_w_out[i])
        x = x + gate * out
    return x
```

```python
import math
import numpy as np
from contextlib import ExitStack
import concourse.bass as bass
import concourse.mybir as mybir
from concourse.bass_utils import run_bass_kernel_spmd

F32 = mybir.dt.float32
BF16 = mybir.dt.bfloat16
I32 = mybir.dt.int32
AF = mybir.ActivationFunctionType
ALU = mybir.AluOpType

NCORES = 8
SEQ = 2048
D = 1024
KC = 8
SPAN = 512
NSPAN = SEQ // SPAN
EPS = 1e-6
PI = math.pi
ENGS = ("pe", "act", "dve", "pool", "sp")
NDMASEM = 6


class Op:
    __slots__ = ("eng", "fn", "deps", "sig", "count", "is_dma", "dsem", "dval", "emitted")

    def __init__(self, eng, fn, is_dma=False):
        self.eng = eng
        self.fn = fn
        self.deps = []
        self.sig = False
        self.count = None
        self.is_dma = is_dma
        self.dsem = None
        self.dval = None
        self.emitted = False


class Prog:
    def __init__(self, nc, es):
        self.nc = nc
        self.ops = {e: [] for e in ENGS}
        self.last_w = {}
        self.readers = {}
        self.last_acc = {}
        self.sems = {e: es.enter_context(nc.semaphore("s_" + e)) for e in ENGS}
        self.dq = ("sp", "act", "pool")
        self.dsems = {e: [es.enter_context(nc.semaphore("d_%s%d" % (e, i))) for i in range(NDMASEM)]
                      for e in self.dq}
        self.dma_n = {e: 0 for e in self.dq}
        self.dma_hist = {e: [] for e in self.dq}

    def _track(self, op, reads, writes):
        deps = set()
        for k in reads:
            w = self.last_w.get(k)
            if w is not None:
                deps.add(w)
        for k in writes:
            w = self.last_w.get(k)
            if w is not None:
                deps.add(w)
            for r in self.readers.get(k, ()):
                deps.add(r)
        for k in list(reads) + list(writes):
            if k.startswith("bank"):
                a = self.last_acc.get(k)
                if a is not None and a.eng != op.eng:
                    deps.add(a)
                self.last_acc[k] = op
        deps.discard(op)
        op.deps = list(deps)
        for k in reads:
            self.readers.setdefault(k, []).append(op)
        for k in writes:
            self.last_w[k] = op
            self.readers[k] = []

    def op(self, eng, fn, reads=(), writes=()):
        o = Op(eng, fn)
        self._track(o, reads, writes)
        self.ops[eng].append(o)
        return o

    def dma(self, q, fn, reads=(), writes=()):
        o = Op(q, fn, is_dma=True)
        self._track(o, reads, writes)
        n = self.dma_n[q]
        self.dma_n[q] = n + 1
        o.dsem = self.dsems[q][n % NDMASEM]
        o.dval = 16 * (n // NDMASEM + 1)
        hist = self.dma_hist[q]
        if n >= NDMASEM:
            o.deps.append(hist[n - NDMASEM])
        hist.append(o)
        self.ops[q].append(o)
        return o

    def emit(self, block):
        if not hasattr(self, "cnt"):
            self.cnt = {e: 0 for e in ENGS}
            self.waited = {e: {} for e in ENGS}
        for e in ENGS:
            for o in self.ops[e]:
                for d in o.deps:
                    if not getattr(d, "emitted", False) and not (d.eng == "pe" and o.eng == "pe"):
                        d.sig = True
        for e in ENGS:
            for o in self.ops[e]:
                if not o.is_dma and o.sig:
                    self.cnt[e] += 1
                    o.count = self.cnt[e]
        engobj = {"pe": "tensor", "act": "scalar", "dve": "vector", "pool": "gpsimd", "sp": "sync"}
        sems = self.sems
        todo = {e: self.ops[e] for e in ENGS}
        self.ops = {e: [] for e in ENGS}

        def replay(e, eng):
            waited = self.waited[e]
            for o in todo[e]:
                need = {}
                for d in o.deps:
                    if d.is_dma:
                        key = ("d", d.eng, id(d.dsem))
                        if need.get(key, (None, 0))[1] < d.dval:
                            need[key] = (d.dsem, d.dval)
                    else:
                        if d.count is None:
                            continue
                        if d.eng == "pe" and e == "pe":
                            continue
                        key = ("e", d.eng)
                        if need.get(key, (None, 0))[1] < d.count:
                            need[key] = (sems[d.eng], d.count)
                for key, (sem, val) in need.items():
                    if waited.get(key, 0) >= val:
                        continue
                    eng.wait_ge(sem, val)
                    waited[key] = val
                ins = o.fn(eng)
                if o.is_dma:
                    ins.then_inc(o.dsem, 16)
                elif o.sig:
                    ins.then_inc(sems[e], 1)
                o.emitted = True
                o.fn = None

        for e in ENGS:
            def body(eng, e=e):
                replay(e, eng)
            getattr(block, engobj[e])(body)


def blk_cols(w, bw=128):
    k = w.shape[0] // 128
    n = w.shape[1] // bw
    return np.ascontiguousarray(w.reshape(k, 128, n, bw).transpose(2, 1, 0, 3))


def colvec(v, nch):
    return np.ascontiguousarray(v.reshape(nch, 128).T)


def host_consts():
    c = {}
    c["ident"] = np.eye(128, dtype=np.float32)
    kq = np.arange(128)
    c["maskc"] = (kq[:, None] <= kq[None, :]).astype(np.float32)
    c["maskw"] = (kq[:, None] >= kq[None, :]).astype(np.float32)
    perm = np.zeros((128, 128), np.float32)
    for e2 in range(128):
        perm[(e2 + 64) % 128, e2] = 1.0
    c["perm"] = perm
    inv = (np.float32(10000.0) ** (-np.arange(0, 128, 2, dtype=np.float32) / np.float32(128))).astype(np.float32)
    misc = np.zeros((128, 8), np.float32)
    misc[:, 0] = np.concatenate([inv, inv])
    misc[:64, 1] = -1.0
    misc[64:, 1] = 1.0
    gl = (np.arange(128) // 16)
    misc[:, 2] = (gl % 2 == 0)
    misc[:, 3] = (gl % 2 == 1)
    c["misc"] = misc
    return c


class Ctx:
    pass


DBG = {"odd_stop": 99, "prep": 99, "even_stop": 99, "setup_stop": 99}


def build(stages):
    nc = bass.Bass("TRN2", target_bir_lowering=False)
    C = Ctx()
    C.nc = nc

    def din(name, shape, dt=F32):
        return nc.dram_tensor(name, list(shape), dt, kind="ExternalInput").ap()

    C.x_in = din("x", [2, SEQ, D])
    C.cT = din("cT", [128, KC, 2])
    C.pos = din("pos", [2, SEQ], I32)
    C.modw = din("modw", [4, D, 3 * D])
    C.modb = din("modb", [128, 4, 24])
    C.normw = din("normw", [128, 4, KC])
    C.ident_d = din("ident", [128, 128])
    C.maskc_d = din("maskc", [128, 128])
    C.maskw_d = din("maskw", [128, 128])
    C.perm_d = din("perm", [128, 128])
    C.misc_d = din("misc", [128, 8])
    C.attw = din("attw", [2, 80, 128, KC * 128])
    C.attwo = din("attwo", [2, 8, 128, KC * 128])
    C.qnw = din("qnw", [128, 2, 2])
    C.knw = din("knw", [128, 2, 2])
    C.evw = din("evw", [2, 20, 128, KC * 128])
    C.evwo = din("evwo", [2, 8, 128, KC * 128])
    C.gluw = din("gluw", [2, 128, 4, 512])
    C.convw = din("convw", [128, 2, 4, 31])
    C.cvec = din("cvec", [128, 2, 5, 4])
    C.s5l1 = din("s5l1", [128, 2, 5, 256])
    C.s5l2a = din("s5l2a", [128, 2, 3, 16])
    C.s5l2b = din("s5l2b", [128, 2, 4, 256])
    C.y_out = nc.dram_tensor("y", [2, SEQ, D], F32, kind="ExternalOutput").ap()
    C.xs = [nc.dram_tensor("xs%d" % i, [2, 128, KC, SEQ], F32, kind="Internal").ap() for i in range(2)]
    C.ats = nc.dram_tensor("ats", [2, 128, KC, SEQ], BF16, kind="Internal").ap()

    with ExitStack() as es:
        P = Prog(nc, es)
        C.P = P

        def sb(name, shape, dt=F32):
            return es.enter_context(nc.sbuf_tensor("sb_" + name, list(shape), dt))

        C.banks = [es.enter_context(nc.psum_tensor("bank%d" % i, [128, 512], F32)) for i in range(8)]
        C.ident = sb("ident", [128, 128])
        C.ones_bf = sb("ones_bf", [128, 128], BF16)
        C.ones_f = sb("ones_f", [128, 128])
        C.ident_bf = sb("ident_bf", [128, 128], BF16)
        C.perm_bf = sb("perm_bf", [128, 128], BF16)
        C.maskc = sb("maskc", [128, 128], BF16)
        C.maskw = sb("maskw", [128, 128], BF16)
        C.misc = sb("misc", [128, 8])
        C.modS = sb("modS", [128, 4, 2, 24])
        C.aS = sb("aS", [128, 4, 2, KC])
        C.normw_s = sb("normw_s", [128, 4, KC])
        C.modb_s = sb("modb_s", [128, 4, 24])
        C.cT_s = sb("cT_s", [128, KC, 2])
        C.qnw_s = sb("qnw_s", [128, 2, 2])
        C.knw_s = sb("knw_s", [128, 2, 2])
        C.eps_col = sb("eps_col", [128, 4])
        C.xt = sb("xt", [128, KC, SPAN])
        C.cur = 0
        C.outs = []

        layers = [s for s in stages if isinstance(s, int)]
        stages = list(stages)
        es_pro = ExitStack()
        gpro = stage_prologue(C, es_pro, layers)
        idx = 0
        if stages and stages[0] == "xpose":
            es_x = ExitStack()
            gx = stage_xpose(C, es_x, C.xs[C.cur])
            alive = [gpro, gx, gx]
            while alive:
                for g_ in list(alive):
                    try:
                        next(g_)
                    except StopIteration:
                        alive = [a_ for a_ in alive if a_ is not g_]
            idx = 1
            with nc.Block() as block:
                P.emit(block)
            es_x.close()
            es_pro.close()
        else:
            for _ in gpro:
                pass
            with nc.Block() as block:
                P.emit(block)
            es_pro.close()
        skip_final = [False]
        for si_ in range(idx, len(stages)):
            st = stages[si_]
            with ExitStack() as les:
                if st == "xpose":
                    for _ in stage_xpose(C, les, C.xs[C.cur]):
                        pass
                elif st == "final":
                    if not skip_final[0]:
                        stage_final(C, les, C.xs[C.cur])
                elif isinstance(st, int):
                    if st % 2 == 1:
                        ff = (si_ + 1 < len(stages) and stages[si_ + 1] == "final")
                        stage_odd(C, les, st, C.xs[C.cur], C.xs[1 - C.cur], fuse_final=ff)
                        if ff:
                            skip_final[0] = True
                    else:
                        with ExitStack() as ses:
                            W = even_setup(C, les, ses, st)
                            with nc.Block() as block:
                                P.emit(block)
                        stage_even(C, les, st, W, C.xs[C.cur], C.xs[1 - C.cur])
                    C.cur = 1 - C.cur
                if si_ == len(stages) - 1:
                    P.op("sp", lambda e: e.nop(), reads=["yout%d" % i for i in range(len(C.outs))])
                with nc.Block() as block:
                    P.emit(block)
    return nc


def bk(i):
    return "bank%d" % i


def stage_prologue(C, les, layers):
    P, nc = C.P, C.nc
    P.dma("sp", lambda e: e.dma_start(out=C.ident[:], in_=C.ident_d), writes=["ident"])
    P.dma("pool", lambda e: e.dma_start(out=C.perm_bf[:], in_=C.perm_d), writes=["perm_bf"])
    P.dma("pool", lambda e: e.dma_start(out=C.ident_bf[:], in_=C.ident_d), writes=["ident_bf"])
    P.dma("pool", lambda e: e.dma_start(out=C.maskc[:], in_=C.maskc_d), writes=["maskc"])
    P.dma("pool", lambda e: e.dma_start(out=C.maskw[:], in_=C.maskw_d), writes=["maskw"])
    P.dma("sp", lambda e: e.dma_start(out=C.misc[:], in_=C.misc_d), writes=["misc"])
    P.dma("sp", lambda e: e.dma_start(out=C.normw_s[:], in_=C.normw), writes=["normw_s"])
    P.dma("sp", lambda e: e.dma_start(out=C.modb_s[:], in_=C.modb), writes=["modb_s"])
    P.dma("sp", lambda e: e.dma_start(out=C.cT_s[:], in_=C.cT), writes=["cT_s"])
    P.dma("sp", lambda e: e.dma_start(out=C.qnw_s[:], in_=C.qnw), writes=["qnw_s"])
    P.dma("sp", lambda e: e.dma_start(out=C.knw_s[:], in_=C.knw), writes=["knw_s"])
    P.op("dve", lambda e: e.memset(C.ones_bf[:], 1.0), writes=["ones_bf"])
    P.op("dve", lambda e: e.memset(C.ones_f[:], 1.0), writes=["ones_f"])
    P.op("dve", lambda e: e.memset(C.eps_col[:, 0:1], EPS), writes=["eps_col"])
    P.op("dve", lambda e: e.memset(C.eps_col[:, 1:2], 128.0 * EPS), writes=["eps_col"])
    if not layers:
        yield
        return
    banks = C.banks
    mwb = [les.enter_context(nc.sbuf_tensor("mwb%d" % i, [128, KC, 384], F32)) for i in range(2)]
    nb = 0
    for l in layers:
        for cb in range(8):
            slot = nb % 2
            nb += 1
            src = C.modw[l].rearrange("(k p) c -> p k c", p=128)[:, :, cb * 384:(cb + 1) * 384]
            P.dma("sp", lambda e, slot=slot, src=src: e.dma_start(out=mwb[slot][:], in_=src),
                  writes=["mwb%d" % slot])
            for j3 in range(3):
                j = cb * 3 + j3
                for k in range(KC):
                    P.op("pe", lambda e, slot=slot, j3=j3, j=j, k=k, l=l: e.matmul(
                        banks[7][:, (l * 24 + j) * 2:(l * 24 + j) * 2 + 2],
                        lhsT=mwb[slot][:, k, j3 * 128:(j3 + 1) * 128], rhs=C.cT_s[:, k, :],
                        start=(k == 0), stop=(k == KC - 1)),
                        reads=["mwb%d" % slot, "cT_s"], writes=[bk(7)])
            yield
    for l in layers:
        for s in range(2):
            src = banks[7][:, l * 48:(l + 1) * 48].rearrange("p (j s) -> p s j", s=2)[:, s, :]
            P.op("dve", lambda e, l=l, s=s, src=src: e.tensor_tensor(
                out=C.modS[:, l, s, :], in0=src, in1=C.modb_s[:, l, :], op=ALU.add),
                reads=[bk(7), "modb_s"], writes=["modS"])
            P.op("dve", lambda e, l=l, s=s: e.scalar_tensor_tensor(
                out=C.aS[:, l, s, :], in0=C.modS[:, l, s, 8:16], scalar=1.0, in1=C.normw_s[:, l, :],
                op0=ALU.add, op1=ALU.mult),
                reads=["modS", "normw_s"], writes=["aS"])


def scr_key(buf, s, sp_i):
    return "scr_%s_%d_%d" % (buf.tensor.name, s, sp_i)


def stage_xpose(C, les, dst):
    P, nc, banks = C.P, C.nc, C.banks
    xtok = [les.enter_context(nc.sbuf_tensor("xtok%d_%d" % (i, id(les) % 100000), [128, D], F32)) for i in range(2)]
    n = 0
    for s in range(2):
        for sp_i in range(NSPAN):
            for tt in range(4):
                slot = n % 2
                n += 1
                t0 = sp_i * SPAN + tt * 128
                P.dma("sp", lambda e, slot=slot, s=s, t0=t0: e.dma_start(
                    out=xtok[slot][:], in_=C.x_in[s, t0:t0 + 128, :]), writes=["xtok%d" % slot])
                for half in range(2):
                    bnk = (n * 2 + half) % 4
                    for kk in range(4):
                        k = half * 4 + kk
                        P.op("pe", lambda e, slot=slot, k=k, kk=kk, bnk=bnk: e.transpose(
                            banks[bnk][:, kk * 128:(kk + 1) * 128],
                            xtok[slot][:, k * 128:(k + 1) * 128], C.ident[:]),
                            reads=["xtok%d" % slot, "ident"], writes=[bk(bnk)])
                    src = banks[bnk][:].rearrange("p (k t) -> p k t", k=4)
                    dstap = C.xt[:, half * 4:(half + 1) * 4, tt * 128:(tt + 1) * 128]
                    if half == 0:
                        P.op("act", lambda e, src=src, dstap=dstap: e.copy(out=dstap, in_=src),
                             reads=[bk(bnk)], writes=["xt"])
                    else:
                        P.op("dve", lambda e, src=src, dstap=dstap: e.tensor_copy(out=dstap, in_=src),
                             reads=[bk(bnk)], writes=["xt"])
                yield
            P.dma("sp", lambda e, s=s, sp_i=sp_i: e.dma_start(
                out=dst[s, :, :, sp_i * SPAN:(sp_i + 1) * SPAN], in_=C.xt[:]),
                reads=["xt"], writes=[scr_key(dst, s, sp_i)])


def stage_final(C, les, src_scr):
    P, nc, banks = C.P, C.nc, C.banks
    ytok = [les.enter_context(nc.sbuf_tensor("ytok%d_%d" % (i, id(les) % 100000), [128, D], F32)) for i in range(2)]
    n = 0
    for s in range(2):
        for sp_i in range(NSPAN):
            P.dma("sp", lambda e, s=s, sp_i=sp_i: e.dma_start(
                out=C.xt[:], in_=src_scr[s, :, :, sp_i * SPAN:(sp_i + 1) * SPAN]),
                reads=[scr_key(src_scr, s, sp_i)], writes=["xt"])
            for tt in range(4):
                slot = n % 2
                n += 1
                for half in range(2):
                    bnk = (n * 2 + half) % 4
                    for kk in range(4):
                        k = half * 4 + kk
                        P.op("pe", lambda e, k=k, kk=kk, bnk=bnk, tt=tt: e.transpose(
                            banks[bnk][:, kk * 128:(kk + 1) * 128],
                            C.xt[:, k, tt * 128:(tt + 1) * 128], C.ident[:]),
                            reads=["xt", "ident"], writes=[bk(bnk)])
                    dstap = ytok[slot][:, half * 512:(half + 1) * 512]
                    if half == 0:
                        P.op("act", lambda e, bnk=bnk, dstap=dstap: e.copy(out=dstap, in_=banks[bnk][:]),
                             reads=[bk(bnk)], writes=["ytok%d" % slot])
                    else:
                        P.op("dve", lambda e, bnk=bnk, dstap=dstap: e.tensor_copy(out=dstap, in_=banks[bnk][:]),
                             reads=[bk(bnk)], writes=["ytok%d" % slot])
                t0 = sp_i * SPAN + tt * 128
                key = "yout%d" % len(C.outs)
                C.outs.append(P.dma("sp", lambda e, slot=slot, s=s, t0=t0: e.dma_start(
                    out=C.y_out[s, t0:t0 + 128, :], in_=ytok[slot][:]),
                    reads=["ytok%d" % slot], writes=[key]))


def norm_span_gen(C, l, s, hdst, hkey, T):
    P, banks = C.P, C.banks
    ssb = 2
    kln, krs, kt = T.get("kln", "n_ln"), T.get("krs", "n_rs"), T.get("kt", ["n_t0", "n_t1"])
    for k in range(KC):
        i = k % 2
        P.op("act", lambda e, k=k, i=i: e.activation(out=T["sq"][i][:], in_=C.xt[:, k, :], func=AF.Square),
             reads=["xt"], writes=["n_sq%d" % i])
        P.op("pe", lambda e, k=k, i=i: e.matmul(banks[ssb][:], lhsT=C.ones_bf[:], rhs=T["sq"][i][:],
                                                start=(k == 0), stop=(k == KC - 1)),
             reads=["n_sq%d" % i, "ones_bf"], writes=[bk(ssb)])
        yield
    P.op("act", lambda e: e.activation(out=T["ln"][:], in_=banks[ssb][:], func=AF.Ln,
                                       scale=1.0 / D, bias=C.eps_col[:, 0:1]),
         reads=[bk(ssb), "eps_col"], writes=[kln])
    P.op("act", lambda e: e.activation(out=T["rs"][:], in_=T["ln"][:], func=AF.Exp, scale=-0.5),
         reads=[kln], writes=[krs])
    yield
    for k in range(KC):
        ti = k % 2
        P.op("dve", lambda e, k=k, ti=ti: e.scalar_tensor_tensor(
            out=T["t"][ti][:], in0=C.xt[:, k, :], scalar=C.aS[:, l, s, k:k + 1], in1=T["rs"][:],
            op0=ALU.mult, op1=ALU.mult),
            reads=["xt", "aS", krs], writes=[kt[ti]])
        P.op("act", lambda e, k=k, ti=ti: e.activation(
            out=hdst[:, k, :], in_=T["t"][ti][:], func=AF.Identity, bias=C.modS[:, l, s, k:k + 1]),
            reads=[kt[ti], "modS"], writes=[hkey])
        yield


def norm_span(C, l, s, hdst, hkey, T):
    for _ in norm_span_gen(C, l, s, hdst, hkey, T):
        pass


class Ring:
    def __init__(self, C, tiles, name, plan, src_fn, look):
        self.C, self.tiles, self.name, self.plan, self.src_fn, self.look = C, tiles, name, plan, src_fn, look
        self.issued = 0
        self.n = len(tiles)

    def get(self, i):
        P = self.C.P
        while self.issued <= min(i + self.look, len(self.plan) - 1):
            j = self.issued
            slot = j % self.n
            src = self.src_fn(self.plan[j])
            t = self.tiles[slot]
            P.dma("pool", lambda e, t=t, src=src: e.dma_start(out=t[:], in_=src),
                  writes=["%s%d" % (self.name, slot)])
            self.issued += 1
        slot = i % self.n
        return self.tiles[slot], "%s%d" % (self.name, slot)


def stage_odd(C, les, l, src_scr, dst_scr, fuse_final=False):
    P, nc, banks = C.P, C.nc, C.banks
    li = l // 2

    def lsb(name, shape, dt=F32):
        return les.enter_context(nc.sbuf_tensor("o%d_%s" % (l, name), list(shape), dt))

    hT = lsb("hT", [128, KC, SEQ], BF16)
    cosT = lsb("cosT", [128, SEQ])
    sinS = lsb("sinS", [128, SEQ])
    vt = [lsb("vt%d" % g, [128, 16, 256], BF16) for g in range(3)]
    qT = [lsb("qT%d" % g, [128, SEQ], BF16) for g in range(3)]
    kT = [lsb("kT%d" % g, [128, SEQ], BF16) for g in range(3)]
    wv_t = [lsb("wv%d" % i, [128, 2, KC, 128], BF16) for i in range(2)]
    wq_t = [lsb("wq%d" % i, [128, KC, 128], BF16) for i in range(5)]
    sq = [lsb("sq%d" % i, [128, SPAN], BF16) for i in range(2)]
    qb = [lsb("qb%d" % i, [128, SPAN], BF16) for i in range(2)]
    t1 = [lsb("t1_%d" % i, [128, SPAN]) for i in range(4)]
    t2 = [lsb("t2_%d" % i, [128, SPAN]) for i in range(3)]
    lnv = [lsb("lnv%d" % i, [128, SPAN]) for i in range(2)]
    rs = [lsb("rs%d" % i, [128, SPAN]) for i in range(3)]
    pb = [lsb("pb%d" % i, [128, SPAN], BF16) for i in range(3)]
    pm = [lsb("pm%d" % i, [128, SPAN], BF16) for i in range(3)]
    szt = [lsb("szt%d" % i, [128, SPAN], BF16) for i in range(4)]
    rden = lsb("rden", [128, SPAN])
    otmp = lsb("otmp", [128, SPAN])
    ao = [lsb("ao%d" % i, [128, SPAN], BF16) for i in range(2)]
    atl = [lsb("atl%d" % i, [128, KC, SPAN], BF16) for i in range(2)]
    NT = {"sq": sq, "ln": lnv[0], "rs": rs[0], "t": t1}
    mk4c = lsb("mk4c", [128, SPAN], BF16)
    mk4w = lsb("mk4w", [128, SPAN], BF16)
    mk2 = [lsb("mk2_%d" % b_, [128, SPAN], BF16) for b_ in range(NSPAN)]
    P.op("pool", lambda e: e.tensor_copy(out=mk4c[:].rearrange("p (u n) -> p u n", u=4),
                                         in_=C.maskc[:].unsqueeze(1).broadcast_to([128, 4, 128])),
         reads=["maskc"], writes=["mk4c"])
    P.op("pool", lambda e: e.tensor_copy(out=mk4w[:].rearrange("p (u n) -> p u n", u=4),
                                         in_=C.maskw[:].unsqueeze(1).broadcast_to([128, 4, 128])),
         reads=["maskw"], writes=["mk4w"])
    for b_ in range(NSPAN):
        P.op("pool", lambda e, b_=b_: e.tensor_copy(
            out=mk2[b_][:].rearrange("p (u n) -> p u n", u=16),
            in_=C.maskc[:, 32 * b_:32 * b_ + 32].unsqueeze(1).broadcast_to([128, 16, 32])),
            reads=["maskc"], writes=["mk2_%d" % b_])
    lnk = ["n_ln", "lnv1"]
    rsk = ["n_rs", "rs1", "rs2"]
    t1k = ["n_t0", "n_t1", "t1_2", "t1_3"]

    wq_plan, wv_plan = [], []
    for s in range(2):
        for hp in range(4):
            for g in range(3):
                wv_plan.append(48 + g * 8 + hp * 2)
            for hh in range(2):
                h = hp * 2 + hh
                for g in range(3):
                    for which in range(2):
                        wq_plan.append(("in", which * 24 + g * 8 + h))
                wq_plan.append(("in", 72 + h))
        for sp_i in range(NSPAN):
            for oc in range(KC):
                wq_plan.append(("out", oc))
    wq_ring = Ring(C, wq_t, "wq", wq_plan,
                   lambda d: (C.attw[li, d[1]] if d[0] == "in" else C.attwo[li, d[1]]).rearrange("p (k c) -> p k c", k=KC), 3)
    wv_ring = Ring(C, wv_t, "wv", wv_plan,
                   lambda b0: C.attw[li, b0:b0 + 2].rearrange("b p (k c) -> p b k c", k=KC), 1)
    cnt = {"wq": 0, "wv": 0, "prep": 0, "pp": 0, "st": 0, "sz": 0, "ao": 0, "atl": 0, "sp": 0, "mk": 0, "yt": 0, "ytb": 0}

    PPB = (0, 1, 2, 7)
    SSB = (3, 4)
    PQB = (5, 6)

    def prep_gen(spec):
        n = cnt["prep"]
        cnt["prep"] += 1
        i2, i3, i4 = n % 2, n % 3, n % 4
        ppb = PPB[n % 4]
        ssb = SSB[n % 2]
        pqb = PQB[n % 2]
        blk = spec["blk"]
        if "wt" not in blk:
            blk["wt"], blk["wk"] = wq_ring.get(cnt["wq"])
            cnt["wq"] += 1
        wt, wk = blk["wt"], blk["wk"]
        sp_i = spec["sp_i"]
        c0 = sp_i * SPAN
        wcol, wpcol, dst, dkey = spec["wcol"], spec["wpcol"], spec["dst"], spec["dkey"]
        for k in range(KC):
            P.op("pe", lambda e, k=k: e.matmul(
                banks[ppb][:], lhsT=wt[:, k, :], rhs=hT[:, k, c0:c0 + SPAN],
                start=(k == 0), stop=(k == KC - 1)),
                reads=["hT", wk], writes=[bk(ppb)])
        yield
        P.op("act", lambda e: e.activation(out=sq[i2][:], in_=banks[ppb][:], func=AF.Square),
             reads=[bk(ppb)], writes=["n_sq%d" % i2])
        P.op("act", lambda e: e.copy(out=qb[i2][:], in_=banks[ppb][:]),
             reads=[bk(ppb)], writes=["qb%d" % i2])
        P.op("dve", lambda e: e.scalar_tensor_tensor(
            out=t1[i4][:], in0=banks[ppb][:], scalar=wcol, in1=cosT[:, c0:c0 + SPAN],
            op0=ALU.mult, op1=ALU.mult),
            reads=[bk(ppb), "cosT", "qnw_s", "knw_s"], writes=[t1k[i4]])
        yield
        P.op("pe", lambda e: e.matmul(banks[ssb][:], lhsT=C.ones_bf[:], rhs=sq[i2][:], start=True, stop=True),
             reads=["n_sq%d" % i2, "ones_bf"], writes=[bk(ssb)])
        P.op("pe", lambda e: e.matmul(banks[pqb][:], lhsT=C.perm_bf[:], rhs=qb[i2][:], start=True, stop=True),
             reads=["qb%d" % i2, "perm_bf"], writes=[bk(pqb)])
        yield
        P.op("dve", lambda e: e.scalar_tensor_tensor(
            out=t2[i3][:], in0=banks[pqb][:], scalar=wpcol, in1=sinS[:, c0:c0 + SPAN],
            op0=ALU.mult, op1=ALU.mult),
            reads=[bk(pqb), "sinS", "qnw_s", "knw_s"], writes=["t2_%d" % i3])
        P.op("act", lambda e: e.activation(out=rs[i3][:], in_=banks[ssb][:], func=AF.Ln,
                                           bias=C.eps_col[:, 1:2]),
             reads=[bk(ssb), "eps_col"], writes=[rsk[i3]])
        P.op("act", lambda e: e.activation(out=rs[i3][:], in_=rs[i3][:], func=AF.Exp, scale=-0.5),
             reads=[rsk[i3]], writes=[rsk[i3]])
        yield
        P.op("dve", lambda e: e.tensor_tensor(out=t1[i4][:], in0=t1[i4][:], in1=t2[i3][:], op=ALU.add),
             reads=[t1k[i4], "t2_%d" % i3], writes=[t1k[i4]])
        P.op("pool", lambda e: e.tensor_tensor(out=dst, in0=t1[i4][:], in1=rs[i3][:], op=ALU.mult),
             reads=[t1k[i4], rsk[i3]], writes=[dkey])

    def run_skewed(specs):
        active = []
        for sp_ in specs:
            active.append(prep_gen(sp_))
            for g_ in list(active):
                try:
                    next(g_)
                except StopIteration:
                    active.remove(g_)
        while active:
            for g_ in list(active):
                try:
                    next(g_)
                except StopIteration:
                    active.remove(g_)

    def attn_batch(units, mask_ap, den_out):
        si = cnt["st"] % 2
        cnt["st"] += 1
        stb = 4 + si
        tot = sum(u[4] for u in units)
        off = 0
        offs = []
        for (k_ap, q_ap, v_ap, o_ap, n, rk) in units:
            offs.append(off)
            P.op("pe", lambda e, k_ap=k_ap, q_ap=q_ap, off=off, n=n: e.matmul(
                banks[stb][:, off:off + n], lhsT=k_ap, rhs=q_ap, start=True, stop=True),
                reads=rk, writes=[bk(stb)])
            off += n
        P.op("act", lambda e: e.activation(out=pb[si][:, 0:tot], in_=banks[stb][:, 0:tot],
                                           func=AF.Exp, scale=math.sqrt(128.0)),
             reads=[bk(stb)], writes=["pb%d" % si])
        nun = len(units)
        n0 = units[0][4]
        P.op("pool", lambda e: e.tensor_tensor(
            out=pm[si][:, 0:tot].rearrange("p (u n) -> p u n", u=nun),
            in0=pb[si][:, 0:tot].rearrange("p (u n) -> p u n", u=nun),
            in1=mask_ap.unsqueeze(1).broadcast_to([128, nun, n0]), op=ALU.mult),
            reads=["pb%d" % si, "maskc", "maskw"], writes=["pm%d" % si])
        for (k_ap, q_ap, v_ap, o_ap, n, rk), off in zip(units, offs):
            P.op("pe", lambda e, v_ap=v_ap, o_ap=o_ap, off=off, n=n: e.matmul(
                o_ap, lhsT=v_ap, rhs=pm[si][:, off:off + n], start=False, stop=False,
                skip_group_check=True),
                reads=["pm%d" % si, "vt"], writes=[bk(6)])
        P.op("pe", lambda e: e.matmul(den_out, lhsT=C.ones_bf[:], rhs=pm[si][:, 0:tot],
                                      start=False, stop=False, skip_group_check=True),
             reads=["pm%d" % si, "ones_bf"], writes=[bk(7)])

    ang, kf, kcp = t2[0], t2[1], lnv[1]
    c1 = float(np.float32(6.28125))
    c2 = float(np.float32(2 * PI - 6.28125))
    c3 = float(2 * PI - 6.28125 - c2)

    def wrap(buf, key):
        P.op("dve", lambda e: e.tensor_scalar(out=kf[:], in0=buf[:], scalar1=PI, scalar2=-2 * PI,
                                              op0=ALU.is_gt, op1=ALU.mult), reads=[key], writes=["t2_1"])
        P.op("dve", lambda e: e.tensor_tensor(out=buf[:], in0=buf[:], in1=kf[:], op=ALU.add),
             reads=[key, "t2_1"], writes=[key])
        P.op("dve", lambda e: e.tensor_scalar(out=kf[:], in0=buf[:], scalar1=-PI, scalar2=2 * PI,
                                              op0=ALU.is_lt, op1=ALU.mult), reads=[key], writes=["t2_1"])
        P.op("dve", lambda e: e.tensor_tensor(out=buf[:], in0=buf[:], in1=kf[:], op=ALU.add),
             reads=[key, "t2_1"], writes=[key])
        P.op("dve", lambda e: e.tensor_scalar(out=buf[:], in0=buf[:], scalar1=PI, scalar2=-PI,
                                              op0=ALU.min, op1=ALU.max), reads=[key], writes=[key])

    for s in range(2):
        for cch in range(NSPAN):
            c0 = cch * SPAN
            posi = kcp[:].bitcast(I32)
            P.dma("sp", lambda e, s=s, c0=c0, posi=posi: e.dma_start(
                out=posi, in_=C.pos[s, c0:c0 + SPAN].partition_broadcast(128)), writes=["lnv1"])
            P.op("dve", lambda e, posi=posi: e.tensor_copy(out=ang[:], in_=posi), reads=["lnv1"], writes=["t2_0"])
            P.op("dve", lambda e: e.tensor_scalar(out=ang[:], in0=ang[:], scalar1=C.misc[:, 0:1], scalar2=None,
                                                  op0=ALU.mult), reads=["t2_0", "misc"], writes=["t2_0"])
            P.op("dve", lambda e: e.tensor_scalar(out=kf[:], in0=ang[:], scalar1=1.0 / (2 * PI), scalar2=None,
                                                  op0=ALU.mult), reads=["t2_0"], writes=["t2_1"])
            P.op("dve", lambda e, posi=posi: e.tensor_copy(out=posi, in_=kf[:]), reads=["t2_1"], writes=["lnv1"])
            P.op("dve", lambda e, posi=posi: e.tensor_copy(out=kf[:], in_=posi), reads=["lnv1"], writes=["t2_1"])
            for cc_ in (c1, c2, c3):
                P.op("dve", lambda e, cc_=cc_: e.scalar_tensor_tensor(
                    out=ang[:], in0=kf[:], scalar=-cc_, in1=ang[:], op0=ALU.mult, op1=ALU.add),
                    reads=["t2_1", "t2_0"], writes=["t2_0"])
            wrap(ang, "t2_0")
            P.op("act", lambda e, c0=c0: e.activation(out=sinS[:, c0:c0 + SPAN], in_=ang[:], func=AF.Sin,
                                                      scale=C.misc[:, 1:2]),
                 reads=["t2_0", "misc"], writes=["sinS"])
            P.op("dve", lambda e: e.tensor_scalar(out=ang[:], in0=ang[:], scalar1=PI / 2, scalar2=None,
                                                  op0=ALU.add), reads=["t2_0"], writes=["t2_0"])
            wrap(ang, "t2_0")
            P.op("act", lambda e, c0=c0: e.activation(out=cosT[:, c0:c0 + SPAN], in_=ang[:], func=AF.Sin),
                 reads=["t2_0"], writes=["cosT"])

        if DBG["odd_stop"] <= 0:
            return
        for sp_i in range(NSPAN):
            P.dma("sp", lambda e, s=s, sp_i=sp_i: e.dma_start(
                out=C.xt[:], in_=src_scr[s, :, :, sp_i * SPAN:(sp_i + 1) * SPAN]),
                reads=[scr_key(src_scr, s, sp_i)], writes=["xt"])
            norm_span(C, l, s, hT[:, :, sp_i * SPAN:(sp_i + 1) * SPAN], "hT", NT)
        if DBG["odd_stop"] <= 1:
            return

        for hp in range(4):
            for g in range(3):
                wvt, wvk = wv_ring.get(cnt["wv"])
                cnt["wv"] += 1
                for tl in range(16):
                    if g == 0:
                        cols = slice(tl * 128, tl * 128 + 128, 1)
                    elif g == 1:
                        r, nb_ = tl // 4, tl % 4
                        cols = slice(nb_ * 512 + r, nb_ * 512 + 512, 4)
                    else:
                        cols = slice(tl, SEQ, 16)
                    ppb = cnt["pp"] % 2
                    cnt["pp"] += 1
                    for k in range(KC):
                        P.op("pe", lambda e, k=k, cols=cols, ppb=ppb, wvt=wvt: e.matmul(
                            banks[ppb][:, 0:256].rearrange("p (b c) -> p b c", b=2),
                            lhsT=hT[:, k, cols], rhs=wvt[:, :, k, :],
                            start=(k == 0), stop=(k == KC - 1)),
                            reads=["hT", wvk], writes=[bk(ppb)])
                    if tl % 2 == 0:
                        P.op("act", lambda e, g=g, tl=tl, ppb=ppb: e.copy(out=vt[g][:, tl, :], in_=banks[ppb][:, 0:256]),
                             reads=[bk(ppb)], writes=["vt"])
                    else:
                        P.op("dve", lambda e, g=g, tl=tl, ppb=ppb: e.tensor_copy(out=vt[g][:, tl, :], in_=banks[ppb][:, 0:256]),
                             reads=[bk(ppb)], writes=["vt"])
            if DBG["odd_stop"] <= 2:
                return
            for hh in range(2):
                h = hp * 2 + hh
                specs = []
                for g in range(3):
                    for which in range(2):
                        nws = C.qnw_s if which == 0 else C.knw_s
                        dstT = qT[g] if which == 0 else kT[g]
                        dkey = ("qT%d" if which == 0 else "kT%d") % g
                        blk = {}
                        for sp_i in range(NSPAN):
                            specs.append(dict(blk=blk, sp_i=sp_i, wcol=nws[:, li, 0:1], wpcol=nws[:, li, 1:2],
                                              dst=dstT[:, sp_i * SPAN:(sp_i + 1) * SPAN], dkey=dkey))
                run_skewed(specs)
                if DBG["odd_stop"] <= 3:
                    return
                wzt, wzk = wq_ring.get(cnt["wq"])
                cnt["wq"] += 1
                for b in range(NSPAN):
                    c0 = b * SPAN
                    ppb = cnt["pp"] % 2
                    cnt["pp"] += 1
                    for k in range(KC):
                        P.op("pe", lambda e, k=k, ppb=ppb, wzt=wzt, c0=c0: e.matmul(
                            banks[ppb][:], lhsT=wzt[:, k, :], rhs=hT[:, k, c0:c0 + SPAN],
                            start=(k == 0), stop=(k == KC - 1)),
                            reads=["hT", wzk], writes=[bk(ppb)])
                    P.op("act", lambda e, ppb=ppb, b=b: e.activation(out=szt[b][:], in_=banks[ppb][:], func=AF.Silu),
                         reads=[bk(ppb)], writes=["szt%d" % b])
                vsl = slice(hh * 128, hh * 128 + 128)
                seqb = []
                for b in range(NSPAN):
                    c0 = b * SPAN
                    nbk = 3 + (cnt["sp"] % 2)
                    dbk = 5 + (cnt["sp"] % 2)
                    cnt["sp"] += 1
                    NB, DB = banks[nbk], banks[dbk]
                    blist = []
                    units = []
                    for i in range(4):
                        n_ = 4 * b + i
                        units.append((kT[0][:, n_ * 128:n_ * 128 + 128], qT[0][:, n_ * 128:n_ * 128 + 128],
                                      vt[0][:, n_, vsl], NB[:, i * 128:i * 128 + 128], 128,
                                      ["kT0", "qT0", "vt"]))
                    blist.append((units, mk4c, DB[:, 0:512]))
                    units = []
                    for i in range(4):
                        n_ = 4 * b + i
                        if n_ == 0:
                            continue
                        units.append((kT[0][:, (n_ - 1) * 128:n_ * 128], qT[0][:, n_ * 128:n_ * 128 + 128],
                                      vt[0][:, n_ - 1, vsl], NB[:, i * 128:i * 128 + 128], 128,
                                      ["kT0", "qT0", "vt"]))
                    i0 = 4 - len(units)
                    blist.append((units, mk4w, DB[:, i0 * 128:512]))
                    units = []
                    for r in range(4):
                        units.append((kT[1][:, c0 + r:c0 + SPAN:4], qT[1][:, c0 + r:c0 + SPAN:4],
                                      vt[1][:, r * 4 + b, vsl], NB[:, r:SPAN:4], 128,
                                      ["kT1", "qT1", "vt"]))
                    den1 = DB[:].rearrange("p (i r) -> p r i", r=4)
                    blist.append((units, mk4c, den1))
                    if b > 0:
                        units = []
                        for r in range(4):
                            units.append((kT[1][:, c0 - SPAN + r:c0:4], qT[1][:, c0 + r:c0 + SPAN:4],
                                          vt[1][:, r * 4 + b - 1, vsl], NB[:, r:SPAN:4], 128,
                                          ["kT1", "qT1", "vt"]))
                        blist.append((units, mk4w, den1))
                    units = []
                    for r in range(16):
                        units.append((kT[2][:, r:SEQ:16], qT[2][:, c0 + r:c0 + SPAN:16],
                                      vt[2][:, r, vsl], NB[:, r:SPAN:16], 32,
                                      ["kT2", "qT2", "vt"]))
                    den2 = DB[:].rearrange("p (j r) -> p r j", r=16)
                    blist.append((units, mk2[b], den2))
                    for bi, (u_, m_, d_) in enumerate(blist):
                        seqb.append(dict(units=u_, mask=m_, den=d_, b=b, first=(bi == 0), last=(bi == len(blist) - 1),
                                         nbk=nbk, dbk=dbk))

                def emit_qk(bt):
                    si = cnt["st"] % 3
                    cnt["st"] += 1
                    bt["si"] = si
                    stb = (0, 1, 2)[si]
                    bt["stb"] = stb
                    off = 0
                    bt["offs"] = []
                    for (k_ap, q_ap, v_ap, o_ap, n, rk) in bt["units"]:
                        bt["offs"].append(off)
                        P.op("pe", lambda e, k_ap=k_ap, q_ap=q_ap, off=off, n=n, stb=stb: e.matmul(
                            banks[stb][:, off:off + n], lhsT=k_ap, rhs=q_ap, start=True, stop=True),
                            reads=rk, writes=[bk(stb)])
                        off += n
                    bt["tot"] = off

                def emit_rest(bt):
                    si, stb, tot = bt["si"], bt["stb"], bt["tot"]
                    nbk, dbk = bt["nbk"], bt["dbk"]
                    P.op("act", lambda e: e.activation(out=pb[si][:, 0:tot], in_=banks[stb][:, 0:tot],
                                                       func=AF.Exp, scale=math.sqrt(128.0)),
                         reads=[bk(stb)], writes=["pb%d" % si])
                    nun = len(bt["units"])
                    n0 = bt["units"][0][4]
                    mask_ap = bt["mask"]
                    meng = "dve" if DBG.get("dvemask", 1) else "pool"
                    cnt["mk"] += 1
                    P.op(meng, lambda e: e.tensor_tensor(
                        out=pm[si][:, 0:tot], in0=pb[si][:, 0:tot], in1=mask_ap[:, 0:tot], op=ALU.mult),
                        reads=["pb%d" % si, "mk4c", "mk4w", "mk2_0", "mk2_1", "mk2_2", "mk2_3"], writes=["pm%d" % si])
                    for (k_ap, q_ap, v_ap, o_ap, n, rk), off in zip(bt["units"], bt["offs"]):
                        P.op("pe", lambda e, v_ap=v_ap, o_ap=o_ap, off=off, n=n: e.matmul(
                            o_ap, lhsT=v_ap, rhs=pm[si][:, off:off + n], start=False, stop=False,
                            skip_group_check=True),
                            reads=["pm%d" % si, "vt"], writes=[bk(nbk)])
                    den_out = bt["den"]
                    P.op("pe", lambda e: e.matmul(den_out, lhsT=C.ones_bf[:], rhs=pm[si][:, 0:tot],
                                                  start=False, stop=False, skip_group_check=True),
                         reads=["pm%d" % si, "ones_bf"], writes=[bk(dbk)])
                def emit_end(bt):
                    nbk, dbk = bt["nbk"], bt["dbk"]
                    if True:
                        b = bt["b"]
                        c0 = b * SPAN
                        P.op("act", lambda e: e.activation(out=rden[:], in_=banks[dbk][:], func=AF.Ln),
                             reads=[bk(dbk)], writes=["rden"])
                        P.op("act", lambda e: e.activation(out=rden[:], in_=rden[:], func=AF.Exp, scale=-1.0),
                             reads=["rden"], writes=["rden"])
                        P.op("dve", lambda e: e.tensor_tensor(out=otmp[:], in0=banks[nbk][:], in1=rden[:], op=ALU.mult),
                             reads=[bk(nbk), "rden"], writes=["otmp"])
                        ai = cnt["ao"] % 2
                        cnt["ao"] += 1
                        P.op("pool", lambda e, b=b, ai=ai: e.tensor_tensor(
                            out=ao[ai][:], in0=otmp[:], in1=szt[b][:], op=ALU.mult),
                            reads=["otmp", "szt%d" % b], writes=["ao%d" % ai])
                        P.dma("sp", lambda e, s=s, h=h, c0=c0, ai=ai: e.dma_start(
                            out=C.ats[s, :, h, c0:c0 + SPAN], in_=ao[ai][:]),
                            reads=["ao%d" % ai], writes=["ats_%d_%d_%d" % (s, h, b)])
                        if b + 2 < NSPAN:
                            P.op("dve", lambda e: e.memset(banks[nbk][:], 0.0), writes=[bk(nbk)])
                            P.op("dve", lambda e: e.memset(banks[dbk][:], 0.0), writes=[bk(dbk)])

                LOOK = 2
                firsts = [bt for bt in seqb if bt["first"]]
                for i_, bt in enumerate(firsts):
                    bt["nxt"] = (firsts[i_ + 1]["nbk"], firsts[i_ + 1]["dbk"]) if i_ + 1 < len(firsts) else None
                for f_ in firsts[:2]:
                    n0_, d0_ = f_["nbk"], f_["dbk"]
                    P.op("dve", lambda e, n0_=n0_: e.memset(banks[n0_][:], 0.0), writes=[bk(n0_)])
                    P.op("dve", lambda e, d0_=d0_: e.memset(banks[d0_][:], 0.0), writes=[bk(d0_)])
                for i in range(min(LOOK, len(seqb))):
                    emit_qk(seqb[i])
                pend = []
                for i, bt in enumerate(seqb):
                    if i + LOOK < len(seqb):
                        emit_qk(seqb[i + LOOK])
                    emit_rest(bt)
                    for pe_ in list(pend):
                        pe_[1] -= 1
                        if pe_[1] <= 0:
                            emit_end(pe_[0])
                            pend.remove(pe_)
                    if bt["last"]:
                        pend.append([bt, 2])
                for pe_ in pend:
                    emit_end(pe_[0])

        if DBG["odd_stop"] <= 4:
            return
        hv = hT[:].bitcast(F32)
        xbufs = [(C.xt[:], "xt", []), (hv[:, :, 0:SPAN], "hTa", ["hT"]), (hv[:, :, SPAN:2 * SPAN], "hTb", ["hT"])]

        def p3_load(sp_j):
            c0_ = sp_j * SPAN
            lj = sp_j % 2
            xb_, xk_, ex_ = xbufs[sp_j % 3]
            P.dma("sp", lambda e, s=s, c0_=c0_, lj=lj: e.dma_start(out=atl[lj][:], in_=C.ats[s, :, :, c0_:c0_ + SPAN]),
                  reads=["ats_%d_%d_%d" % (s, h_, sp_j) for h_ in range(KC)], writes=["atl%d" % lj])
            P.dma("sp", lambda e, s=s, c0_=c0_, xb_=xb_: e.dma_start(out=xb_, in_=src_scr[s, :, :, c0_:c0_ + SPAN]),
                  reads=[scr_key(src_scr, s, sp_j)], writes=[xk_] + ex_)

        p3_load(0)
        p3_load(1)
        for sp_i in range(NSPAN):
            c0 = sp_i * SPAN
            li_ = sp_i % 2
            xb, xk0, xex = xbufs[sp_i % 3]
            xk = xk0
            for oc in range(KC):
                wt, wk = wq_ring.get(cnt["wq"])
                cnt["wq"] += 1
                ppb = cnt["pp"] % 2
                cnt["pp"] += 1
                for k in range(KC):
                    P.op("pe", lambda e, k=k, ppb=ppb, wt=wt, li_=li_: e.matmul(
                        banks[ppb][:], lhsT=wt[:, k, :], rhs=atl[li_][:, k, :],
                        start=(k == 0), stop=(k == KC - 1)),
                        reads=[wk, "atl%d" % li_], writes=[bk(ppb)])
                P.op("dve", lambda e, oc=oc, ppb=ppb, s=s, xb=xb: e.scalar_tensor_tensor(
                    out=xb[:, oc, :], in0=banks[ppb][:], scalar=C.modS[:, l, s, 16 + oc:17 + oc],
                    in1=xb[:, oc, :], op0=ALU.mult, op1=ALU.add),
                    reads=[bk(ppb), "modS", xk] + xex, writes=[xk])
            if not fuse_final:
                P.dma("sp", lambda e, s=s, c0=c0, xb=xb: e.dma_start(out=dst_scr[s, :, :, c0:c0 + SPAN], in_=xb),
                      reads=[xk] + xex, writes=[scr_key(dst_scr, s, sp_i)])
            else:
                for tt_ in range(4):
                    slot = cnt["yt"] % 2
                    cnt["yt"] += 1
                    for half in range(2):
                        bnk = 4 + (cnt["ytb"] % 4)
                        cnt["ytb"] += 1
                        for kk in range(4):
                            k = half * 4 + kk
                            P.op("pe", lambda e, k=k, kk=kk, bnk=bnk, tt_=tt_, xb=xb: e.transpose(
                                banks[bnk][:, kk * 128:(kk + 1) * 128],
                                xb[:, k, tt_ * 128:(tt_ + 1) * 128], C.ident[:]),
                                reads=[xk, "ident"] + xex, writes=[bk(bnk)])
                        yi = (slot * 2 + half) % 4
                        dstap = t1[yi][:]
                        if half == 0:
                            P.op("act", lambda e, bnk=bnk, dstap=dstap: e.copy(out=dstap, in_=banks[bnk][:]),
                                 reads=[bk(bnk)], writes=[t1k[yi]])
                        else:
                            P.op("dve", lambda e, bnk=bnk, dstap=dstap: e.tensor_copy(out=dstap, in_=banks[bnk][:]),
                                 reads=[bk(bnk)], writes=[t1k[yi]])
                        t0 = c0 + tt_ * 128
                        key = "yout%d" % len(C.outs)
                        C.outs.append(P.dma("sp", lambda e, yi=yi, s=s, t0=t0, half=half: e.dma_start(
                            out=C.y_out[s, t0:t0 + 128, half * 512:(half + 1) * 512], in_=t1[yi][:]),
                            reads=[t1k[yi]], writes=[key]))
            if sp_i + 2 < NSPAN:
                p3_load(sp_i + 2)


def even_setup(C, les, ses, l):
    P, nc, banks = C.P, C.nc, C.banks
    li = l // 2
    W = Ctx()

    def lsb(name, shape, dt=F32):
        return les.enter_context(nc.sbuf_tensor("e%d_%s" % (l, name), list(shape), dt))

    def tsb(name, shape, dt=F32):
        return ses.enter_context(nc.sbuf_tensor("es%d_%s" % (l, name), list(shape), dt))

    W.RW = lsb("RW", [128, 4, 8, 2, 128], BF16)
    W.OW = lsb("OW", [128, 16, 9, 2, 32], BF16)
    W.KW = lsb("KW", [128, 4, 8, 128], BF16)
    W.SCr = lsb("SCr", [128, 7, 16])
    W.SCi = lsb("SCi", [128, 7, 16])
    W.SCn = lsb("SCn", [128, 7, 16])
    W.Er = lsb("Er", [128, 16, 128])
    W.Ei = lsb("Ei", [128, 16, 128])
    W.rho8 = lsb("rho8", [128, 16])
    W.U1r = lsb("U1r", [128, 16])
    W.U1i = lsb("U1i", [128, 16])
    W.convw = lsb("convw", [128, 4, 31])
    W.cvec = lsb("cvec", [128, 5, 4])
    W.gluw = lsb("gluw", [128, 4, 512], BF16)
    W.mark = lsb("mark", [128, 2])
    p1 = tsb("p1", [128, 5, 256])
    p2a = tsb("p2a", [128, 3, 16])
    p2b = tsb("p2b", [128, 4, 16, 16])
    K_ = []
    chains = {"a": [], "b": []}
    chn = ["b"]

    def rec(eng, fn, reads=(), writes=()):
        key = ["su_" + chn[0]]
        chains[chn[0]].append((eng, fn, list(reads) + key, list(writes) + key))
    P.dma("sp", lambda e: e.dma_start(out=W.convw[:], in_=C.convw[:, li]), writes=["e_convw"])
    P.dma("sp", lambda e: e.dma_start(out=W.cvec[:], in_=C.cvec[:, li]), writes=["e_cvec"])
    P.dma("pool", lambda e: e.dma_start(out=W.gluw[:], in_=C.gluw[li]), writes=["e_gluw"])
    P.dma("sp", lambda e: e.dma_start(out=p1[:], in_=C.s5l1[:, li]), writes=["su_a"])
    P.dma("sp", lambda e: e.dma_start(out=p2a[:], in_=C.s5l2a[:, li]), writes=["su_b"])
    P.dma("sp", lambda e: e.dma_start(out=p2b[:], in_=C.s5l2b[:, li].rearrange("p f (a b) -> p f a b", a=16)), writes=["su_b"])

    def tt(out, a, b, op):
        rec("dve", lambda e: e.tensor_tensor(out=out, in0=a, in1=b, op=op), reads=K_, writes=K_)

    def ts(out, a, s1, op0, s2=None, op1=None):
        if op1 is None:
            rec("dve", lambda e: e.tensor_scalar(out=out, in0=a, scalar1=s1, scalar2=None, op0=op0), reads=K_, writes=K_)
        else:
            rec("dve", lambda e: e.tensor_scalar(out=out, in0=a, scalar1=s1, scalar2=s2, op0=op0, op1=op1), reads=K_, writes=K_)

    def stt(out, a, sc, b, op0, op1):
        rec("dve", lambda e: e.scalar_tensor_tensor(out=out, in0=a, scalar=sc, in1=b, op0=op0, op1=op1), reads=K_, writes=K_)

    def aexp(out, a, scale=1.0):
        rec("act", lambda e: e.activation(out=out, in_=a, func=AF.Exp, scale=scale), reads=K_, writes=K_)

    def cmul(outr, outi, ar, ai, br, bi, t1, t2):
        tt(t1, ar, br, ALU.mult)
        tt(t2, ai, bi, ALU.mult)
        tt(outr, t1, t2, ALU.subtract)
        tt(t1, ar, bi, ALU.mult)
        tt(t2, ai, br, ALU.mult)
        tt(outi, t1, t2, ALU.add)

    def cexp_kappa(pre, lamre, lamim, logdt, shape):
        T = lambda n: tsb(pre + n, shape)
        dt, zr, zi, rho, th, q, c, s, t1, t2 = [T(n) for n in ("dt", "zr", "zi", "rho", "th", "q", "c", "s", "t1", "t2")]
        Ar, Ai, kr, ki, nr = [T(n) for n in ("Ar", "Ai", "kr", "ki", "nr")]
        aexp(dt[:], logdt)
        tt(zr[:], lamre, dt[:], ALU.mult)
        tt(zi[:], lamim, dt[:], ALU.mult)
        aexp(rho[:], zr[:])
        ts(th[:], zi[:], 1.0 / 64, ALU.mult)
        tt(q[:], th[:], th[:], ALU.mult)
        ca = [-1.0 / 2, 1.0 / 24, -1.0 / 720, 1.0 / 40320]
        sa = [-1.0 / 6, 1.0 / 120, -1.0 / 5040, 1.0 / 362880]
        ts(c[:], q[:], ca[3], ALU.mult)
        for a_ in (ca[2], ca[1], ca[0]):
            stt(c[:], c[:], a_, q[:], ALU.add, ALU.mult)
        ts(c[:], c[:], 1.0, ALU.add)
        ts(s[:], q[:], sa[3], ALU.mult)
        for a_ in (sa[2], sa[1], sa[0]):
            stt(s[:], s[:], a_, q[:], ALU.add, ALU.mult)
        stt(s[:], s[:], 1.0, th[:], ALU.add, ALU.mult)
        for _ in range(6):
            tt(t1[:], c[:], c[:], ALU.mult)
            tt(t2[:], s[:], s[:], ALU.mult)
            stt(s[:], c[:], 2.0, s[:], ALU.mult, ALU.mult)
            tt(c[:], t1[:], t2[:], ALU.subtract)
        tt(Ar[:], rho[:], c[:], ALU.mult)
        tt(Ai[:], rho[:], s[:], ALU.mult)
        ts(nr[:], Ar[:], -1.0, ALU.add)
        tt(t1[:], lamre, lamre, ALU.mult)
        tt(t2[:], lamim, lamim, ALU.mult)
        tt(t1[:], t1[:], t2[:], ALU.add)
        rec("dve", lambda e: e.reciprocal(out=q[:], in_=t1[:]), reads=K_, writes=K_)
        tt(t1[:], nr[:], lamre, ALU.mult)
        tt(t2[:], Ai[:], lamim, ALU.mult)
        tt(t1[:], t1[:], t2[:], ALU.add)
        tt(kr[:], t1[:], q[:], ALU.mult)
        tt(t1[:], Ai[:], lamre, ALU.mult)
        tt(t2[:], nr[:], lamim, ALU.mult)
        tt(t1[:], t1[:], t2[:], ALU.subtract)
        tt(ki[:], t1[:], q[:], ALU.mult)
        return Ar, Ai, kr, ki, t1, t2, c, s, rho

    chn[0] = "a"
    A1r, A1i, k1r, k1i, u1, u2, _c1, _s1, _r1 = cexp_kappa("a", p1[:, 0, :], p1[:, 1, :], p1[:, 2, :], [128, 256])
    cur = [(tsb("cur%dr" % i, [128, 256]), tsb("cur%di" % i, [128, 256])) for i in range(2)]
    cmul(cur[0][0][:], cur[0][1][:], k1r[:], k1i[:], p1[:, 3, :], p1[:, 4, :], u1[:], u2[:])
    for k in range(8):
        s_ = 7 - k
        cr_, ci_ = cur[k % 2]
        for part, src in ((0, cr_), (1, ci_)):
            v = src[:].rearrange("p (c q) -> p c q", c=4)
            ts(W.RW[:, :, s_, part, 0:64], v, C.misc[:, 2:3], ALU.mult)
            ts(W.RW[:, :, s_, part, 64:128], v, C.misc[:, 3:4], ALU.mult)
        if k < 7:
            nr_, ni_ = cur[(k + 1) % 2]
            cmul(nr_[:], ni_[:], cr_[:], ci_[:], A1r[:], A1i[:], u1[:], u2[:])

    if DBG["setup_stop"] <= 1:
        return W
    chn[0] = "b"
    A2r, A2i, k2r, k2i, v1, v2, c2_, s2_, rho2 = cexp_kappa("b", p2a[:, 0, :], p2a[:, 1, :], p2a[:, 2, :], [128, 16])
    PWr = tsb("PWr", [128, 9, 16])
    PWi = tsb("PWi", [128, 9, 16])
    rec("dve", lambda e: e.memset(PWr[:, 0, :], 1.0), reads=K_, writes=K_)
    rec("dve", lambda e: e.memset(PWi[:, 0, :], 0.0), reads=K_, writes=K_)
    for k in range(1, 9):
        cmul(PWr[:, k, :], PWi[:, k, :], PWr[:, k - 1, :], PWi[:, k - 1, :], A2r[:], A2i[:], v1[:], v2[:])
    rec("dve", lambda e: e.tensor_copy(out=W.SCr[:, 0, :], in_=PWr[:, 8, :]), reads=K_, writes=K_)
    rec("dve", lambda e: e.tensor_copy(out=W.SCi[:, 0, :], in_=PWi[:, 8, :]), reads=K_, writes=K_)
    for m in range(1, 7):
        tt(v1[:], W.SCr[:, m - 1, :], W.SCr[:, m - 1, :], ALU.mult)
        tt(v2[:], W.SCi[:, m - 1, :], W.SCi[:, m - 1, :], ALU.mult)
        tt(W.SCr[:, m, :], v1[:], v2[:], ALU.subtract)
        stt(W.SCi[:, m, :], W.SCr[:, m - 1, :], 2.0, W.SCi[:, m - 1, :], ALU.mult, ALU.mult)
    ts(W.SCn[:], W.SCi[:], -1.0, ALU.mult)
    um_r = tsb("um_r", [128, 16])
    um_i = tsb("um_i", [128, 16])
    e1 = tsb("e1", [128, 16, 64])
    e2 = tsb("e2", [128, 16, 64])
    rec("dve", lambda e: e.tensor_copy(out=um_r[:], in_=c2_[:]), reads=K_, writes=K_)
    rec("dve", lambda e: e.tensor_copy(out=um_i[:], in_=s2_[:]), reads=K_, writes=K_)
    rec("dve", lambda e: e.tensor_copy(out=W.rho8[:], in_=rho2[:]), reads=K_, writes=K_)
    for _ in range(3):
        tt(v1[:], um_r[:], um_r[:], ALU.mult)
        tt(v2[:], um_i[:], um_i[:], ALU.mult)
        stt(um_i[:], um_r[:], 2.0, um_i[:], ALU.mult, ALU.mult)
        tt(um_r[:], v1[:], v2[:], ALU.subtract)
        tt(W.rho8[:], W.rho8[:], W.rho8[:], ALU.mult)
    rec("dve", lambda e: e.tensor_copy(out=W.U1r[:], in_=um_r[:]), reads=K_, writes=K_)
    rec("dve", lambda e: e.tensor_copy(out=W.U1i[:], in_=um_i[:]), reads=K_, writes=K_)
    rec("dve", lambda e: e.memset(W.Er[:, :, 0:1], 1.0), reads=K_, writes=K_)
    rec("dve", lambda e: e.memset(W.Ei[:, :, 0:1], 0.0), reads=K_, writes=K_)
    for m in range(7):
        sh = 1 << m
        ur = um_r[:].unsqueeze(2).broadcast_to([128, 16, sh])
        ui = um_i[:].unsqueeze(2).broadcast_to([128, 16, sh])
        tt(e1[:, :, 0:sh], W.Er[:, :, 0:sh], ur, ALU.mult)
        tt(e2[:, :, 0:sh], W.Ei[:, :, 0:sh], ui, ALU.mult)
        tt(W.Er[:, :, sh:2 * sh], e1[:, :, 0:sh], e2[:, :, 0:sh], ALU.subtract)
        tt(e1[:, :, 0:sh], W.Er[:, :, 0:sh], ui, ALU.mult)
        tt(e2[:, :, 0:sh], W.Ei[:, :, 0:sh], ur, ALU.mult)
        tt(W.Ei[:, :, sh:2 * sh], e1[:, :, 0:sh], e2[:, :, 0:sh], ALU.add)
        if m < 6:
            tt(v1[:], um_r[:], um_r[:], ALU.mult)
            tt(v2[:], um_i[:], um_i[:], ALU.mult)
            stt(um_i[:], um_r[:], 2.0, um_i[:], ALU.mult, ALU.mult)
            tt(um_r[:], v1[:], v2[:], ALU.subtract)
    B2r = tsb("B2r", [128, 16, 16])
    B2i = tsb("B2i", [128, 16, 16])
    w1 = tsb("w1", [128, 16, 16])
    w2 = tsb("w2", [128, 16, 16])
    bc = lambda t: t[:].unsqueeze(2).broadcast_to([128, 16, 16])
    cmul(B2r[:], B2i[:], bc(k2r), bc(k2i), p2b[:, 0], p2b[:, 1], w1[:], w2[:])
    BBbd = tsb("BBbd", [128, 16, 2, 32], BF16)
    rec("dve", lambda e: e.memset(BBbd[:], 0.0), reads=K_, writes=K_)
    for part, src in ((0, B2r), (1, B2i)):
        rec("dve", lambda e, part=part, src=src: e.tensor_copy(out=BBbd[0:64, :, part, 0:16], in_=src[0:64]), reads=K_, writes=K_)
        rec("dve", lambda e, part=part, src=src: e.tensor_copy(out=BBbd[64:128, :, part, 16:32], in_=src[64:128]), reads=K_, writes=K_)
    rec("dve", lambda e: e.memset(W.OW[:], 0.0), reads=K_, writes=K_)
    Cr, Ci = p2b[:, 2], p2b[:, 3]
    for k in range(9):
        prb = PWr[:, k, :].unsqueeze(2).broadcast_to([128, 16, 16])
        pib = PWi[:, k, :].unsqueeze(2).broadcast_to([128, 16, 16])
        tt(w1[:], Cr, prb, ALU.mult)
        tt(w2[:], Ci, pib, ALU.mult)
        tt(W.OW[0:64, :, k, 0, 0:16], w1[0:64], w2[0:64], ALU.subtract)
        tt(W.OW[64:128, :, k, 0, 16:32], w1[64:128], w2[64:128], ALU.subtract)
        tt(w1[:], Cr, pib, ALU.mult)
        tt(w2[:], Ci, prb, ALU.mult)
        stt(W.OW[0:64, :, k, 1, 0:16], w1[0:64], -1.0, w2[0:64], ALU.mult, ALU.subtract)
        stt(W.OW[64:128, :, k, 1, 16:32], w1[64:128], -1.0, w2[64:128], ALU.mult, ALU.subtract)
    if DBG["setup_stop"] <= 2:
        return W
    rec("dve", lambda e: e.memset(W.KW[:], 0.0), reads=K_, writes=K_)
    for cc in range(4):
        for half in range(2):
            b = half
            for jl in range(4):
                j = cc * 4 + jl
                outap = banks[b][32 * jl:32 * jl + 32, :].rearrange("p (t c) -> p t c", t=4)[:, :, 32 * jl:32 * jl + 32]
                for part in range(2):
                    rec("pe", lambda e, outap=outap, j=j, part=part, half=half, jl=jl: e.matmul(
                        outap, lhsT=BBbd[:, j, part, :], rhs=W.OW[:, j, half * 4:half * 4 + 4, part, :],
                        start=(part == 0), stop=(part == 1), tile_position=(0, 32 * jl), skip_group_check=True),
                        reads=K_, writes=[bk(b)])
            for jl in range(4):
                src = banks[b][32 * jl:32 * jl + 32, :].rearrange("p (t c) -> p t c", t=4)[:, :, 32 * jl:32 * jl + 32]
                rec("dve", lambda e, src=src, cc=cc, half=half, jl=jl: e.tensor_copy(
                    out=W.KW[32 * jl:32 * jl + 32, cc, half * 4:half * 4 + 4, 32 * jl:32 * jl + 32], in_=src),
                    reads=[bk(b)] + K_, writes=K_)
    for cc in range(4):
        rec("dve", lambda e, cc=cc: e.scalar_tensor_tensor(
            out=W.KW[:, cc, 0, :], in0=C.ident[:], scalar=W.cvec[:, 3, cc:cc + 1], in1=W.KW[:, cc, 0, :],
            op0=ALU.mult, op1=ALU.add), reads=K_ + ["e_cvec", "ident"], writes=K_)
    la, lb = chains["a"], chains["b"]
    ia = ib = 0
    while ia < len(la) or ib < len(lb):
        if ib < len(lb):
            P.op(*lb[ib][:2], reads=lb[ib][2], writes=lb[ib][3])
            ib += 1
        if ia < len(la) and ia * len(lb) <= ib * len(la):
            P.op(*la[ia][:2], reads=la[ia][2], writes=la[ia][3])
            ia += 1
    P.op("dve", lambda e: e.memset(W.mark[:], 0.0), reads=["su_a", "su_b"], writes=["su"])
    return W


def stage_even(C, les, l, W, src_scr, dst_scr):
    P, nc, banks = C.P, C.nc, C.banks
    li = l // 2
    SEG = 1024

    def lsb(name, shape, dt=F32):
        return les.enter_context(nc.sbuf_tensor("e%d_%s" % (l, name), list(shape), dt))

    hT = lsb("hT", [128, KC, SEG], BF16)
    useg = lsb("useg", [128, 4, SEG], BF16)
    yaseg = lsb("yaseg", [128, 4, SEG], BF16)
    szseg = lsb("szseg", [128, 4, SEG], BF16)
    Xp = [lsb("Xp%d" % i, [128, 16, 128], BF16) for i in range(2)]
    Xl = [lsb("Xl%d" % i, [128, 16]) for i in range(2)]
    wq_t = [lsb("wq%d" % i, [128, KC, 128], BF16) for i in range(3)]
    abuf = [lsb("abuf%d" % i, [128, 4, 542], BF16) for i in range(2)]
    halo = lsb("halo", [128, 4, 30], BF16)
    sg = [lsb("sg%d" % i, [128, SPAN], BF16) for i in range(2)]
    azb = [lsb("azb%d" % i, [128, 4, SPAN], BF16) for i in range(2)]
    ycv = lsb("ycv", [128, 4, SPAN])
    ysq = lsb("ysq", [128, 4, SPAN])
    tpool = lsb("tpool", [128, 8, SPAN])
    st_mean, st_b, st_c = tpool[:, 0, :], tpool[:, 1, :], tpool[:, 2, :]
    tmpA = [tpool[:, 3, :], tpool[:, 4, :]]
    tmpB = tpool[:, 5, :]
    dg = [lsb("dg%d" % i, [128, 16, 128], BF16) for i in range(2)]
    sqn = [lsb("sqn%d" % i, [128, SPAN], BF16) for i in range(2)]
    ysb = lsb("ysb", [128, 4, SPAN], BF16)
    gs = tpool[:, 6, :]
    mixb = lsb("mixb", [128, 4, SPAN], BF16)
    cw1 = lsb("cw1", [128, 16])
    cw2 = lsb("cw2", [128, 16])
    zi0 = lsb("zi0", [128, 16])
    zi1 = lsb("zi1", [128, 16])
    NT = {"sq": sqn, "ln": st_b, "rs": st_c, "t": tmpA}
    XA = [ycv[:].rearrange("p c t -> p (c t)").rearrange("p (j c) -> p j c", j=16),
          ysq[:].rearrange("p c t -> p (c t)").rearrange("p (j c) -> p j c", j=16)]
    XB = [tpool[:, 4 * i:4 * i + 4, :].rearrange("p c t -> p (c t)").rearrange("p (j c) -> p j c", j=16) for i in range(2)]
    XAk = ["ycv0", "ycv1", "ycv2", "ycv3"], ["ysq"]
    XBk = ["st_mean", "n_ln", "n_rs", "n_t0"], ["n_t1", "tmpB", "gs", "tp7"]
    ycvk = ["ycv0", "ycv1", "ycv2", "ycv3"]

    order = [4, 0, 5, 1, 6, 2, 7, 3] + list(range(8, 20))
    plan = []
    for s in range(2):
        for sg_i in range(2):
            for b_ in order:
                plan.append(("in", b_))
            for sp2 in range(2):
                for oc in range(KC):
                    plan.append(("out", oc))
    ring = Ring(C, wq_t, "ewq", plan,
                lambda d: (C.evw[li, d[1]] if d[0] == "in" else C.evwo[li, d[1]]).rearrange("p (k c) -> p k c", k=KC), 2)
    cnt = {"w": 0, "pp": 0, "dg": 0}
    prenormed = set()
    az_f = [azb[j_][:].rearrange("p c t -> p (c t)").bitcast(F32) for j_ in range(2)]
    NT_alt = {"sq": sqn, "ln": az_f[0][:, 0:SPAN], "rs": az_f[0][:, SPAN:2 * SPAN],
              "t": [az_f[1][:, 0:SPAN], az_f[1][:, SPAN:2 * SPAN]],
              "kln": "azb0", "krs": "azb0", "kt": ["azb1", "azb1"]}

    def norm_seg_gen(s_, sg_, T_):
        for sp2_ in range(2):
            sp_i_ = sg_ * 2 + sp2_
            P.dma("sp", lambda e, s_=s_, sp_i_=sp_i_: e.dma_start(
                out=C.xt[:], in_=src_scr[s_, :, :, sp_i_ * SPAN:(sp_i_ + 1) * SPAN]),
                reads=[scr_key(src_scr, s_, sp_i_)], writes=["xt"])
            yield
            for _ in norm_span_gen(C, l, s_, hT[:, :, sp2_ * SPAN:(sp2_ + 1) * SPAN], "ehT", T_):
                yield

    class Rec:
        def __init__(self):
            self.ops = []

        def op(self, eng, fn, reads=(), writes=()):
            self.ops.append((eng, fn, list(reads), list(writes)))

    if DBG["even_stop"] <= 0:
        return
    for s in range(2):
        P.op("dve", lambda e: e.memset(halo[:], 0.0), writes=["halo"])
        P.op("dve", lambda e: e.memset(Xl[0][:], 0.0), writes=["Xl"])
        P.op("dve", lambda e: e.memset(Xl[1][:], 0.0), writes=["Xl"])
        for sg_i in range(2):
            if (s, sg_i) not in prenormed:
                for _ in norm_seg_gen(s, sg_i, NT):
                    pass
            for b_ in order:
                wt, wk = ring.get(cnt["w"])
                cnt["w"] += 1
                for sp2 in range(2):
                    ppb = cnt["pp"] % 2
                    cnt["pp"] += 1
                    cs = slice(sp2 * SPAN, (sp2 + 1) * SPAN)
                    for k in range(KC):
                        P.op("pe", lambda e, k=k, ppb=ppb, wt=wt, cs=cs: e.matmul(
                            banks[ppb][:], lhsT=wt[:, k, :], rhs=hT[:, k, cs],
                            start=(k == 0), stop=(k == KC - 1)),
                            reads=["ehT", wk], writes=[bk(ppb)])
                    cc = b_ % 4
                    if 4 <= b_ < 8:
                        P.op("act", lambda e, ppb=ppb, sp2=sp2: e.activation(out=sg[sp2][:], in_=banks[ppb][:], func=AF.Sigmoid),
                             reads=[bk(ppb)], writes=["sg%d" % sp2])
                    elif b_ < 4:
                        P.op("dve", lambda e, ppb=ppb, sp2=sp2, cc=cc: e.tensor_tensor(
                            out=abuf[sp2][:, cc, 30:542], in0=banks[ppb][:], in1=sg[sp2][:], op=ALU.mult),
                            reads=[bk(ppb), "sg%d" % sp2], writes=["abuf%d" % sp2])
                    elif b_ < 12:
                        P.op("act", lambda e, ppb=ppb, sp2=sp2, cc=cc: e.activation(out=azb[sp2][:, cc, :], in_=banks[ppb][:], func=AF.Silu),
                             reads=[bk(ppb)], writes=["azb%d" % sp2])
                    elif b_ < 16:
                        P.op("act", lambda e, ppb=ppb, cs=cs, cc=cc: e.copy(out=useg[:, cc, cs], in_=banks[ppb][:]),
                             reads=[bk(ppb)], writes=["useg"])
                    else:
                        P.op("act", lambda e, ppb=ppb, cs=cs, cc=cc: e.activation(out=szseg[:, cc, cs], in_=banks[ppb][:], func=AF.Silu),
                             reads=[bk(ppb)], writes=["szseg"])
            if DBG["even_stop"] <= 1:
                return
            for sp2 in range(2):
                cs = slice(sp2 * SPAN, (sp2 + 1) * SPAN)
                ab = abuf[sp2]
                abk = "abuf%d" % sp2
                if sp2 == 0:
                    P.op("pool", lambda e: e.tensor_copy(out=abuf[0][:, :, 0:30], in_=halo[:]),
                         reads=["halo"], writes=["abuf0"])
                else:
                    P.op("pool", lambda e: e.tensor_copy(out=abuf[1][:, :, 0:30], in_=abuf[0][:, :, 512:542]),
                         reads=["abuf0"], writes=["abuf1"])
                    P.op("pool", lambda e: e.tensor_copy(out=halo[:], in_=abuf[1][:, :, 512:542]),
                         reads=["abuf1"], writes=["halo"])
                for cc in range(4):
                    cb = 5 + (cc % 2)
                    for hf in range(2):
                        j0, nj = (0, 16) if hf == 0 else (16, 15)
                        di = cnt["dg"] % 2
                        cnt["dg"] += 1
                        P.op("dve", lambda e, cc=cc, di=di, j0=j0, nj=nj: e.tensor_tensor(
                            out=dg[di][:, 0:nj, :], in0=C.ident_bf[:].unsqueeze(1).broadcast_to([128, nj, 128]),
                            in1=W.convw[:, cc, j0:j0 + nj].unsqueeze(2).broadcast_to([128, nj, 128]), op=ALU.mult),
                            reads=["ident_bf", "e_convw"], writes=["dg%d" % di])
                        for jj in range(nj):
                            j = j0 + jj
                            P.op("pe", lambda e, cc=cc, di=di, j=j, jj=jj, cb=cb, ab=ab: e.matmul(
                                banks[cb][:], lhsT=dg[di][:, jj, :], rhs=ab[:, cc, j:j + 512],
                                start=(j == 0), stop=(j == 30)),
                                reads=["dg%d" % di, abk], writes=[bk(cb)])
                    P.op("act", lambda e, cc=cc, cb=cb: e.activation(
                        out=ycv[:, cc, :], in_=banks[cb][:], func=AF.Identity, bias=W.cvec[:, 0, cc:cc + 1]),
                        reads=[bk(cb), "e_cvec"], writes=[ycvk[cc]])
                P.op("act", lambda e: e.activation(out=ysq[:], in_=ycv[:], func=AF.Square), reads=ycvk, writes=["ysq"])
                for cc in range(4):
                    P.op("pe", lambda e, cc=cc: e.matmul(banks[3][:], lhsT=C.ones_f[:], rhs=ycv[:, cc, :],
                                                         start=(cc == 0), stop=(cc == 3)),
                         reads=[ycvk[cc], "ones_f"], writes=[bk(3)])
                for cc in range(4):
                    P.op("pe", lambda e, cc=cc: e.matmul(banks[4][:], lhsT=C.ones_f[:], rhs=ysq[:, cc, :],
                                                         start=(cc == 0), stop=(cc == 3)),
                         reads=["ysq", "ones_f"], writes=[bk(4)])
                P.op("act", lambda e: e.mul(out=st_mean[:], in_=banks[3][:], mul=1.0 / 512), reads=[bk(3)], writes=["st_mean"])
                P.op("act", lambda e: e.activation(out=st_b[:], in_=banks[3][:], func=AF.Square, scale=1.0 / 512),
                     reads=[bk(3)], writes=["n_ln"])
                P.op("dve", lambda e: e.scalar_tensor_tensor(out=st_b[:], in0=banks[4][:], scalar=1.0 / 512, in1=st_b[:],
                                                             op0=ALU.mult, op1=ALU.subtract),
                     reads=[bk(4), "n_ln"], writes=["n_ln"])
                P.op("act", lambda e: e.activation(out=st_c[:], in_=st_b[:], func=AF.Ln, bias=C.eps_col[:, 0:1]),
                     reads=["n_ln", "eps_col"], writes=["n_rs"])
                P.op("act", lambda e: e.activation(out=st_c[:], in_=st_c[:], func=AF.Exp, scale=-0.5),
                     reads=["n_rs"], writes=["n_rs"])
                for cc in range(4):
                    ti = cc % 2
                    P.op("dve", lambda e, cc=cc, ti=ti: e.tensor_tensor(out=tmpA[ti][:], in0=ycv[:, cc, :], in1=st_mean[:], op=ALU.subtract),
                         reads=[ycvk[cc], "st_mean"], writes=["n_t%d" % ti])
                    P.op("dve", lambda e, ti=ti: e.tensor_tensor(out=tmpA[ti][:], in0=tmpA[ti][:], in1=st_c[:], op=ALU.mult),
                         reads=["n_t%d" % ti, "n_rs"], writes=["n_t%d" % ti])
                    P.op("act", lambda e, cc=cc, ti=ti: e.activation(out=tmpB[:], in_=tmpA[ti][:], func=AF.Silu,
                                                                     scale=W.cvec[:, 1, cc:cc + 1], bias=W.cvec[:, 2, cc:cc + 1]),
                         reads=["n_t%d" % ti, "e_cvec"], writes=["tmpB"])
                    P.op("pool", lambda e, cc=cc, cs=cs, sp2=sp2: e.tensor_tensor(out=yaseg[:, cc, cs], in0=tmpB[:], in1=azb[sp2][:, cc, :], op=ALU.mult),
                         reads=["tmpB", "azb%d" % sp2], writes=["yaseg"])

            if DBG["even_stop"] <= 2:
                return
            for jl in range(4):
                for c2 in range(2):
                    bnk = 4 + (jl % 2) * 2 + c2
                    for ccl in range(2):
                        cc = c2 * 2 + ccl
                        for part in range(2):
                            r0 = (ccl * 2 + part) * 128
                            for s_ in range(8):
                                P.op("pe", lambda e, jl=jl, cc=cc, part=part, s_=s_, r0=r0, bnk=bnk: e.matmul(
                                    banks[bnk][:, r0:r0 + 128], lhsT=W.RW[32 * jl:32 * jl + 32, cc, s_, part, :],
                                    rhs=useg[32 * jl:32 * jl + 32, cc, s_:SEG:8],
                                    start=(s_ == 0), stop=(s_ == 7), tile_position=(32 * jl, 0), skip_group_check=True),
                                    reads=["useg", "su"], writes=[bk(bnk)])
                    j0 = (c2 * 2) * 4 + jl
                    v = banks[bnk][:].rearrange("p (a b c) -> p a b c", a=2, b=2)
                    P.op("act", lambda e, v=v, j0=j0: e.copy(out=XA[0][:, j0:j0 + 5:4, :], in_=v[:, :, 0, :]),
                         reads=[bk(bnk)], writes=XAk[0])
                    P.op("dve", lambda e, v=v, j0=j0: e.tensor_copy(out=XA[1][:, j0:j0 + 5:4, :], in_=v[:, :, 1, :]),
                         reads=[bk(bnk)], writes=XAk[1])
            if DBG["even_stop"] <= 3:
                return
            PP = Rec()
            Er, Ei = W.Er[:], W.Ei[:]
            big = lambda out, a, b_, op, rk, wk: PP.op(
                "dve", lambda e: e.tensor_tensor(out=out, in0=a, in1=b_, op=op), reads=rk + ["su"], writes=wk)
            ka0, ka1, kb0, kb1 = XAk[0], XAk[1], XBk[0], XBk[1]
            big(XB[0], Er, XA[0], ALU.mult, ka0, kb0)
            big(XB[1], Ei, XA[1], ALU.mult, ka1, kb1)
            big(XB[0], XB[0], XB[1], ALU.add, kb0 + kb1, kb0)
            big(XB[1], Er, XA[1], ALU.mult, ka1, kb1)
            big(XA[0], Ei, XA[0], ALU.mult, ka0, ka0)
            big(XB[1], XB[1], XA[0], ALU.subtract, kb1 + ka0, kb1)
            ck = ["cw"]
            sm = lambda out, a, b_, op, rk: PP.op(
                "dve", lambda e: e.tensor_tensor(out=out, in0=a, in1=b_, op=op), reads=rk + ck + ["su"], writes=ck)
            sm(cw1[:], Xl[0][:], W.U1r[:], ALU.mult, ["Xl"])
            sm(cw2[:], Xl[1][:], W.U1i[:], ALU.mult, ["Xl"])
            sm(zi0[:], cw1[:], cw2[:], ALU.subtract, [])
            sm(cw1[:], Xl[0][:], W.U1i[:], ALU.mult, ["Xl"])
            sm(cw2[:], Xl[1][:], W.U1r[:], ALU.mult, ["Xl"])
            sm(zi1[:], cw1[:], cw2[:], ALU.add, [])
            zin = [zi0, zi1]
            for j in range(16):
                for part in range(2):
                    PP.op("dve", lambda e, j=j, part=part: e.tensor_tensor_scan(
                        out=XA[part][:, j, :], data0=W.rho8[:, j:j + 1].broadcast_to([128, 128]),
                        data1=XB[part][:, j, :], initial=zin[part][:, j:j + 1], op0=ALU.mult, op1=ALU.add),
                        reads=XBk[part] + ck + ["su"], writes=XAk[part])
            big(XB[0], Er, XA[0], ALU.mult, ka0, kb0)
            big(XB[1], Ei, XA[1], ALU.mult, ka1, kb1)
            big(XB[0], XB[0], XB[1], ALU.subtract, kb0 + kb1, kb0)
            big(XB[1], Er, XA[1], ALU.mult, ka1, kb1)
            big(XA[0], Ei, XA[0], ALU.mult, ka0, ka0)
            big(XB[1], XB[1], XA[0], ALU.add, kb1 + ka0, kb1)
            src, srck = XB, XBk
            nxt = (s, sg_i + 1) if sg_i + 1 < 2 else ((s + 1, 0) if s + 1 < 2 else None)
            gn = norm_seg_gen(nxt[0], nxt[1], NT_alt) if nxt is not None else None
            if nxt is not None:
                prenormed.add(nxt)
            for oi_, op_ in enumerate(PP.ops):
                P.op(op_[0], op_[1], reads=op_[2], writes=op_[3])
                if gn is not None:
                    try:
                        next(gn)
                    except StopIteration:
                        gn = None
            if gn is not None:
                for _ in gn:
                    pass
            fin, fink = src, srck
            for part in range(2):
                P.op("pool", lambda e, part=part: e.tensor_copy(out=Xp[part][:, :, 0], in_=Xl[part][:]),
                     reads=["Xl"], writes=["Xp%d" % part])
                P.op("pool", lambda e, part=part, fin=fin: e.tensor_copy(out=Xp[part][:, :, 1:128], in_=fin[part][:, :, 0:127]),
                     reads=fink[part], writes=["Xp%d" % part])
            for part in range(2):
                P.op("pool", lambda e, part=part, fin=fin: e.tensor_copy(out=Xl[part][:], in_=fin[part][:, :, 127]),
                     reads=fink[part] + ["Xp0", "Xp1"], writes=["Xl"])

            if DBG["even_stop"] <= 4:
                return
            for sp2 in range(2):
                sp_i = sg_i * 2 + sp2
                c0 = sp2 * SPAN
                ch0 = sp2 * 64
                for cc in range(4):
                    yb = cc % 2
                    P.op("dve", lambda e, yb=yb: e.memset(banks[yb][:], 0.0), writes=[bk(yb)])
                    for t in range(8):
                        for tau in range(t + 1):
                            P.op("pe", lambda e, yb=yb, t=t, tau=tau, cc=cc, c0=c0: e.matmul(
                                banks[yb][:, t * 64:(t + 1) * 64], lhsT=W.KW[:, cc, tau, :],
                                rhs=useg[:, cc, c0 + t - tau:c0 + SPAN:8],
                                start=False, stop=False, skip_group_check=True),
                                reads=["useg", "su"], writes=[bk(yb)])
                    for t in range(8):
                        for jl in range(4):
                            j = cc * 4 + jl
                            for part in range(2):
                                P.op("pe", lambda e, yb=yb, t=t, jl=jl, j=j, part=part, ch0=ch0: e.matmul(
                                    banks[yb][32 * jl:32 * jl + 32, t * 64:(t + 1) * 64],
                                    lhsT=W.OW[:, j, t + 1, part, :], rhs=Xp[part][:, j, ch0:ch0 + 64],
                                    start=False, stop=False, tile_position=(0, 32 * jl), skip_group_check=True),
                                    reads=["Xp%d" % part, "su"], writes=[bk(yb)])
                    P.op("act", lambda e, yb=yb, cc=cc: e.activation(
                        out=ysb[:, cc, :].rearrange("p (c t) -> p t c", t=8),
                        in_=banks[yb][:].rearrange("p (t c) -> p t c", t=8), func=AF.Gelu),
                        reads=[bk(yb)], writes=["ysb"])
                if DBG["even_stop"] <= 5:
                    return
                for oc in range(4):
                    gb = 2 + oc % 2
                    for k in range(4):
                        P.op("pe", lambda e, oc=oc, k=k, gb=gb: e.matmul(
                            banks[gb][:], lhsT=W.gluw[:, k, oc * 128:(oc + 1) * 128], rhs=ysb[:, k, :],
                            start=(k == 0), stop=(k == 3)), reads=["ysb", "e_gluw"], writes=[bk(gb)])
                    P.op("act", lambda e, oc=oc, gb=gb: e.activation(out=gs[:], in_=banks[gb][:], func=AF.Sigmoid,
                                                                     bias=W.cvec[:, 4, oc:oc + 1]),
                         reads=[bk(gb), "e_cvec"], writes=["gs"])
                    P.op("dve", lambda e, oc=oc: e.tensor_tensor(out=tmpB[:], in0=gs[:], in1=ysb[:, oc, :], op=ALU.mult),
                         reads=["gs", "ysb"], writes=["tmpB"])
                    P.op("pool", lambda e, oc=oc, c0=c0: e.tensor_tensor(out=mixb[:, oc, :], in0=tmpB[:], in1=szseg[:, oc, c0:c0 + SPAN], op=ALU.mult),
                         reads=["tmpB", "szseg"], writes=["mixb"])
                if DBG["even_stop"] <= 6:
                    return
                P.dma("sp", lambda e, s=s, sp_i=sp_i: e.dma_start(
                    out=C.xt[:], in_=src_scr[s, :, :, sp_i * SPAN:(sp_i + 1) * SPAN]),
                    reads=[scr_key(src_scr, s, sp_i)], writes=["xt"])
                for oc in range(KC):
                    wt, wk = ring.get(cnt["w"])
                    cnt["w"] += 1
                    ob = 4 + oc % 2
                    for k in range(KC):
                        rhs = yaseg[:, k, c0:c0 + SPAN] if k < 4 else mixb[:, k - 4, :]
                        P.op("pe", lambda e, k=k, ob=ob, wt=wt, rhs=rhs: e.matmul(
                            banks[ob][:], lhsT=wt[:, k, :], rhs=rhs, start=(k == 0), stop=(k == KC - 1)),
                            reads=[wk, "yaseg", "mixb"], writes=[bk(ob)])
                    P.op("dve", lambda e, oc=oc, ob=ob, s=s: e.scalar_tensor_tensor(
                        out=C.xt[:, oc, :], in0=banks[ob][:], scalar=C.modS[:, l, s, 16 + oc:17 + oc],
                        in1=C.xt[:, oc, :], op0=ALU.mult, op1=ALU.add),
                        reads=[bk(ob), "modS", "xt"], writes=["xt"])
                P.dma("sp", lambda e, s=s, sp_i=sp_i: e.dma_start(
                    out=dst_scr[s, :, :, sp_i * SPAN:(sp_i + 1) * SPAN], in_=C.xt[:]),
                    reads=["xt"], writes=[scr_key(dst_scr, s, sp_i)])
                if DBG["even_stop"] <= 7:
                    return
            if DBG["even_stop"] <= 8:
                return
        if DBG["even_stop"] <= 9:
            return


_CACHE = {}


def make_in_maps(inputs):
    f32 = np.float32
    g = lambda k: np.asarray(inputs[k], f32)
    x = g("x")
    c = g("c")
    pos = np.asarray(inputs["positions"], np.int32)
    nb = x.shape[0] // 2
    shared = dict(host_consts())
    shared["modw"] = np.ascontiguousarray(g("mod_w"))
    shared["modb"] = np.ascontiguousarray(np.stack([colvec(g("mod_b")[l], 24) for l in range(4)], axis=1))
    shared["normw"] = np.ascontiguousarray(np.stack([colvec(g("norm_w")[l], 8) for l in range(4)], axis=1))
    aw = g("attn_w_in")
    shared["attw"] = np.ascontiguousarray(np.stack([blk_cols(aw[i]).reshape(80, 128, KC * 128) for i in range(2)]))
    awo = g("attn_w_out")
    shared["attwo"] = np.ascontiguousarray(np.stack([blk_cols(awo[i]).reshape(8, 128, KC * 128) for i in range(2)]))
    for nm, key in (("qnw", "attn_q_norm_w"), ("knw", "attn_k_norm_w")):
        w = g(key)
        wp = np.concatenate([w[:, 64:], w[:, :64]], axis=1)
        shared[nm] = np.ascontiguousarray(np.stack([w.T, wp.T], axis=2))
    ew = g("even_w_in")
    shared["evw"] = np.ascontiguousarray(np.stack([blk_cols(ew[i]).reshape(20, 128, KC * 128) for i in range(2)]))
    ewo = g("even_w_out")
    shared["evwo"] = np.ascontiguousarray(np.stack([blk_cols(ewo[i]).reshape(8, 128, KC * 128) for i in range(2)]))
    gw = g("ssm_glu_w")
    shared["gluw"] = np.ascontiguousarray(np.stack([gw[i].reshape(4, 128, 512).transpose(1, 0, 2) for i in range(2)]))
    cw = g("conv_dw_w")
    shared["convw"] = np.ascontiguousarray(np.stack([cw[i].T.reshape(4, 128, 31).transpose(1, 0, 2) for i in range(2)], axis=1))
    vecs = []
    for i in range(2):
        vecs.append(np.stack([colvec(g(k)[i], 4) for k in ("conv_dw_b", "conv_ln_w", "conv_ln_b", "ssm_d", "ssm_glu_b")], axis=1))
    shared["cvec"] = np.ascontiguousarray(np.stack(vecs, axis=1))

    def l1(a):
        t = a.reshape(4, 8, 64).transpose(1, 0, 2)
        return np.repeat(t[:, None], 16, axis=1).reshape(128, 4, 64)

    def l1b(b):
        return b.reshape(4, 8, 64, 16).transpose(1, 3, 0, 2).reshape(128, 4, 64)

    def l2(a):
        return a.reshape(16, 2, 64).transpose(1, 2, 0).reshape(128, 16)

    s5l1, s5l2a, s5l2b = [], [], []
    for i in range(2):
        ldt = np.broadcast_to(g("ssm_log_dt")[i][:, None], (32, 64))
        s5l1.append(np.stack([l1(g("ssm_lam_re")[i]), l1(g("ssm_lam_im")[i]), l1(ldt),
                              l1b(g("ssm_b_re")[i]), l1b(g("ssm_b_im")[i])], axis=1).reshape(128, 5, 256))
        s5l2a.append(np.stack([l2(g("ssm_lam_re")[i]), l2(g("ssm_lam_im")[i]), l2(ldt)], axis=1))
        bb = [g(k)[i].reshape(16, 2, 64, 16).transpose(1, 2, 0, 3).reshape(128, 256) for k in ("ssm_b_re", "ssm_b_im")]
        cc = [g(k)[i].reshape(16, 2, 16, 64).transpose(1, 3, 0, 2).reshape(128, 256) for k in ("ssm_c_re", "ssm_c_im")]
        s5l2b.append(np.stack(bb + cc, axis=1))
    shared["s5l1"] = np.ascontiguousarray(np.stack(s5l1, axis=1))
    shared["s5l2a"] = np.ascontiguousarray(np.stack(s5l2a, axis=1))
    shared["s5l2b"] = np.ascontiguousarray(np.stack(s5l2b, axis=1))
    maps = []
    for ci in range(nb):
        m = dict(shared)
        m["x"] = np.ascontiguousarray(x[2 * ci:2 * ci + 2])
        cc = c[2 * ci:2 * ci + 2]
        m["cT"] = np.ascontiguousarray(cc.reshape(2, KC, 128).transpose(2, 1, 0))
        m["pos"] = np.ascontiguousarray(pos[2 * ci:2 * ci + 2])
        maps.append(m)
    return maps


def kernel(**inputs):
    stages = ("xpose", 0, 1, 2, 3, "final")
    if stages not in _CACHE:
        _CACHE[stages] = build(stages)
    nc = _CACHE[stages]
    maps = make_in_maps(inputs)
    res = run_bass_kernel_spmd(nc, maps, core_ids=list(range(NCORES)))
    return np.concatenate([r["y"] for r in res.results], axis=0).astype(np.float32)
```

```python
import math
import numpy as np
from contextlib import ExitStack
import concourse.bass as bass
import concourse.mybir as mybir
from concourse.bass_utils import run_bass_kernel_spmd

F32 = mybir.dt.float32
BF16 = mybir.dt.bfloat16
I32 = mybir.dt.int32
AF = mybir.ActivationFunctionType
ALU = mybir.AluOpType

NCORES = 8
SEQ = 2048
D = 1024
KC = 8
SPAN = 512
NSPAN = SEQ // SPAN
EPS = 1e-6
PI = math.pi
ENGS = ("pe", "act", "dve", "pool", "sp")
NDMASEM = 6


class Op:
    __slots__ = ("eng", "fn", "deps", "sig", "count", "is_dma", "dsem", "dval", "emitted")

    def __init__(self, eng, fn, is_dma=False):
        self.eng = eng
        self.fn = fn
        self.deps = []
        self.sig = False
        self.count = None
        self.is_dma = is_dma
        self.dsem = None
        self.dval = None
        self.emitted = False


class Prog:
    def __init__(self, nc, es):
        self.nc = nc
        self.ops = {e: [] for e in ENGS}
        self.last_w = {}
        self.readers = {}
        self.last_acc = {}
        self.sems = {e: es.enter_context(nc.semaphore("s_" + e)) for e in ENGS}
        self.dq = ("sp", "act", "pool")
        self.dsems = {e: [es.enter_context(nc.semaphore("d_%s%d" % (e, i))) for i in range(NDMASEM)]
                      for e in self.dq}
        self.dma_n = {e: 0 for e in self.dq}
        self.dma_hist = {e: [] for e in self.dq}

    def _track(self, op, reads, writes):
        deps = set()
        for k in reads:
            w = self.last_w.get(k)
            if w is not None:
                deps.add(w)
        for k in writes:
            w = self.last_w.get(k)
            if w is not None:
                deps.add(w)
            for r in self.readers.get(k, ()):
                deps.add(r)
        for k in list(reads) + list(writes):
            if k.startswith("bank"):
                a = self.last_acc.get(k)
                if a is not None and a.eng != op.eng:
                    deps.add(a)
                self.last_acc[k] = op
        deps.discard(op)
        op.deps = list(deps)
        for k in reads:
            self.readers.setdefault(k, []).append(op)
        for k in writes:
            self.last_w[k] = op
            self.readers[k] = []

    def op(self, eng, fn, reads=(), writes=()):
        o = Op(eng, fn)
        self._track(o, reads, writes)
        self.ops[eng].append(o)
        return o

    def dma(self, q, fn, reads=(), writes=()):
        o = Op(q, fn, is_dma=True)
        self._track(o, reads, writes)
        n = self.dma_n[q]
        self.dma_n[q] = n + 1
        o.dsem = self.dsems[q][n % NDMASEM]
        o.dval = 16 * (n // NDMASEM + 1)
        hist = self.dma_hist[q]
        if n >= NDMASEM:
            o.deps.append(hist[n - NDMASEM])
        hist.append(o)
        self.ops[q].append(o)
        return o

    def emit(self, block):
        if not hasattr(self, "cnt"):
            self.cnt = {e: 0 for e in ENGS}
            self.waited = {e: {} for e in ENGS}
        for e in ENGS:
            for o in self.ops[e]:
                for d in o.deps:
                    if not getattr(d, "emitted", False) and not (d.eng == "pe" and o.eng == "pe"):
                        d.sig = True
        for e in ENGS:
            for o in self.ops[e]:
                if not o.is_dma and o.sig:
                    self.cnt[e] += 1
                    o.count = self.cnt[e]
        engobj = {"pe": "tensor", "act": "scalar", "dve": "vector", "pool": "gpsimd", "sp": "sync"}
        sems = self.sems
        todo = {e: self.ops[e] for e in ENGS}
        self.ops = {e: [] for e in ENGS}

        def replay(e, eng):
            waited = self.waited[e]
            for o in todo[e]:
                need = {}
                for d in o.deps:
                    if d.is_dma:
                        key = ("d", d.eng, id(d.dsem))
                        if need.get(key, (None, 0))[1] < d.dval:
                            need[key] = (d.dsem, d.dval)
                    else:
                        if d.count is None:
                            continue
                        if d.eng == "pe" and e == "pe":
                            continue
                        key = ("e", d.eng)
                        if need.get(key, (None, 0))[1] < d.count:
                            need[key] = (sems[d.eng], d.count)
                for key, (sem, val) in need.items():
                    if waited.get(key, 0) >= val:
                        continue
                    eng.wait_ge(sem, val)
                    waited[key] = val
                ins = o.fn(eng)
                if o.is_dma:
                    ins.then_inc(o.dsem, 16)
                elif o.sig:
                    ins.then_inc(sems[e], 1)
                o.emitted = True
                o.fn = None

        for e in ENGS:
            def body(eng, e=e):
                replay(e, eng)
            getattr(block, engobj[e])(body)


def blk_cols(w, bw=128):
    k = w.shape[0] // 128
    n = w.shape[1] // bw
    return np.ascontiguousarray(w.reshape(k, 128, n, bw).transpose(2, 1, 0, 3))


def colvec(v, nch):
    return np.ascontiguousarray(v.reshape(nch, 128).T)


def host_consts():
    c = {}
    c["ident"] = np.eye(128, dtype=np.float32)
    kq = np.arange(128)
    c["maskc"] = (kq[:, None] <= kq[None, :]).astype(np.float32)
    c["maskw"] = (kq[:, None] >= kq[None, :]).astype(np.float32)
    perm = np.zeros((128, 128), np.float32)
    for e2 in range(128):
        perm[(e2 + 64) % 128, e2] = 1.0
    c["perm"] = perm
    inv = (np.float32(10000.0) ** (-np.arange(0, 128, 2, dtype=np.float32) / np.float32(128))).astype(np.float32)
    misc = np.zeros((128, 8), np.float32)
    misc[:, 0] = np.concatenate([inv, inv])
    misc[:64, 1] = -1.0
    misc[64:, 1] = 1.0
    gl = (np.arange(128) // 16)
    misc[:, 2] = (gl % 2 == 0)
    misc[:, 3] = (gl % 2 == 1)
    c["misc"] = misc
    return c


class Ctx:
    pass


DBG = {"odd_stop": 99, "prep": 99, "even_stop": 99, "setup_stop": 99}


def build(stages):
    nc = bass.Bass("TRN2", target_bir_lowering=False)
    C = Ctx()
    C.nc = nc

    def din(name, shape, dt=F32):
        return nc.dram_tensor(name, list(shape), dt, kind="ExternalInput").ap()

    C.x_in = din("x", [2, SEQ, D])
    C.cT = din("cT", [128, KC, 2])
    C.pos = din("pos", [2, SEQ], I32)
    C.modw = din("modw", [4, D, 3 * D])
    C.modb = din("modb", [128, 4, 24])
    C.normw = din("normw", [128, 4, KC])
    C.ident_d = din("ident", [128, 128])
    C.maskc_d = din("maskc", [128, 128])
    C.maskw_d = din("maskw", [128, 128])
    C.perm_d = din("perm", [128, 128])
    C.misc_d = din("misc", [128, 8])
    C.attw = din("attw", [2, 80, 128, KC * 128])
    C.attwo = din("attwo", [2, 8, 128, KC * 128])
    C.qnw = din("qnw", [128, 2, 2])
    C.knw = din("knw", [128, 2, 2])
    C.evw = din("evw", [2, 20, 128, KC * 128])
    C.evwo = din("evwo", [2, 8, 128, KC * 128])
    C.gluw = din("gluw", [2, 128, 4, 512])
    C.convw = din("convw", [128, 2, 4, 31])
    C.cvec = din("cvec", [128, 2, 5, 4])
    C.s5l1 = din("s5l1", [128, 2, 5, 256])
    C.s5l2a = din("s5l2a", [128, 2, 3, 16])
    C.s5l2b = din("s5l2b", [128, 2, 4, 256])
    C.y_out = nc.dram_tensor("y", [2, SEQ, D], F32, kind="ExternalOutput").ap()
    C.xs = [nc.dram_tensor("xs%d" % i, [2, 128, KC, SEQ], F32, kind="Internal").ap() for i in range(2)]
    C.ats = nc.dram_tensor("ats", [2, 128, KC, SEQ], BF16, kind="Internal").ap()

    with ExitStack() as es:
        P = Prog(nc, es)
        C.P = P

        def sb(name, shape, dt=F32):
            return es.enter_context(nc.sbuf_tensor("sb_" + name, list(shape), dt))

        C.banks = [es.enter_context(nc.psum_tensor("bank%d" % i, [128, 512], F32)) for i in range(8)]
        C.ident = sb("ident", [128, 128])
        C.ones_bf = sb("ones_bf", [128, 128], BF16)
        C.ones_f = sb("ones_f", [128, 128])
        C.ident_bf = sb("ident_bf", [128, 128], BF16)
        C.perm_bf = sb("perm_bf", [128, 128], BF16)
        C.maskc = sb("maskc", [128, 128], BF16)
        C.maskw = sb("maskw", [128, 128], BF16)
        C.misc = sb("misc", [128, 8])
        C.modS = sb("modS", [128, 4, 2, 24])
        C.aS = sb("aS", [128, 4, 2, KC])
        C.normw_s = sb("normw_s", [128, 4, KC])
        C.modb_s = sb("modb_s", [128, 4, 24])
        C.cT_s = sb("cT_s", [128, KC, 2])
        C.qnw_s = sb("qnw_s", [128, 2, 2])
        C.knw_s = sb("knw_s", [128, 2, 2])
        C.eps_col = sb("eps_col", [128, 4])
        C.xt = sb("xt", [128, KC, SPAN])
        C.cur = 0
        C.outs = []

        layers = [s for s in stages if isinstance(s, int)]
        stages = list(stages)
        es_pro = ExitStack()
        gpro = stage_prologue(C, es_pro, layers)
        idx = 0
        if stages and stages[0] == "xpose":
            es_x = ExitStack()
            gx = stage_xpose(C, es_x, C.xs[C.cur])
            alive = [gpro, gx, gx]
            while alive:
                for g_ in list(alive):
                    try:
                        next(g_)
                    except StopIteration:
                        alive = [a_ for a_ in alive if a_ is not g_]
            idx = 1
            with nc.Block() as block:
                P.emit(block)
            es_x.close()
            es_pro.close()
        else:
            for _ in gpro:
                pass
            with nc.Block() as block:
                P.emit(block)
            es_pro.close()
        skip_final = [False]
        for si_ in range(idx, len(stages)):
            st = stages[si_]
            with ExitStack() as les:
                if st == "xpose":
                    for _ in stage_xpose(C, les, C.xs[C.cur]):
                        pass
                elif st == "final":
                    if not skip_final[0]:
                        stage_final(C, les, C.xs[C.cur])
                elif isinstance(st, int):
                    if st % 2 == 1:
                        ff = (si_ + 1 < len(stages) and stages[si_ + 1] == "final")
                        stage_odd(C, les, st, C.xs[C.cur], C.xs[1 - C.cur], fuse_final=ff)
                        if ff:
                            skip_final[0] = True
                    else:
                        with ExitStack() as ses:
                            W = even_setup(C, les, ses, st)
                            with nc.Block() as block:
                                P.emit(block)
                        stage_even(C, les, st, W, C.xs[C.cur], C.xs[1 - C.cur])
                    C.cur = 1 - C.cur
                if si_ == len(stages) - 1:
                    P.op("sp", lambda e: e.nop(), reads=["yout%d" % i for i in range(len(C.outs))])
                with nc.Block() as block:
                    P.emit(block)
    return nc


def bk(i):
    return "bank%d" % i


def stage_prologue(C, les, layers):
    P, nc = C.P, C.nc
    P.dma("sp", lambda e: e.dma_start(out=C.ident[:], in_=C.ident_d), writes=["ident"])
    P.dma("pool", lambda e: e.dma_start(out=C.perm_bf[:], in_=C.perm_d), writes=["perm_bf"])
    P.dma("pool", lambda e: e.dma_start(out=C.ident_bf[:], in_=C.ident_d), writes=["ident_bf"])
    P.dma("pool", lambda e: e.dma_start(out=C.maskc[:], in_=C.maskc_d), writes=["maskc"])
    P.dma("pool", lambda e: e.dma_start(out=C.maskw[:], in_=C.maskw_d), writes=["maskw"])
    P.dma("sp", lambda e: e.dma_start(out=C.misc[:], in_=C.misc_d), writes=["misc"])
    P.dma("sp", lambda e: e.dma_start(out=C.normw_s[:], in_=C.normw), writes=["normw_s"])
    P.dma("sp", lambda e: e.dma_start(out=C.modb_s[:], in_=C.modb), writes=["modb_s"])
    P.dma("sp", lambda e: e.dma_start(out=C.cT_s[:], in_=C.cT), writes=["cT_s"])
    P.dma("sp", lambda e: e.dma_start(out=C.qnw_s[:], in_=C.qnw), writes=["qnw_s"])
    P.dma("sp", lambda e: e.dma_start(out=C.knw_s[:], in_=C.knw), writes=["knw_s"])
    P.op("dve", lambda e: e.memset(C.ones_bf[:], 1.0), writes=["ones_bf"])
    P.op("dve", lambda e: e.memset(C.ones_f[:], 1.0), writes=["ones_f"])
    P.op("dve", lambda e: e.memset(C.eps_col[:, 0:1], EPS), writes=["eps_col"])
    P.op("dve", lambda e: e.memset(C.eps_col[:, 1:2], 128.0 * EPS), writes=["eps_col"])
    if not layers:
        yield
        return
    banks = C.banks
    mwb = [les.enter_context(nc.sbuf_tensor("mwb%d" % i, [128, KC, 384], F32)) for i in range(3)]
    nb = 0
    for l in layers:
        for cb in range(8):
            slot = nb % 3
            nb += 1
            src = C.modw[l].rearrange("(k p) c -> p k c", p=128)[:, :, cb * 384:(cb + 1) * 384]
            P.dma("sp", lambda e, slot=slot, src=src: e.dma_start(out=mwb[slot][:], in_=src),
                  writes=["mwb%d" % slot])
            for j3 in range(3):
                j = cb * 3 + j3
                for k in range(KC):
                    P.op("pe", lambda e, slot=slot, j3=j3, j=j, k=k, l=l: e.matmul(
                        banks[7][:, (l * 24 + j) * 2:(l * 24 + j) * 2 + 2],
                        lhsT=mwb[slot][:, k, j3 * 128:(j3 + 1) * 128], rhs=C.cT_s[:, k, :],
                        start=(k == 0), stop=(k == KC - 1)),
                        reads=["mwb%d" % slot, "cT_s"], writes=[bk(7)])
            yield
    for l in layers:
        for s in range(2):
            src = banks[7][:, l * 48:(l + 1) * 48].rearrange("p (j s) -> p s j", s=2)[:, s, :]
            P.op("dve", lambda e, l=l, s=s, src=src: e.tensor_tensor(
                out=C.modS[:, l, s, :], in0=src, in1=C.modb_s[:, l, :], op=ALU.add),
                reads=[bk(7), "modb_s"], writes=["modS"])
            P.op("dve", lambda e, l=l, s=s: e.scalar_tensor_tensor(
                out=C.aS[:, l, s, :], in0=C.modS[:, l, s, 8:16], scalar=1.0, in1=C.normw_s[:, l, :],
                op0=ALU.add, op1=ALU.mult),
                reads=["modS", "normw_s"], writes=["aS"])


def scr_key(buf, s, sp_i):
    return "scr_%s_%d_%d" % (buf.tensor.name, s, sp_i)


def stage_xpose(C, les, dst):
    P, nc, banks = C.P, C.nc, C.banks
    xtok = [les.enter_context(nc.sbuf_tensor("xtok%d_%d" % (i, id(les) % 100000), [128, D], F32)) for i in range(4)]
    n = 0
    for s in range(2):
        for sp_i in range(NSPAN):
            for tt in range(4):
                slot = n % 4
                n += 1
                t0 = sp_i * SPAN + tt * 128
                P.dma("sp", lambda e, slot=slot, s=s, t0=t0: e.dma_start(
                    out=xtok[slot][:], in_=C.x_in[s, t0:t0 + 128, :]), writes=["xtok%d" % slot])
                for half in range(2):
                    bnk = (n * 2 + half) % 4
                    for kk in range(4):
                        k = half * 4 + kk
                        P.op("pe", lambda e, slot=slot, k=k, kk=kk, bnk=bnk: e.transpose(
                            banks[bnk][:, kk * 128:(kk + 1) * 128],
                            xtok[slot][:, k * 128:(k + 1) * 128], C.ident[:]),
                            reads=["xtok%d" % slot, "ident"], writes=[bk(bnk)])
                    src = banks[bnk][:].rearrange("p (k t) -> p k t", k=4)
                    dstap = C.xt[:, half * 4:(half + 1) * 4, tt * 128:(tt + 1) * 128]
                    if half == 0:
                        P.op("act", lambda e, src=src, dstap=dstap: e.copy(out=dstap, in_=src),
                             reads=[bk(bnk)], writes=["xt"])
                    else:
                        P.op("dve", lambda e, src=src, dstap=dstap: e.tensor_copy(out=dstap, in_=src),
                             reads=[bk(bnk)], writes=["xt"])
                yield
            P.dma("sp", lambda e, s=s, sp_i=sp_i: e.dma_start(
                out=dst[s, :, :, sp_i * SPAN:(sp_i + 1) * SPAN], in_=C.xt[:]),
                reads=["xt"], writes=[scr_key(dst, s, sp_i)])


def stage_final(C, les, src_scr):
    P, nc, banks = C.P, C.nc, C.banks
    ytok = [les.enter_context(nc.sbuf_tensor("ytok%d_%d" % (i, id(les) % 100000), [128, D], F32)) for i in range(2)]
    n = 0
    for s in range(2):
        for sp_i in range(NSPAN):
            P.dma("sp", lambda e, s=s, sp_i=sp_i: e.dma_start(
                out=C.xt[:], in_=src_scr[s, :, :, sp_i * SPAN:(sp_i + 1) * SPAN]),
                reads=[scr_key(src_scr, s, sp_i)], writes=["xt"])
            for tt in range(4):
                slot = n % 2
                n += 1
                for half in range(2):
                    bnk = (n * 2 + half) % 4
                    for kk in range(4):
                        k = half * 4 + kk
                        P.op("pe", lambda e, k=k, kk=kk, bnk=bnk, tt=tt: e.transpose(
                            banks[bnk][:, kk * 128:(kk + 1) * 128],
                            C.xt[:, k, tt * 128:(tt + 1) * 128], C.ident[:]),
                            reads=["xt", "ident"], writes=[bk(bnk)])
                    dstap = ytok[slot][:, half * 512:(half + 1) * 512]
                    if half == 0:
                        P.op("act", lambda e, bnk=bnk, dstap=dstap: e.copy(out=dstap, in_=banks[bnk][:]),
                             reads=[bk(bnk)], writes=["ytok%d" % slot])
                    else:
                        P.op("dve", lambda e, bnk=bnk, dstap=dstap: e.tensor_copy(out=dstap, in_=banks[bnk][:]),
                             reads=[bk(bnk)], writes=["ytok%d" % slot])
                t0 = sp_i * SPAN + tt * 128
                key = "yout%d" % len(C.outs)
                C.outs.append(P.dma("sp", lambda e, slot=slot, s=s, t0=t0: e.dma_start(
                    out=C.y_out[s, t0:t0 + 128, :], in_=ytok[slot][:]),
                    reads=["ytok%d" % slot], writes=[key]))


def norm_span_gen(C, l, s, hdst, hkey, T):
    P, banks = C.P, C.banks
    ssb = 2
    kln, krs, kt = T.get("kln", "n_ln"), T.get("krs", "n_rs"), T.get("kt", ["n_t0", "n_t1"])
    for k in range(KC):
        i = k % 2
        P.op("act", lambda e, k=k, i=i: e.activation(out=T["sq"][i][:], in_=C.xt[:, k, :], func=AF.Square),
             reads=["xt"], writes=["n_sq%d" % i])
        P.op("pe", lambda e, k=k, i=i: e.matmul(banks[ssb][:], lhsT=C.ones_bf[:], rhs=T["sq"][i][:],
                                                start=(k == 0), stop=(k == KC - 1)),
             reads=["n_sq%d" % i, "ones_bf"], writes=[bk(ssb)])
        yield
    P.op("act", lambda e: e.activation(out=T["ln"][:], in_=banks[ssb][:], func=AF.Ln,
                                       scale=1.0 / D, bias=C.eps_col[:, 0:1]),
         reads=[bk(ssb), "eps_col"], writes=[kln])
    P.op("act", lambda e: e.activation(out=T["rs"][:], in_=T["ln"][:], func=AF.Exp, scale=-0.5),
         reads=[kln], writes=[krs])
    yield
    for k in range(KC):
        ti = k % 2
        P.op("dve", lambda e, k=k, ti=ti: e.scalar_tensor_tensor(
            out=T["t"][ti][:], in0=C.xt[:, k, :], scalar=C.aS[:, l, s, k:k + 1], in1=T["rs"][:],
            op0=ALU.mult, op1=ALU.mult),
            reads=["xt", "aS", krs], writes=[kt[ti]])
        P.op("act", lambda e, k=k, ti=ti: e.activation(
            out=hdst[:, k, :], in_=T["t"][ti][:], func=AF.Identity, bias=C.modS[:, l, s, k:k + 1]),
            reads=[kt[ti], "modS"], writes=[hkey])
        yield


def norm_span(C, l, s, hdst, hkey, T):
    for _ in norm_span_gen(C, l, s, hdst, hkey, T):
        pass


class Ring:
    def __init__(self, C, tiles, name, plan, src_fn, look):
        self.C, self.tiles, self.name, self.plan, self.src_fn, self.look = C, tiles, name, plan, src_fn, look
        self.issued = 0
        self.n = len(tiles)

    def get(self, i):
        P = self.C.P
        while self.issued <= min(i + self.look, len(self.plan) - 1):
            j = self.issued
            slot = j % self.n
            src = self.src_fn(self.plan[j])
            t = self.tiles[slot]
            P.dma("pool", lambda e, t=t, src=src: e.dma_start(out=t[:], in_=src),
                  writes=["%s%d" % (self.name, slot)])
            self.issued += 1
        slot = i % self.n
        return self.tiles[slot], "%s%d" % (self.name, slot)


def stage_odd(C, les, l, src_scr, dst_scr, fuse_final=False):
    P, nc, banks = C.P, C.nc, C.banks
    li = l // 2

    def lsb(name, shape, dt=F32):
        return les.enter_context(nc.sbuf_tensor("o%d_%s" % (l, name), list(shape), dt))

    hT = lsb("hT", [128, KC, SEQ], BF16)
    cosT = lsb("cosT", [128, SEQ])
    sinS = lsb("sinS", [128, SEQ])
    vt = [lsb("vt%d" % g, [128, 16, 256], BF16) for g in range(3)]
    qT = [lsb("qT%d" % g, [128, SEQ], BF16) for g in range(3)]
    kT = [lsb("kT%d" % g, [128, SEQ], BF16) for g in range(3)]
    wv_t = [lsb("wv%d" % i, [128, 2, KC, 128], BF16) for i in range(2)]
    wq_t = [lsb("wq%d" % i, [128, KC, 128], BF16) for i in range(5)]
    sq = [lsb("sq%d" % i, [128, SPAN], BF16) for i in range(2)]
    qb = [lsb("qb%d" % i, [128, SPAN], BF16) for i in range(2)]
    t1 = [lsb("t1_%d" % i, [128, SPAN]) for i in range(4)]
    t2 = [lsb("t2_%d" % i, [128, SPAN]) for i in range(3)]
    lnv = [lsb("lnv%d" % i, [128, SPAN]) for i in range(2)]
    rs = [lsb("rs%d" % i, [128, SPAN]) for i in range(3)]
    pb = [lsb("pb%d" % i, [128, SPAN], BF16) for i in range(3)]
    pm = [lsb("pm%d" % i, [128, SPAN], BF16) for i in range(3)]
    szt = [lsb("szt%d" % i, [128, SPAN], BF16) for i in range(4)]
    rden = lsb("rden", [128, SPAN])
    otmp = lsb("otmp", [128, SPAN])
    ao = [lsb("ao%d" % i, [128, SPAN], BF16) for i in range(2)]
    atl = [lsb("atl%d" % i, [128, KC, SPAN], BF16) for i in range(2)]
    NT = {"sq": sq, "ln": lnv[0], "rs": rs[0], "t": t1}
    mk4c = lsb("mk4c", [128, SPAN], BF16)
    mk4w = lsb("mk4w", [128, SPAN], BF16)
    mk2 = [lsb("mk2_%d" % b_, [128, SPAN], BF16) for b_ in range(NSPAN)]
    P.op("pool", lambda e: e.tensor_copy(out=mk4c[:].rearrange("p (u n) -> p u n", u=4),
                                         in_=C.maskc[:].unsqueeze(1).broadcast_to([128, 4, 128])),
         reads=["maskc"], writes=["mk4c"])
    P.op("pool", lambda e: e.tensor_copy(out=mk4w[:].rearrange("p (u n) -> p u n", u=4),
                                         in_=C.maskw[:].unsqueeze(1).broadcast_to([128, 4, 128])),
         reads=["maskw"], writes=["mk4w"])
    for b_ in range(NSPAN):
        P.op("pool", lambda e, b_=b_: e.tensor_copy(
            out=mk2[b_][:].rearrange("p (u n) -> p u n", u=16),
            in_=C.maskc[:, 32 * b_:32 * b_ + 32].unsqueeze(1).broadcast_to([128, 16, 32])),
            reads=["maskc"], writes=["mk2_%d" % b_])
    lnk = ["n_ln", "lnv1"]
    rsk = ["n_rs", "rs1", "rs2"]
    t1k = ["n_t0", "n_t1", "t1_2", "t1_3"]

    wq_plan, wv_plan = [], []
    for s in range(2):
        for hp in range(4):
            for g in range(3):
                wv_plan.append(48 + g * 8 + hp * 2)
            for hh in range(2):
                h = hp * 2 + hh
                for g in range(3):
                    for which in range(2):
                        wq_plan.append(("in", which * 24 + g * 8 + h))
                wq_plan.append(("in", 72 + h))
        for sp_i in range(NSPAN):
            for oc in range(KC):
                wq_plan.append(("out", oc))
    wq_ring = Ring(C, wq_t, "wq", wq_plan,
                   lambda d: (C.attw[li, d[1]] if d[0] == "in" else C.attwo[li, d[1]]).rearrange("p (k c) -> p k c", k=KC), 3)
    wv_ring = Ring(C, wv_t, "wv", wv_plan,
                   lambda b0: C.attw[li, b0:b0 + 2].rearrange("b p (k c) -> p b k c", k=KC), 1)
    cnt = {"wq": 0, "wv": 0, "prep": 0, "pp": 0, "st": 0, "sz": 0, "ao": 0, "atl": 0, "sp": 0, "mk": 0, "yt": 0, "ytb": 0}

    PPB = (0, 1, 2, 7)
    SSB = (3, 4)
    PQB = (5, 6)

    def prep_gen(spec):
        n = cnt["prep"]
        cnt["prep"] += 1
        i2, i3, i4 = n % 2, n % 3, n % 4
        ppb = PPB[n % 4]
        ssb = SSB[n % 2]
        pqb = PQB[n % 2]
        blk = spec["blk"]
        if "wt" not in blk:
            blk["wt"], blk["wk"] = wq_ring.get(cnt["wq"])
            cnt["wq"] += 1
        wt, wk = blk["wt"], blk["wk"]
        sp_i = spec["sp_i"]
        c0 = sp_i * SPAN
        wcol, wpcol, dst, dkey = spec["wcol"], spec["wpcol"], spec["dst"], spec["dkey"]
        for k in range(KC):
            P.op("pe", lambda e, k=k: e.matmul(
                banks[ppb][:], lhsT=wt[:, k, :], rhs=hT[:, k, c0:c0 + SPAN],
                start=(k == 0), stop=(k == KC - 1)),
                reads=["hT", wk], writes=[bk(ppb)])
        yield
        P.op("act", lambda e: e.activation(out=sq[i2][:], in_=banks[ppb][:], func=AF.Square),
             reads=[bk(ppb)], writes=["n_sq%d" % i2])
        P.op("act", lambda e: e.copy(out=qb[i2][:], in_=banks[ppb][:]),
             reads=[bk(ppb)], writes=["qb%d" % i2])
        P.op("dve", lambda e: e.scalar_tensor_tensor(
            out=t1[i4][:], in0=banks[ppb][:], scalar=wcol, in1=cosT[:, c0:c0 + SPAN],
            op0=ALU.mult, op1=ALU.mult),
            reads=[bk(ppb), "cosT", "qnw_s", "knw_s"], writes=[t1k[i4]])
        yield
        P.op("pe", lambda e: e.matmul(banks[ssb][:], lhsT=C.ones_bf[:], rhs=sq[i2][:], start=True, stop=True),
             reads=["n_sq%d" % i2, "ones_bf"], writes=[bk(ssb)])
        P.op("pe", lambda e: e.matmul(banks[pqb][:], lhsT=C.perm_bf[:], rhs=qb[i2][:], start=True, stop=True),
             reads=["qb%d" % i2, "perm_bf"], writes=[bk(pqb)])
        yield
        P.op("dve", lambda e: e.scalar_tensor_tensor(
            out=t2[i3][:], in0=banks[pqb][:], scalar=wpcol, in1=sinS[:, c0:c0 + SPAN],
            op0=ALU.mult, op1=ALU.mult),
            reads=[bk(pqb), "sinS", "qnw_s", "knw_s"], writes=["t2_%d" % i3])
        P.op("act", lambda e: e.activation(out=rs[i3][:], in_=banks[ssb][:], func=AF.Ln,
                                           bias=C.eps_col[:, 1:2]),
             reads=[bk(ssb), "eps_col"], writes=[rsk[i3]])
        P.op("act", lambda e: e.activation(out=rs[i3][:], in_=rs[i3][:], func=AF.Exp, scale=-0.5),
             reads=[rsk[i3]], writes=[rsk[i3]])
        yield
        P.op("dve", lambda e: e.tensor_tensor(out=t1[i4][:], in0=t1[i4][:], in1=t2[i3][:], op=ALU.add),
             reads=[t1k[i4], "t2_%d" % i3], writes=[t1k[i4]])
        P.op("pool", lambda e: e.tensor_tensor(out=dst, in0=t1[i4][:], in1=rs[i3][:], op=ALU.mult),
             reads=[t1k[i4], rsk[i3]], writes=[dkey])

    def run_skewed(specs):
        active = []
        for sp_ in specs:
            active.append(prep_gen(sp_))
            for g_ in list(active):
                try:
                    next(g_)
                except StopIteration:
                    active.remove(g_)
        while active:
            for g_ in list(active):
                try:
                    next(g_)
                except StopIteration:
                    active.remove(g_)

    def attn_batch(units, mask_ap, den_out):
        si = cnt["st"] % 2
        cnt["st"] += 1
        stb = 4 + si
        tot = sum(u[4] for u in units)
        off = 0
        offs = []
        for (k_ap, q_ap, v_ap, o_ap, n, rk) in units:
            offs.append(off)
            P.op("pe", lambda e, k_ap=k_ap, q_ap=q_ap, off=off, n=n: e.matmul(
                banks[stb][:, off:off + n], lhsT=k_ap, rhs=q_ap, start=True, stop=True),
                reads=rk, writes=[bk(stb)])
            off += n
        P.op("act", lambda e: e.activation(out=pb[si][:, 0:tot], in_=banks[stb][:, 0:tot],
                                           func=AF.Exp, scale=math.sqrt(128.0)),
             reads=[bk(stb)], writes=["pb%d" % si])
        nun = len(units)
        n0 = units[0][4]
        P.op("pool", lambda e: e.tensor_tensor(
            out=pm[si][:, 0:tot].rearrange("p (u n) -> p u n", u=nun),
            in0=pb[si][:, 0:tot].rearrange("p (u n) -> p u n", u=nun),
            in1=mask_ap.unsqueeze(1).broadcast_to([128, nun, n0]), op=ALU.mult),
            reads=["pb%d" % si, "maskc", "maskw"], writes=["pm%d" % si])
        for (k_ap, q_ap, v_ap, o_ap, n, rk), off in zip(units, offs):
            P.op("pe", lambda e, v_ap=v_ap, o_ap=o_ap, off=off, n=n: e.matmul(
                o_ap, lhsT=v_ap, rhs=pm[si][:, off:off + n], start=False, stop=False,
                skip_group_check=True),
                reads=["pm%d" % si, "vt"], writes=[bk(6)])
        P.op("pe", lambda e: e.matmul(den_out, lhsT=C.ones_bf[:], rhs=pm[si][:, 0:tot],
                                      start=False, stop=False, skip_group_check=True),
             reads=["pm%d" % si, "ones_bf"], writes=[bk(7)])

    ang, kf, kcp = t2[0], t2[1], lnv[1]
    c1 = float(np.float32(6.28125))
    c2 = float(np.float32(2 * PI - 6.28125))
    c3 = float(2 * PI - 6.28125 - c2)

    def wrap(buf, key):
        P.op("dve", lambda e: e.tensor_scalar(out=kf[:], in0=buf[:], scalar1=PI, scalar2=-2 * PI,
                                              op0=ALU.is_gt, op1=ALU.mult), reads=[key], writes=["t2_1"])
        P.op("dve", lambda e: e.tensor_tensor(out=buf[:], in0=buf[:], in1=kf[:], op=ALU.add),
             reads=[key, "t2_1"], writes=[key])
        P.op("dve", lambda e: e.tensor_scalar(out=kf[:], in0=buf[:], scalar1=-PI, scalar2=2 * PI,
                                              op0=ALU.is_lt, op1=ALU.mult), reads=[key], writes=["t2_1"])
        P.op("dve", lambda e: e.tensor_tensor(out=buf[:], in0=buf[:], in1=kf[:], op=ALU.add),
             reads=[key, "t2_1"], writes=[key])
        P.op("dve", lambda e: e.tensor_scalar(out=buf[:], in0=buf[:], scalar1=PI, scalar2=-PI,
                                              op0=ALU.min, op1=ALU.max), reads=[key], writes=[key])

    for s in range(2):
        for cch in range(NSPAN):
            c0 = cch * SPAN
            posi = kcp[:].bitcast(I32)
            P.dma("sp", lambda e, s=s, c0=c0, posi=posi: e.dma_start(
                out=posi, in_=C.pos[s, c0:c0 + SPAN].partition_broadcast(128)), writes=["lnv1"])
            P.op("dve", lambda e, posi=posi: e.tensor_copy(out=ang[:], in_=posi), reads=["lnv1"], writes=["t2_0"])
            P.op("dve", lambda e: e.tensor_scalar(out=ang[:], in0=ang[:], scalar1=C.misc[:, 0:1], scalar2=None,
                                                  op0=ALU.mult), reads=["t2_0", "misc"], writes=["t2_0"])
            P.op("dve", lambda e: e.tensor_scalar(out=kf[:], in0=ang[:], scalar1=1.0 / (2 * PI), scalar2=None,
                                                  op0=ALU.mult), reads=["t2_0"], writes=["t2_1"])
            P.op("dve", lambda e, posi=posi: e.tensor_copy(out=posi, in_=kf[:]), reads=["t2_1"], writes=["lnv1"])
            P.op("dve", lambda e, posi=posi: e.tensor_copy(out=kf[:], in_=posi), reads=["lnv1"], writes=["t2_1"])
            for cc_ in (c1, c2, c3):
                P.op("dve", lambda e, cc_=cc_: e.scalar_tensor_tensor(
                    out=ang[:], in0=kf[:], scalar=-cc_, in1=ang[:], op0=ALU.mult, op1=ALU.add),
                    reads=["t2_1", "t2_0"], writes=["t2_0"])
            wrap(ang, "t2_0")
            P.op("act", lambda e, c0=c0: e.activation(out=sinS[:, c0:c0 + SPAN], in_=ang[:], func=AF.Sin,
                                                      scale=C.misc[:, 1:2]),
                 reads=["t2_0", "misc"], writes=["sinS"])
            P.op("dve", lambda e: e.tensor_scalar(out=ang[:], in0=ang[:], scalar1=PI / 2, scalar2=None,
                                                  op0=ALU.add), reads=["t2_0"], writes=["t2_0"])
            wrap(ang, "t2_0")
            P.op("act", lambda e, c0=c0: e.activation(out=cosT[:, c0:c0 + SPAN], in_=ang[:], func=AF.Sin),
                 reads=["t2_0"], writes=["cosT"])

        if DBG["odd_stop"] <= 0:
            return
        for sp_i in range(NSPAN):
            P.dma("sp", lambda e, s=s, sp_i=sp_i: e.dma_start(
                out=C.xt[:], in_=src_scr[s, :, :, sp_i * SPAN:(sp_i + 1) * SPAN]),
                reads=[scr_key(src_scr, s, sp_i)], writes=["xt"])
            norm_span(C, l, s, hT[:, :, sp_i * SPAN:(sp_i + 1) * SPAN], "hT", NT)
        if DBG["odd_stop"] <= 1:
            return

        for hp in range(4):
            for g in range(3):
                wvt, wvk = wv_ring.get(cnt["wv"])
                cnt["wv"] += 1
                for tl in range(16):
                    if g == 0:
                        cols = slice(tl * 128, tl * 128 + 128, 1)
                    elif g == 1:
                        r, nb_ = tl // 4, tl % 4
                        cols = slice(nb_ * 512 + r, nb_ * 512 + 512, 4)
                    else:
                        cols = slice(tl, SEQ, 16)
                    ppb = cnt["pp"] % 2
                    cnt["pp"] += 1
                    for k in range(KC):
                        P.op("pe", lambda e, k=k, cols=cols, ppb=ppb, wvt=wvt: e.matmul(
                            banks[ppb][:, 0:256].rearrange("p (b c) -> p b c", b=2),
                            lhsT=hT[:, k, cols], rhs=wvt[:, :, k, :],
                            start=(k == 0), stop=(k == KC - 1)),
                            reads=["hT", wvk], writes=[bk(ppb)])
                    if tl % 2 == 0:
                        P.op("act", lambda e, g=g, tl=tl, ppb=ppb: e.copy(out=vt[g][:, tl, :], in_=banks[ppb][:, 0:256]),
                             reads=[bk(ppb)], writes=["vt"])
                    else:
                        P.op("dve", lambda e, g=g, tl=tl, ppb=ppb: e.tensor_copy(out=vt[g][:, tl, :], in_=banks[ppb][:, 0:256]),
                             reads=[bk(ppb)], writes=["vt"])
            if DBG["odd_stop"] <= 2:
                return
            for hh in range(2):
                h = hp * 2 + hh
                specs = []
                for g in range(3):
                    for which in range(2):
                        nws = C.qnw_s if which == 0 else C.knw_s
                        dstT = qT[g] if which == 0 else kT[g]
                        dkey = ("qT%d" if which == 0 else "kT%d") % g
                        blk = {}
                        for sp_i in range(NSPAN):
                            specs.append(dict(blk=blk, sp_i=sp_i, wcol=nws[:, li, 0:1], wpcol=nws[:, li, 1:2],
                                              dst=dstT[:, sp_i * SPAN:(sp_i + 1) * SPAN], dkey=dkey))
                run_skewed(specs)
                if DBG["odd_stop"] <= 3:
                    return
                wzt, wzk = wq_ring.get(cnt["wq"])
                cnt["wq"] += 1
                for b in range(NSPAN):
                    c0 = b * SPAN
                    ppb = cnt["pp"] % 2
                    cnt["pp"] += 1
                    for k in range(KC):
                        P.op("pe", lambda e, k=k, ppb=ppb, wzt=wzt, c0=c0: e.matmul(
                            banks[ppb][:], lhsT=wzt[:, k, :], rhs=hT[:, k, c0:c0 + SPAN],
                            start=(k == 0), stop=(k == KC - 1)),
                            reads=["hT", wzk], writes=[bk(ppb)])
                    P.op("act", lambda e, ppb=ppb, b=b: e.activation(out=szt[b][:], in_=banks[ppb][:], func=AF.Silu),
                         reads=[bk(ppb)], writes=["szt%d" % b])
                vsl = slice(hh * 128, hh * 128 + 128)
                seqb = []
                for b in range(NSPAN):
                    c0 = b * SPAN
                    nbk = 3 + (cnt["sp"] % 2)
                    dbk = 5 + (cnt["sp"] % 2)
                    cnt["sp"] += 1
                    NB, DB = banks[nbk], banks[dbk]
                    blist = []
                    units = []
                    for i in range(4):
                        n_ = 4 * b + i
                        units.append((kT[0][:, n_ * 128:n_ * 128 + 128], qT[0][:, n_ * 128:n_ * 128 + 128],
                                      vt[0][:, n_, vsl], NB[:, i * 128:i * 128 + 128], 128,
                                      ["kT0", "qT0", "vt"]))
                    blist.append((units, mk4c, DB[:, 0:512]))
                    units = []
                    for i in range(4):
                        n_ = 4 * b + i
                        if n_ == 0:
                            continue
                        units.append((kT[0][:, (n_ - 1) * 128:n_ * 128], qT[0][:, n_ * 128:n_ * 128 + 128],
                                      vt[0][:, n_ - 1, vsl], NB[:, i * 128:i * 128 + 128], 128,
                                      ["kT0", "qT0", "vt"]))
                    i0 = 4 - len(units)
                    blist.append((units, mk4w, DB[:, i0 * 128:512]))
                    units = []
                    for r in range(4):
                        units.append((kT[1][:, c0 + r:c0 + SPAN:4], qT[1][:, c0 + r:c0 + SPAN:4],
                                      vt[1][:, r * 4 + b, vsl], NB[:, r:SPAN:4], 128,
                                      ["kT1", "qT1", "vt"]))
                    den1 = DB[:].rearrange("p (i r) -> p r i", r=4)
                    blist.append((units, mk4c, den1))
                    if b > 0:
                        units = []
                        for r in range(4):
                            units.append((kT[1][:, c0 - SPAN + r:c0:4], qT[1][:, c0 + r:c0 + SPAN:4],
                                          vt[1][:, r * 4 + b - 1, vsl], NB[:, r:SPAN:4], 128,
                                          ["kT1", "qT1", "vt"]))
                        blist.append((units, mk4w, den1))
                    units = []
                    for r in range(16):
                        units.append((kT[2][:, r:SEQ:16], qT[2][:, c0 + r:c0 + SPAN:16],
                                      vt[2][:, r, vsl], NB[:, r:SPAN:16], 32,
                                      ["kT2", "qT2", "vt"]))
                    den2 = DB[:].rearrange("p (j r) -> p r j", r=16)
                    blist.append((units, mk2[b], den2))
                    for bi, (u_, m_, d_) in enumerate(blist):
                        seqb.append(dict(units=u_, mask=m_, den=d_, b=b, first=(bi == 0), last=(bi == len(blist) - 1),
                                         nbk=nbk, dbk=dbk))

                def emit_qk(bt):
                    si = cnt["st"] % 3
                    cnt["st"] += 1
                    bt["si"] = si
                    stb = (0, 1, 2)[si]
                    bt["stb"] = stb
                    off = 0
                    bt["offs"] = []
                    for (k_ap, q_ap, v_ap, o_ap, n, rk) in bt["units"]:
                        bt["offs"].append(off)
                        P.op("pe", lambda e, k_ap=k_ap, q_ap=q_ap, off=off, n=n, stb=stb: e.matmul(
                            banks[stb][:, off:off + n], lhsT=k_ap, rhs=q_ap, start=True, stop=True),
                            reads=rk, writes=[bk(stb)])
                        off += n
                    bt["tot"] = off

                def emit_rest(bt):
                    si, stb, tot = bt["si"], bt["stb"], bt["tot"]
                    nbk, dbk = bt["nbk"], bt["dbk"]
                    P.op("act", lambda e: e.activation(out=pb[si][:, 0:tot], in_=banks[stb][:, 0:tot],
                                                       func=AF.Exp, scale=math.sqrt(128.0)),
                         reads=[bk(stb)], writes=["pb%d" % si])
                    nun = len(bt["units"])
                    n0 = bt["units"][0][4]
                    mask_ap = bt["mask"]
                    meng = "dve" if DBG.get("dvemask", 1) else "pool"
                    cnt["mk"] += 1
                    P.op(meng, lambda e: e.tensor_tensor(
                        out=pm[si][:, 0:tot], in0=pb[si][:, 0:tot], in1=mask_ap[:, 0:tot], op=ALU.mult),
                        reads=["pb%d" % si, "mk4c", "mk4w", "mk2_0", "mk2_1", "mk2_2", "mk2_3"], writes=["pm%d" % si])
                    for (k_ap, q_ap, v_ap, o_ap, n, rk), off in zip(bt["units"], bt["offs"]):
                        P.op("pe", lambda e, v_ap=v_ap, o_ap=o_ap, off=off, n=n: e.matmul(
                            o_ap, lhsT=v_ap, rhs=pm[si][:, off:off + n], start=False, stop=False,
                            skip_group_check=True),
                            reads=["pm%d" % si, "vt"], writes=[bk(nbk)])
                    den_out = bt["den"]
                    P.op("pe", lambda e: e.matmul(den_out, lhsT=C.ones_bf[:], rhs=pm[si][:, 0:tot],
                                                  start=False, stop=False, skip_group_check=True),
                         reads=["pm%d" % si, "ones_bf"], writes=[bk(dbk)])
                def emit_end(bt):
                    nbk, dbk = bt["nbk"], bt["dbk"]
                    if True:
                        b = bt["b"]
                        c0 = b * SPAN
                        P.op("act", lambda e: e.activation(out=rden[:], in_=banks[dbk][:], func=AF.Ln),
                             reads=[bk(dbk)], writes=["rden"])
                        P.op("act", lambda e: e.activation(out=rden[:], in_=rden[:], func=AF.Exp, scale=-1.0),
                             reads=["rden"], writes=["rden"])
                        P.op("dve", lambda e: e.tensor_tensor(out=otmp[:], in0=banks[nbk][:], in1=rden[:], op=ALU.mult),
                             reads=[bk(nbk), "rden"], writes=["otmp"])
                        ai = cnt["ao"] % 2
                        cnt["ao"] += 1
                        P.op("pool", lambda e, b=b, ai=ai: e.tensor_tensor(
                            out=ao[ai][:], in0=otmp[:], in1=szt[b][:], op=ALU.mult),
                            reads=["otmp", "szt%d" % b], writes=["ao%d" % ai])
                        P.dma("sp", lambda e, s=s, h=h, c0=c0, ai=ai: e.dma_start(
                            out=C.ats[s, :, h, c0:c0 + SPAN], in_=ao[ai][:]),
                            reads=["ao%d" % ai], writes=["ats_%d_%d_%d" % (s, h, b)])
                        if b + 2 < NSPAN:
                            P.op("dve", lambda e: e.memset(banks[nbk][:], 0.0), writes=[bk(nbk)])
                            P.op("dve", lambda e: e.memset(banks[dbk][:], 0.0), writes=[bk(dbk)])

                LOOK = 2
                firsts = [bt for bt in seqb if bt["first"]]
                for i_, bt in enumerate(firsts):
                    bt["nxt"] = (firsts[i_ + 1]["nbk"], firsts[i_ + 1]["dbk"]) if i_ + 1 < len(firsts) else None
                for f_ in firsts[:2]:
                    n0_, d0_ = f_["nbk"], f_["dbk"]
                    P.op("dve", lambda e, n0_=n0_: e.memset(banks[n0_][:], 0.0), writes=[bk(n0_)])
                    P.op("dve", lambda e, d0_=d0_: e.memset(banks[d0_][:], 0.0), writes=[bk(d0_)])
                for i in range(min(LOOK, len(seqb))):
                    emit_qk(seqb[i])
                pend = []
                for i, bt in enumerate(seqb):
                    if i + LOOK < len(seqb):
                        emit_qk(seqb[i + LOOK])
                    emit_rest(bt)
                    for pe_ in list(pend):
                        pe_[1] -= 1
                        if pe_[1] <= 0:
                            emit_end(pe_[0])
                            pend.remove(pe_)
                    if bt["last"]:
                        pend.append([bt, 2])
                for pe_ in pend:
                    emit_end(pe_[0])

        if DBG["odd_stop"] <= 4:
            return
        hv = hT[:].bitcast(F32)
        xbufs = [(C.xt[:], "xt", []), (hv[:, :, 0:SPAN], "hTa", ["hT"]), (hv[:, :, SPAN:2 * SPAN], "hTb", ["hT"])]

        def p3_load(sp_j):
            c0_ = sp_j * SPAN
            lj = sp_j % 2
            xb_, xk_, ex_ = xbufs[sp_j % 3]
            P.dma("sp", lambda e, s=s, c0_=c0_, lj=lj: e.dma_start(out=atl[lj][:], in_=C.ats[s, :, :, c0_:c0_ + SPAN]),
                  reads=["ats_%d_%d_%d" % (s, h_, sp_j) for h_ in range(KC)], writes=["atl%d" % lj])
            P.dma("sp", lambda e, s=s, c0_=c0_, xb_=xb_: e.dma_start(out=xb_, in_=src_scr[s, :, :, c0_:c0_ + SPAN]),
                  reads=[scr_key(src_scr, s, sp_j)], writes=[xk_] + ex_)

        p3_load(0)
        p3_load(1)
        for sp_i in range(NSPAN):
            c0 = sp_i * SPAN
            li_ = sp_i % 2
            xb, xk0, xex = xbufs[sp_i % 3]
            xk = xk0
            for oc in range(KC):
                wt, wk = wq_ring.get(cnt["wq"])
                cnt["wq"] += 1
                ppb = cnt["pp"] % 2
                cnt["pp"] += 1
                for k in range(KC):
                    P.op("pe", lambda e, k=k, ppb=ppb, wt=wt, li_=li_: e.matmul(
                        banks[ppb][:], lhsT=wt[:, k, :], rhs=atl[li_][:, k, :],
                        start=(k == 0), stop=(k == KC - 1)),
                        reads=[wk, "atl%d" % li_], writes=[bk(ppb)])
                P.op("dve", lambda e, oc=oc, ppb=ppb, s=s, xb=xb: e.scalar_tensor_tensor(
                    out=xb[:, oc, :], in0=banks[ppb][:], scalar=C.modS[:, l, s, 16 + oc:17 + oc],
                    in1=xb[:, oc, :], op0=ALU.mult, op1=ALU.add),
                    reads=[bk(ppb), "modS", xk] + xex, writes=[xk])
            if not fuse_final:
                P.dma("sp", lambda e, s=s, c0=c0, xb=xb: e.dma_start(out=dst_scr[s, :, :, c0:c0 + SPAN], in_=xb),
                      reads=[xk] + xex, writes=[scr_key(dst_scr, s, sp_i)])
            else:
                for tt_ in range(4):
                    slot = cnt["yt"] % 2
                    cnt["yt"] += 1
                    for half in range(2):
                        bnk = 4 + (cnt["ytb"] % 4)
                        cnt["ytb"] += 1
                        for kk in range(4):
                            k = half * 4 + kk
                            P.op("pe", lambda e, k=k, kk=kk, bnk=bnk, tt_=tt_, xb=xb: e.transpose(
                                banks[bnk][:, kk * 128:(kk + 1) * 128],
                                xb[:, k, tt_ * 128:(tt_ + 1) * 128], C.ident[:]),
                                reads=[xk, "ident"] + xex, writes=[bk(bnk)])
                        yi = (slot * 2 + half) % 4
                        dstap = t1[yi][:]
                        if half == 0:
                            P.op("act", lambda e, bnk=bnk, dstap=dstap: e.copy(out=dstap, in_=banks[bnk][:]),
                                 reads=[bk(bnk)], writes=[t1k[yi]])
                        else:
                            P.op("dve", lambda e, bnk=bnk, dstap=dstap: e.tensor_copy(out=dstap, in_=banks[bnk][:]),
                                 reads=[bk(bnk)], writes=[t1k[yi]])
                        t0 = c0 + tt_ * 128
                        key = "yout%d" % len(C.outs)
                        C.outs.append(P.dma("sp", lambda e, yi=yi, s=s, t0=t0, half=half: e.dma_start(
                            out=C.y_out[s, t0:t0 + 128, half * 512:(half + 1) * 512], in_=t1[yi][:]),
                            reads=[t1k[yi]], writes=[key]))
            if sp_i + 2 < NSPAN:
                p3_load(sp_i + 2)


def even_setup(C, les, ses, l):
    P, nc, banks = C.P, C.nc, C.banks
    li = l // 2
    W = Ctx()

    def lsb(name, shape, dt=F32):
        return les.enter_context(nc.sbuf_tensor("e%d_%s" % (l, name), list(shape), dt))

    def tsb(name, shape, dt=F32):
        return ses.enter_context(nc.sbuf_tensor("es%d_%s" % (l, name), list(shape), dt))

    W.RW = lsb("RW", [128, 4, 8, 2, 128], BF16)
    W.OW = lsb("OW", [128, 16, 9, 2, 32], BF16)
    W.KW = lsb("KW", [128, 4, 8, 128], BF16)
    W.SCr = lsb("SCr", [128, 7, 16])
    W.SCi = lsb("SCi", [128, 7, 16])
    W.SCn = lsb("SCn", [128, 7, 16])
    W.Er = lsb("Er", [128, 16, 128])
    W.Ei = lsb("Ei", [128, 16, 128])
    W.rho8 = lsb("rho8", [128, 16])
    W.U1r = lsb("U1r", [128, 16])
    W.U1i = lsb("U1i", [128, 16])
    W.convw = lsb("convw", [128, 4, 31])
    W.cvec = lsb("cvec", [128, 5, 4])
    W.gluw = lsb("gluw", [128, 4, 512], BF16)
    W.mark = lsb("mark", [128, 2])
    p1 = tsb("p1", [128, 5, 256])
    p2a = tsb("p2a", [128, 3, 16])
    p2b = tsb("p2b", [128, 4, 16, 16])
    K_ = []
    chains = {"a": [], "b": []}
    chn = ["b"]

    def rec(eng, fn, reads=(), writes=()):
        key = ["su_" + chn[0]]
        chains[chn[0]].append((eng, fn, list(reads) + key, list(writes) + key))
    P.dma("sp", lambda e: e.dma_start(out=W.convw[:], in_=C.convw[:, li]), writes=["e_convw"])
    P.dma("sp", lambda e: e.dma_start(out=W.cvec[:], in_=C.cvec[:, li]), writes=["e_cvec"])
    P.dma("pool", lambda e: e.dma_start(out=W.gluw[:], in_=C.gluw[li]), writes=["e_gluw"])
    P.dma("sp", lambda e: e.dma_start(out=p1[:], in_=C.s5l1[:, li]), writes=["su_a"])
    P.dma("sp", lambda e: e.dma_start(out=p2a[:], in_=C.s5l2a[:, li]), writes=["su_b"])
    P.dma("sp", lambda e: e.dma_start(out=p2b[:], in_=C.s5l2b[:, li].rearrange("p f (a b) -> p f a b", a=16)), writes=["su_b"])

    def tt(out, a, b, op):
        rec("dve", lambda e: e.tensor_tensor(out=out, in0=a, in1=b, op=op), reads=K_, writes=K_)

    def ts(out, a, s1, op0, s2=None, op1=None):
        if op1 is None:
            rec("dve", lambda e: e.tensor_scalar(out=out, in0=a, scalar1=s1, scalar2=None, op0=op0), reads=K_, writes=K_)
        else:
            rec("dve", lambda e: e.tensor_scalar(out=out, in0=a, scalar1=s1, scalar2=s2, op0=op0, op1=op1), reads=K_, writes=K_)

    def stt(out, a, sc, b, op0, op1):
        rec("dve", lambda e: e.scalar_tensor_tensor(out=out, in0=a, scalar=sc, in1=b, op0=op0, op1=op1), reads=K_, writes=K_)

    def aexp(out, a, scale=1.0):
        rec("act", lambda e: e.activation(out=out, in_=a, func=AF.Exp, scale=scale), reads=K_, writes=K_)

    def cmul(outr, outi, ar, ai, br, bi, t1, t2):
        tt(t1, ar, br, ALU.mult)
        tt(t2, ai, bi, ALU.mult)
        tt(outr, t1, t2, ALU.subtract)
        tt(t1, ar, bi, ALU.mult)
        tt(t2, ai, br, ALU.mult)
        tt(outi, t1, t2, ALU.add)

    def cexp_kappa(pre, lamre, lamim, logdt, shape):
        T = lambda n: tsb(pre + n, shape)
        dt, zr, zi, rho, th, q, c, s, t1, t2 = [T(n) for n in ("dt", "zr", "zi", "rho", "th", "q", "c", "s", "t1", "t2")]
        Ar, Ai, kr, ki, nr = [T(n) for n in ("Ar", "Ai", "kr", "ki", "nr")]
        aexp(dt[:], logdt)
        tt(zr[:], lamre, dt[:], ALU.mult)
        tt(zi[:], lamim, dt[:], ALU.mult)
        aexp(rho[:], zr[:])
        ts(th[:], zi[:], 1.0 / 64, ALU.mult)
        tt(q[:], th[:], th[:], ALU.mult)
        ca = [-1.0 / 2, 1.0 / 24, -1.0 / 720, 1.0 / 40320]
        sa = [-1.0 / 6, 1.0 / 120, -1.0 / 5040, 1.0 / 362880]
        ts(c[:], q[:], ca[3], ALU.mult)
        for a_ in (ca[2], ca[1], ca[0]):
            stt(c[:], c[:], a_, q[:], ALU.add, ALU.mult)
        ts(c[:], c[:], 1.0, ALU.add)
        ts(s[:], q[:], sa[3], ALU.mult)
        for a_ in (sa[2], sa[1], sa[0]):
            stt(s[:], s[:], a_, q[:], ALU.add, ALU.mult)
        stt(s[:], s[:], 1.0, th[:], ALU.add, ALU.mult)
        for _ in range(6):
            tt(t1[:], c[:], c[:], ALU.mult)
            tt(t2[:], s[:], s[:], ALU.mult)
            stt(s[:], c[:], 2.0, s[:], ALU.mult, ALU.mult)
            tt(c[:], t1[:], t2[:], ALU.subtract)
        tt(Ar[:], rho[:], c[:], ALU.mult)
        tt(Ai[:], rho[:], s[:], ALU.mult)
        ts(nr[:], Ar[:], -1.0, ALU.add)
        tt(t1[:], lamre, lamre, ALU.mult)
        tt(t2[:], lamim, lamim, ALU.mult)
        tt(t1[:], t1[:], t2[:], ALU.add)
        rec("dve", lambda e: e.reciprocal(out=q[:], in_=t1[:]), reads=K_, writes=K_)
        tt(t1[:], nr[:], lamre, ALU.mult)
        tt(t2[:], Ai[:], lamim, ALU.mult)
        tt(t1[:], t1[:], t2[:], ALU.add)
        tt(kr[:], t1[:], q[:], ALU.mult)
        tt(t1[:], Ai[:], lamre, ALU.mult)
        tt(t2[:], nr[:], lamim, ALU.mult)
        tt(t1[:], t1[:], t2[:], ALU.subtract)
        tt(ki[:], t1[:], q[:], ALU.mult)
        return Ar, Ai, kr, ki, t1, t2, c, s, rho

    chn[0] = "a"
    A1r, A1i, k1r, k1i, u1, u2, _c1, _s1, _r1 = cexp_kappa("a", p1[:, 0, :], p1[:, 1, :], p1[:, 2, :], [128, 256])
    cur = [(tsb("cur%dr" % i, [128, 256]), tsb("cur%di" % i, [128, 256])) for i in range(2)]
    cmul(cur[0][0][:], cur[0][1][:], k1r[:], k1i[:], p1[:, 3, :], p1[:, 4, :], u1[:], u2[:])
    for k in range(8):
        s_ = 7 - k
        cr_, ci_ = cur[k % 2]
        for part, src in ((0, cr_), (1, ci_)):
            v = src[:].rearrange("p (c q) -> p c q", c=4)
            ts(W.RW[:, :, s_, part, 0:64], v, C.misc[:, 2:3], ALU.mult)
            ts(W.RW[:, :, s_, part, 64:128], v, C.misc[:, 3:4], ALU.mult)
        if k < 7:
            nr_, ni_ = cur[(k + 1) % 2]
            cmul(nr_[:], ni_[:], cr_[:], ci_[:], A1r[:], A1i[:], u1[:], u2[:])

    if DBG["setup_stop"] <= 1:
        return W
    chn[0] = "b"
    A2r, A2i, k2r, k2i, v1, v2, c2_, s2_, rho2 = cexp_kappa("b", p2a[:, 0, :], p2a[:, 1, :], p2a[:, 2, :], [128, 16])
    PWr = tsb("PWr", [128, 9, 16])
    PWi = tsb("PWi", [128, 9, 16])
    rec("dve", lambda e: e.memset(PWr[:, 0, :], 1.0), reads=K_, writes=K_)
    rec("dve", lambda e: e.memset(PWi[:, 0, :], 0.0), reads=K_, writes=K_)
    for k in range(1, 9):
        cmul(PWr[:, k, :], PWi[:, k, :], PWr[:, k - 1, :], PWi[:, k - 1, :], A2r[:], A2i[:], v1[:], v2[:])
    rec("dve", lambda e: e.tensor_copy(out=W.SCr[:, 0, :], in_=PWr[:, 8, :]), reads=K_, writes=K_)
    rec("dve", lambda e: e.tensor_copy(out=W.SCi[:, 0, :], in_=PWi[:, 8, :]), reads=K_, writes=K_)
    for m in range(1, 7):
        tt(v1[:], W.SCr[:, m - 1, :], W.SCr[:, m - 1, :], ALU.mult)
        tt(v2[:], W.SCi[:, m - 1, :], W.SCi[:, m - 1, :], ALU.mult)
        tt(W.SCr[:, m, :], v1[:], v2[:], ALU.subtract)
        stt(W.SCi[:, m, :], W.SCr[:, m - 1, :], 2.0, W.SCi[:, m - 1, :], ALU.mult, ALU.mult)
    ts(W.SCn[:], W.SCi[:], -1.0, ALU.mult)
    um_r = tsb("um_r", [128, 16])
    um_i = tsb("um_i", [128, 16])
    e1 = tsb("e1", [128, 16, 64])
    e2 = tsb("e2", [128, 16, 64])
    rec("dve", lambda e: e.tensor_copy(out=um_r[:], in_=c2_[:]), reads=K_, writes=K_)
    rec("dve", lambda e: e.tensor_copy(out=um_i[:], in_=s2_[:]), reads=K_, writes=K_)
    rec("dve", lambda e: e.tensor_copy(out=W.rho8[:], in_=rho2[:]), reads=K_, writes=K_)
    for _ in range(3):
        tt(v1[:], um_r[:], um_r[:], ALU.mult)
        tt(v2[:], um_i[:], um_i[:], ALU.mult)
        stt(um_i[:], um_r[:], 2.0, um_i[:], ALU.mult, ALU.mult)
        tt(um_r[:], v1[:], v2[:], ALU.subtract)
        tt(W.rho8[:], W.rho8[:], W.rho8[:], ALU.mult)
    rec("dve", lambda e: e.tensor_copy(out=W.U1r[:], in_=um_r[:]), reads=K_, writes=K_)
    rec("dve", lambda e: e.tensor_copy(out=W.U1i[:], in_=um_i[:]), reads=K_, writes=K_)
    rec("dve", lambda e: e.memset(W.Er[:, :, 0:1], 1.0), reads=K_, writes=K_)
    rec("dve", lambda e: e.memset(W.Ei[:, :, 0:1], 0.0), reads=K_, writes=K_)
    for m in range(7):
        sh = 1 << m
        ur = um_r[:].unsqueeze(2).broadcast_to([128, 16, sh])
        ui = um_i[:].unsqueeze(2).broadcast_to([128, 16, sh])
        tt(e1[:, :, 0:sh], W.Er[:, :, 0:sh], ur, ALU.mult)
        tt(e2[:, :, 0:sh], W.Ei[:, :, 0:sh], ui, ALU.mult)
        tt(W.Er[:, :, sh:2 * sh], e1[:, :, 0:sh], e2[:, :, 0:sh], ALU.subtract)
        tt(e1[:, :, 0:sh], W.Er[:, :, 0:sh], ui, ALU.mult)
        tt(e2[:, :, 0:sh], W.Ei[:, :, 0:sh], ur, ALU.mult)
        tt(W.Ei[:, :, sh:2 * sh], e1[:, :, 0:sh], e2[:, :, 0:sh], ALU.add)
        if m < 6:
            tt(v1[:], um_r[:], um_r[:], ALU.mult)
            tt(v2[:], um_i[:], um_i[:], ALU.mult)
            stt(um_i[:], um_r[:], 2.0, um_i[:], ALU.mult, ALU.mult)
            tt(um_r[:], v1[:], v2[:], ALU.subtract)
    B2r = tsb("B2r", [128, 16, 16])
    B2i = tsb("B2i", [128, 16, 16])
    w1 = tsb("w1", [128, 16, 16])
    w2 = tsb("w2", [128, 16, 16])
    bc = lambda t: t[:].unsqueeze(2).broadcast_to([128, 16, 16])
    cmul(B2r[:], B2i[:], bc(k2r), bc(k2i), p2b[:, 0], p2b[:, 1], w1[:], w2[:])
    BBbd = tsb("BBbd", [128, 16, 2, 32], BF16)
    rec("dve", lambda e: e.memset(BBbd[:], 0.0), reads=K_, writes=K_)
    for part, src in ((0, B2r), (1, B2i)):
        rec("dve", lambda e, part=part, src=src: e.tensor_copy(out=BBbd[0:64, :, part, 0:16], in_=src[0:64]), reads=K_, writes=K_)
        rec("dve", lambda e, part=part, src=src: e.tensor_copy(out=BBbd[64:128, :, part, 16:32], in_=src[64:128]), reads=K_, writes=K_)
    rec("dve", lambda e: e.memset(W.OW[:], 0.0), reads=K_, writes=K_)
    Cr, Ci = p2b[:, 2], p2b[:, 3]
    for k in range(9):
        prb = PWr[:, k, :].unsqueeze(2).broadcast_to([128, 16, 16])
        pib = PWi[:, k, :].unsqueeze(2).broadcast_to([128, 16, 16])
        tt(w1[:], Cr, prb, ALU.mult)
        tt(w2[:], Ci, pib, ALU.mult)
        tt(W.OW[0:64, :, k, 0, 0:16], w1[0:64], w2[0:64], ALU.subtract)
        tt(W.OW[64:128, :, k, 0, 16:32], w1[64:128], w2[64:128], ALU.subtract)
        tt(w1[:], Cr, pib, ALU.mult)
        tt(w2[:], Ci, prb, ALU.mult)
        stt(W.OW[0:64, :, k, 1, 0:16], w1[0:64], -1.0, w2[0:64], ALU.mult, ALU.subtract)
        stt(W.OW[64:128, :, k, 1, 16:32], w1[64:128], -1.0, w2[64:128], ALU.mult, ALU.subtract)
    if DBG["setup_stop"] <= 2:
        return W
    rec("dve", lambda e: e.memset(W.KW[:], 0.0), reads=K_, writes=K_)
    for cc in range(4):
        for half in range(2):
            b = half
            for jl in range(4):
                j = cc * 4 + jl
                outap = banks[b][32 * jl:32 * jl + 32, :].rearrange("p (t c) -> p t c", t=4)[:, :, 32 * jl:32 * jl + 32]
                for part in range(2):
                    rec("pe", lambda e, outap=outap, j=j, part=part, half=half, jl=jl: e.matmul(
                        outap, lhsT=BBbd[:, j, part, :], rhs=W.OW[:, j, half * 4:half * 4 + 4, part, :],
                        start=(part == 0), stop=(part == 1), tile_position=(0, 32 * jl), skip_group_check=True),
                        reads=K_, writes=[bk(b)])
            for jl in range(4):
                src = banks[b][32 * jl:32 * jl + 32, :].rearrange("p (t c) -> p t c", t=4)[:, :, 32 * jl:32 * jl + 32]
                rec("dve", lambda e, src=src, cc=cc, half=half, jl=jl: e.tensor_copy(
                    out=W.KW[32 * jl:32 * jl + 32, cc, half * 4:half * 4 + 4, 32 * jl:32 * jl + 32], in_=src),
                    reads=[bk(b)] + K_, writes=K_)
    for cc in range(4):
        rec("dve", lambda e, cc=cc: e.scalar_tensor_tensor(
            out=W.KW[:, cc, 0, :], in0=C.ident[:], scalar=W.cvec[:, 3, cc:cc + 1], in1=W.KW[:, cc, 0, :],
            op0=ALU.mult, op1=ALU.add), reads=K_ + ["e_cvec", "ident"], writes=K_)
    la, lb = chains["a"], chains["b"]
    ia = ib = 0
    while ia < len(la) or ib < len(lb):
        if ib < len(lb):
            P.op(*lb[ib][:2], reads=lb[ib][2], writes=lb[ib][3])
            ib += 1
        if ia < len(la) and ia * len(lb) <= ib * len(la):
            P.op(*la[ia][:2], reads=la[ia][2], writes=la[ia][3])
            ia += 1
    P.op("dve", lambda e: e.memset(W.mark[:], 0.0), reads=["su_a", "su_b"], writes=["su"])
    return W


def stage_even(C, les, l, W, src_scr, dst_scr):
    P, nc, banks = C.P, C.nc, C.banks
    li = l // 2
    SEG = 1024

    def lsb(name, shape, dt=F32):
        return les.enter_context(nc.sbuf_tensor("e%d_%s" % (l, name), list(shape), dt))

    hT = lsb("hT", [128, KC, SEG], BF16)
    useg = lsb("useg", [128, 4, SEG], BF16)
    yaseg = lsb("yaseg", [128, 4, SEG], BF16)
    szseg = lsb("szseg", [128, 4, SEG], BF16)
    Xp = [lsb("Xp%d" % i, [128, 16, 128], BF16) for i in range(2)]
    Xl = [lsb("Xl%d" % i, [128, 16]) for i in range(2)]
    wq_t = [lsb("wq%d" % i, [128, KC, 128], BF16) for i in range(3)]
    abuf = [lsb("abuf%d" % i, [128, 4, 542], BF16) for i in range(2)]
    halo = lsb("halo", [128, 4, 30], BF16)
    sg = [lsb("sg%d" % i, [128, SPAN], BF16) for i in range(2)]
    azb = [lsb("azb%d" % i, [128, 4, SPAN], BF16) for i in range(2)]
    ycv = lsb("ycv", [128, 4, SPAN])
    ysq = lsb("ysq", [128, 4, SPAN])
    tpool = lsb("tpool", [128, 8, SPAN])
    st_mean, st_b, st_c = tpool[:, 0, :], tpool[:, 1, :], tpool[:, 2, :]
    tmpA = [tpool[:, 3, :], tpool[:, 4, :]]
    tmpB = tpool[:, 5, :]
    dg = [lsb("dg%d" % i, [128, 16, 128], BF16) for i in range(2)]
    sqn = [lsb("sqn%d" % i, [128, SPAN], BF16) for i in range(2)]
    ysb = lsb("ysb", [128, 4, SPAN], BF16)
    gs = tpool[:, 6, :]
    mixb = lsb("mixb", [128, 4, SPAN], BF16)
    cw1 = lsb("cw1", [128, 16])
    cw2 = lsb("cw2", [128, 16])
    zi0 = lsb("zi0", [128, 16])
    zi1 = lsb("zi1", [128, 16])
    NT = {"sq": sqn, "ln": st_b, "rs": st_c, "t": tmpA}
    XA = [ycv[:].rearrange("p c t -> p (c t)").rearrange("p (j c) -> p j c", j=16),
          ysq[:].rearrange("p c t -> p (c t)").rearrange("p (j c) -> p j c", j=16)]
    XB = [tpool[:, 4 * i:4 * i + 4, :].rearrange("p c t -> p (c t)").rearrange("p (j c) -> p j c", j=16) for i in range(2)]
    XAk = ["ycv0", "ycv1", "ycv2", "ycv3"], ["ysq"]
    XBk = ["st_mean", "n_ln", "n_rs", "n_t0"], ["n_t1", "tmpB", "gs", "tp7"]
    ycvk = ["ycv0", "ycv1", "ycv2", "ycv3"]

    order = [4, 0, 5, 1, 6, 2, 7, 3] + list(range(8, 20))
    plan = []
    for s in range(2):
        for sg_i in range(2):
            for b_ in order:
                plan.append(("in", b_))
            for sp2 in range(2):
                for oc in range(KC):
                    plan.append(("out", oc))
    ring = Ring(C, wq_t, "ewq", plan,
                lambda d: (C.evw[li, d[1]] if d[0] == "in" else C.evwo[li, d[1]]).rearrange("p (k c) -> p k c", k=KC), 2)
    cnt = {"w": 0, "pp": 0, "dg": 0}
    prenormed = set()
    az_f = [azb[j_][:].rearrange("p c t -> p (c t)").bitcast(F32) for j_ in range(2)]
    NT_alt = {"sq": sqn, "ln": az_f[0][:, 0:SPAN], "rs": az_f[0][:, SPAN:2 * SPAN],
              "t": [az_f[1][:, 0:SPAN], az_f[1][:, SPAN:2 * SPAN]],
              "kln": "azb0", "krs": "azb0", "kt": ["azb1", "azb1"]}

    def norm_seg_gen(s_, sg_, T_):
        for sp2_ in range(2):
            sp_i_ = sg_ * 2 + sp2_
            P.dma("sp", lambda e, s_=s_, sp_i_=sp_i_: e.dma_start(
                out=C.xt[:], in_=src_scr[s_, :, :, sp_i_ * SPAN:(sp_i_ + 1) * SPAN]),
                reads=[scr_key(src_scr, s_, sp_i_)], writes=["xt"])
            yield
            for _ in norm_span_gen(C, l, s_, hT[:, :, sp2_ * SPAN:(sp2_ + 1) * SPAN], "ehT", T_):
                yield

    class Rec:
        def __init__(self):
            self.ops = []

        def op(self, eng, fn, reads=(), writes=()):
            self.ops.append((eng, fn, list(reads), list(writes)))

    if DBG["even_stop"] <= 0:
        return
    for s in range(2):
        P.op("dve", lambda e: e.memset(halo[:], 0.0), writes=["halo"])
        P.op("dve", lambda e: e.memset(Xl[0][:], 0.0), writes=["Xl"])
        P.op("dve", lambda e: e.memset(Xl[1][:], 0.0), writes=["Xl"])
        for sg_i in range(2):
            if (s, sg_i) not in prenormed:
                for _ in norm_seg_gen(s, sg_i, NT):
                    pass
            for b_ in order:
                wt, wk = ring.get(cnt["w"])
                cnt["w"] += 1
                for sp2 in range(2):
                    ppb = cnt["pp"] % 2
                    cnt["pp"] += 1
                    cs = slice(sp2 * SPAN, (sp2 + 1) * SPAN)
                    for k in range(KC):
                        P.op("pe", lambda e, k=k, ppb=ppb, wt=wt, cs=cs: e.matmul(
                            banks[ppb][:], lhsT=wt[:, k, :], rhs=hT[:, k, cs],
                            start=(k == 0), stop=(k == KC - 1)),
                            reads=["ehT", wk], writes=[bk(ppb)])
                    cc = b_ % 4
                    if 4 <= b_ < 8:
                        P.op("act", lambda e, ppb=ppb, sp2=sp2: e.activation(out=sg[sp2][:], in_=banks[ppb][:], func=AF.Sigmoid),
                             reads=[bk(ppb)], writes=["sg%d" % sp2])
                    elif b_ < 4:
                        P.op("dve", lambda e, ppb=ppb, sp2=sp2, cc=cc: e.tensor_tensor(
                            out=abuf[sp2][:, cc, 30:542], in0=banks[ppb][:], in1=sg[sp2][:], op=ALU.mult),
                            reads=[bk(ppb), "sg%d" % sp2], writes=["abuf%d" % sp2])
                    elif b_ < 12:
                        P.op("act", lambda e, ppb=ppb, sp2=sp2, cc=cc: e.activation(out=azb[sp2][:, cc, :], in_=banks[ppb][:], func=AF.Silu),
                             reads=[bk(ppb)], writes=["azb%d" % sp2])
                    elif b_ < 16:
                        P.op("act", lambda e, ppb=ppb, cs=cs, cc=cc: e.copy(out=useg[:, cc, cs], in_=banks[ppb][:]),
                             reads=[bk(ppb)], writes=["useg"])
                    else:
                        P.op("act", lambda e, ppb=ppb, cs=cs, cc=cc: e.activation(out=szseg[:, cc, cs], in_=banks[ppb][:], func=AF.Silu),
                             reads=[bk(ppb)], writes=["szseg"])
            if DBG["even_stop"] <= 1:
                return
            for sp2 in range(2):
                cs = slice(sp2 * SPAN, (sp2 + 1) * SPAN)
                ab = abuf[sp2]
                abk = "abuf%d" % sp2
                if sp2 == 0:
                    P.op("pool", lambda e: e.tensor_copy(out=abuf[0][:, :, 0:30], in_=halo[:]),
                         reads=["halo"], writes=["abuf0"])
                else:
                    P.op("pool", lambda e: e.tensor_copy(out=abuf[1][:, :, 0:30], in_=abuf[0][:, :, 512:542]),
                         reads=["abuf0"], writes=["abuf1"])
                    P.op("pool", lambda e: e.tensor_copy(out=halo[:], in_=abuf[1][:, :, 512:542]),
                         reads=["abuf1"], writes=["halo"])
                for cc in range(4):
                    cb = 5 + (cc % 2)
                    for hf in range(2):
                        j0, nj = (0, 16) if hf == 0 else (16, 15)
                        di = cnt["dg"] % 2
                        cnt["dg"] += 1
                        P.op("dve", lambda e, cc=cc, di=di, j0=j0, nj=nj: e.tensor_tensor(
                            out=dg[di][:, 0:nj, :], in0=C.ident_bf[:].unsqueeze(1).broadcast_to([128, nj, 128]),
                            in1=W.convw[:, cc, j0:j0 + nj].unsqueeze(2).broadcast_to([128, nj, 128]), op=ALU.mult),
                            reads=["ident_bf", "e_convw"], writes=["dg%d" % di])
                        for jj in range(nj):
                            j = j0 + jj
                            P.op("pe", lambda e, cc=cc, di=di, j=j, jj=jj, cb=cb, ab=ab: e.matmul(
                                banks[cb][:], lhsT=dg[di][:, jj, :], rhs=ab[:, cc, j:j + 512],
                                start=(j == 0), stop=(j == 30)),
                                reads=["dg%d" % di, abk], writes=[bk(cb)])
                    P.op("act", lambda e, cc=cc, cb=cb: e.activation(
                        out=ycv[:, cc, :], in_=banks[cb][:], func=AF.Identity, bias=W.cvec[:, 0, cc:cc + 1]),
                        reads=[bk(cb), "e_cvec"], writes=[ycvk[cc]])
                P.op("act", lambda e: e.activation(out=ysq[:], in_=ycv[:], func=AF.Square), reads=ycvk, writes=["ysq"])
                for cc in range(4):
                    P.op("pe", lambda e, cc=cc: e.matmul(banks[3][:], lhsT=C.ones_f[:], rhs=ycv[:, cc, :],
                                                         start=(cc == 0), stop=(cc == 3)),
                         reads=[ycvk[cc], "ones_f"], writes=[bk(3)])
                for cc in range(4):
                    P.op("pe", lambda e, cc=cc: e.matmul(banks[4][:], lhsT=C.ones_f[:], rhs=ysq[:, cc, :],
                                                         start=(cc == 0), stop=(cc == 3)),
                         reads=["ysq", "ones_f"], writes=[bk(4)])
                P.op("act", lambda e: e.mul(out=st_mean[:], in_=banks[3][:], mul=1.0 / 512), reads=[bk(3)], writes=["st_mean"])
                P.op("act", lambda e: e.activation(out=st_b[:], in_=banks[3][:], func=AF.Square, scale=1.0 / 512),
                     reads=[bk(3)], writes=["n_ln"])
                P.op("dve", lambda e: e.scalar_tensor_tensor(out=st_b[:], in0=banks[4][:], scalar=1.0 / 512, in1=st_b[:],
                                                             op0=ALU.mult, op1=ALU.subtract),
                     reads=[bk(4), "n_ln"], writes=["n_ln"])
                P.op("act", lambda e: e.activation(out=st_c[:], in_=st_b[:], func=AF.Ln, bias=C.eps_col[:, 0:1]),
                     reads=["n_ln", "eps_col"], writes=["n_rs"])
                P.op("act", lambda e: e.activation(out=st_c[:], in_=st_c[:], func=AF.Exp, scale=-0.5),
                     reads=["n_rs"], writes=["n_rs"])
                for cc in range(4):
                    ti = cc % 2
                    P.op("dve", lambda e, cc=cc, ti=ti: e.tensor_tensor(out=tmpA[ti][:], in0=ycv[:, cc, :], in1=st_mean[:], op=ALU.subtract),
                         reads=[ycvk[cc], "st_mean"], writes=["n_t%d" % ti])
                    P.op("dve", lambda e, ti=ti: e.tensor_tensor(out=tmpA[ti][:], in0=tmpA[ti][:], in1=st_c[:], op=ALU.mult),
                         reads=["n_t%d" % ti, "n_rs"], writes=["n_t%d" % ti])
                    P.op("act", lambda e, cc=cc, ti=ti: e.activation(out=tmpB[:], in_=tmpA[ti][:], func=AF.Silu,
                                                                     scale=W.cvec[:, 1, cc:cc + 1], bias=W.cvec[:, 2, cc:cc + 1]),
                         reads=["n_t%d" % ti, "e_cvec"], writes=["tmpB"])
                    P.op("pool", lambda e, cc=cc, cs=cs, sp2=sp2: e.tensor_tensor(out=yaseg[:, cc, cs], in0=tmpB[:], in1=azb[sp2][:, cc, :], op=ALU.mult),
                         reads=["tmpB", "azb%d" % sp2], writes=["yaseg"])

            if DBG["even_stop"] <= 2:
                return
            for jl in range(4):
                for c2 in range(2):
                    bnk = 4 + (jl % 2) * 2 + c2
                    for ccl in range(2):
                        cc = c2 * 2 + ccl
                        for part in range(2):
                            r0 = (ccl * 2 + part) * 128
                            for s_ in range(8):
                                P.op("pe", lambda e, jl=jl, cc=cc, part=part, s_=s_, r0=r0, bnk=bnk: e.matmul(
                                    banks[bnk][:, r0:r0 + 128], lhsT=W.RW[32 * jl:32 * jl + 32, cc, s_, part, :],
                                    rhs=useg[32 * jl:32 * jl + 32, cc, s_:SEG:8],
                                    start=(s_ == 0), stop=(s_ == 7), tile_position=(32 * jl, 0), skip_group_check=True),
                                    reads=["useg", "su"], writes=[bk(bnk)])
                    j0 = (c2 * 2) * 4 + jl
                    v = banks[bnk][:].rearrange("p (a b c) -> p a b c", a=2, b=2)
                    P.op("act", lambda e, v=v, j0=j0: e.copy(out=XA[0][:, j0:j0 + 5:4, :], in_=v[:, :, 0, :]),
                         reads=[bk(bnk)], writes=XAk[0])
                    P.op("dve", lambda e, v=v, j0=j0: e.tensor_copy(out=XA[1][:, j0:j0 + 5:4, :], in_=v[:, :, 1, :]),
                         reads=[bk(bnk)], writes=XAk[1])
            if DBG["even_stop"] <= 3:
                return
            PP = Rec()
            Er, Ei = W.Er[:], W.Ei[:]
            big = lambda out, a, b_, op, rk, wk: PP.op(
                "dve", lambda e: e.tensor_tensor(out=out, in0=a, in1=b_, op=op), reads=rk + ["su"], writes=wk)
            ka0, ka1, kb0, kb1 = XAk[0], XAk[1], XBk[0], XBk[1]
            big(XB[0], Er, XA[0], ALU.mult, ka0, kb0)
            big(XB[1], Ei, XA[1], ALU.mult, ka1, kb1)
            big(XB[0], XB[0], XB[1], ALU.add, kb0 + kb1, kb0)
            big(XB[1], Er, XA[1], ALU.mult, ka1, kb1)
            big(XA[0], Ei, XA[0], ALU.mult, ka0, ka0)
            big(XB[1], XB[1], XA[0], ALU.subtract, kb1 + ka0, kb1)
            ck = ["cw"]
            sm = lambda out, a, b_, op, rk: PP.op(
                "dve", lambda e: e.tensor_tensor(out=out, in0=a, in1=b_, op=op), reads=rk + ck + ["su"], writes=ck)
            sm(cw1[:], Xl[0][:], W.U1r[:], ALU.mult, ["Xl"])
            sm(cw2[:], Xl[1][:], W.U1i[:], ALU.mult, ["Xl"])
            sm(zi0[:], cw1[:], cw2[:], ALU.subtract, [])
            sm(cw1[:], Xl[0][:], W.U1i[:], ALU.mult, ["Xl"])
            sm(cw2[:], Xl[1][:], W.U1r[:], ALU.mult, ["Xl"])
            sm(zi1[:], cw1[:], cw2[:], ALU.add, [])
            zin = [zi0, zi1]
            for j in range(16):
                for part in range(2):
                    PP.op("dve", lambda e, j=j, part=part: e.tensor_tensor_scan(
                        out=XA[part][:, j, :], data0=W.rho8[:, j:j + 1].broadcast_to([128, 128]),
                        data1=XB[part][:, j, :], initial=zin[part][:, j:j + 1], op0=ALU.mult, op1=ALU.add),
                        reads=XBk[part] + ck + ["su"], writes=XAk[part])
            big(XB[0], Er, XA[0], ALU.mult, ka0, kb0)
            big(XB[1], Ei, XA[1], ALU.mult, ka1, kb1)
            big(XB[0], XB[0], XB[1], ALU.subtract, kb0 + kb1, kb0)
            big(XB[1], Er, XA[1], ALU.mult, ka1, kb1)
            big(XA[0], Ei, XA[0], ALU.mult, ka0, ka0)
            big(XB[1], XB[1], XA[0], ALU.add, kb1 + ka0, kb1)
            src, srck = XB, XBk
            nxt = (s, sg_i + 1) if sg_i + 1 < 2 else ((s + 1, 0) if s + 1 < 2 else None)
            gn = norm_seg_gen(nxt[0], nxt[1], NT_alt) if nxt is not None else None
            if nxt is not None:
                prenormed.add(nxt)
            for oi_, op_ in enumerate(PP.ops):
                P.op(op_[0], op_[1], reads=op_[2], writes=op_[3])
                if gn is not None:
                    try:
                        next(gn)
                    except StopIteration:
                        gn = None
            if gn is not None:
                for _ in gn:
                    pass
            fin, fink = src, srck
            for part in range(2):
                P.op("pool", lambda e, part=part: e.tensor_copy(out=Xp[part][:, :, 0], in_=Xl[part][:]),
                     reads=["Xl"], writes=["Xp%d" % part])
                P.op("pool", lambda e, part=part, fin=fin: e.tensor_copy(out=Xp[part][:, :, 1:128], in_=fin[part][:, :, 0:127]),
                     reads=fink[part], writes=["Xp%d" % part])
            for part in range(2):
                P.op("pool", lambda e, part=part, fin=fin: e.tensor_copy(out=Xl[part][:], in_=fin[part][:, :, 127]),
                     reads=fink[part] + ["Xp0", "Xp1"], writes=["Xl"])

            if DBG["even_stop"] <= 4:
                return
            for sp2 in range(2):
                sp_i = sg_i * 2 + sp2
                c0 = sp2 * SPAN
                ch0 = sp2 * 64
                for cc in range(4):
                    yb = cc % 2
                    P.op("dve", lambda e, yb=yb: e.memset(banks[yb][:], 0.0), writes=[bk(yb)])
                    for t in range(8):
                        for tau in range(t + 1):
                            P.op("pe", lambda e, yb=yb, t=t, tau=tau, cc=cc, c0=c0: e.matmul(
                                banks[yb][:, t * 64:(t + 1) * 64], lhsT=W.KW[:, cc, tau, :],
                                rhs=useg[:, cc, c0 + t - tau:c0 + SPAN:8],
                                start=False, stop=False, skip_group_check=True),
                                reads=["useg", "su"], writes=[bk(yb)])
                    for t in range(8):
                        for jl in range(4):
                            j = cc * 4 + jl
                            for part in range(2):
                                P.op("pe", lambda e, yb=yb, t=t, jl=jl, j=j, part=part, ch0=ch0: e.matmul(
                                    banks[yb][32 * jl:32 * jl + 32, t * 64:(t + 1) * 64],
                                    lhsT=W.OW[:, j, t + 1, part, :], rhs=Xp[part][:, j, ch0:ch0 + 64],
                                    start=False, stop=False, tile_position=(0, 32 * jl), skip_group_check=True),
                                    reads=["Xp%d" % part, "su"], writes=[bk(yb)])
                    P.op("act", lambda e, yb=yb, cc=cc: e.activation(
                        out=ysb[:, cc, :].rearrange("p (c t) -> p t c", t=8),
                        in_=banks[yb][:].rearrange("p (t c) -> p t c", t=8), func=AF.Gelu),
                        reads=[bk(yb)], writes=["ysb"])
                if DBG["even_stop"] <= 5:
                    return
                for oc in range(4):
                    gb = 2 + oc % 2
                    for k in range(4):
                        P.op("pe", lambda e, oc=oc, k=k, gb=gb: e.matmul(
                            banks[gb][:], lhsT=W.gluw[:, k, oc * 128:(oc + 1) * 128], rhs=ysb[:, k, :],
                            start=(k == 0), stop=(k == 3)), reads=["ysb", "e_gluw"], writes=[bk(gb)])
                    P.op("act", lambda e, oc=oc, gb=gb: e.activation(out=gs[:], in_=banks[gb][:], func=AF.Sigmoid,
                                                                     bias=W.cvec[:, 4, oc:oc + 1]),
                         reads=[bk(gb), "e_cvec"], writes=["gs"])
                    P.op("dve", lambda e, oc=oc: e.tensor_tensor(out=tmpB[:], in0=gs[:], in1=ysb[:, oc, :], op=ALU.mult),
                         reads=["gs", "ysb"], writes=["tmpB"])
                    P.op("pool", lambda e, oc=oc, c0=c0: e.tensor_tensor(out=mixb[:, oc, :], in0=tmpB[:], in1=szseg[:, oc, c0:c0 + SPAN], op=ALU.mult),
                         reads=["tmpB", "szseg"], writes=["mixb"])
                if DBG["even_stop"] <= 6:
                    return
                P.dma("sp", lambda e, s=s, sp_i=sp_i: e.dma_start(
                    out=C.xt[:], in_=src_scr[s, :, :, sp_i * SPAN:(sp_i + 1) * SPAN]),
                    reads=[scr_key(src_scr, s, sp_i)], writes=["xt"])
                for oc in range(KC):
                    wt, wk = ring.get(cnt["w"])
                    cnt["w"] += 1
                    ob = 4 + oc % 2
                    for k in range(KC):
                        rhs = yaseg[:, k, c0:c0 + SPAN] if k < 4 else mixb[:, k - 4, :]
                        P.op("pe", lambda e, k=k, ob=ob, wt=wt, rhs=rhs: e.matmul(
                            banks[ob][:], lhsT=wt[:, k, :], rhs=rhs, start=(k == 0), stop=(k == KC - 1)),
                            reads=[wk, "yaseg", "mixb"], writes=[bk(ob)])
                    P.op("dve", lambda e, oc=oc, ob=ob, s=s: e.scalar_tensor_tensor(
                        out=C.xt[:, oc, :], in0=banks[ob][:], scalar=C.modS[:, l, s, 16 + oc:17 + oc],
                        in1=C.xt[:, oc, :], op0=ALU.mult, op1=ALU.add),
                        reads=[bk(ob), "modS", "xt"], writes=["xt"])
                P.dma("sp", lambda e, s=s, sp_i=sp_i: e.dma_start(
                    out=dst_scr[s, :, :, sp_i * SPAN:(sp_i + 1) * SPAN], in_=C.xt[:]),
                    reads=["xt"], writes=[scr_key(dst_scr, s, sp_i)])
                if DBG["even_stop"] <= 7:
                    return
            if DBG["even_stop"] <= 8:
                return
        if DBG["even_stop"] <= 9:
            return


_CACHE = {}


def make_in_maps(inputs):
    f32 = np.float32
    g = lambda k: np.asarray(inputs[k], f32)
    x = g("x")
    c = g("c")
    pos = np.asarray(inputs["positions"], np.int32)
    nb = x.shape[0] // 2
    shared = dict(host_consts())
    shared["modw"] = np.ascontiguousarray(g("mod_w"))
    shared["modb"] = np.ascontiguousarray(np.stack([colvec(g("mod_b")[l], 24) for l in range(4)], axis=1))
    shared["normw"] = np.ascontiguousarray(np.stack([colvec(g("norm_w")[l], 8) for l in range(4)], axis=1))
    aw = g("attn_w_in")
    shared["attw"] = np.ascontiguousarray(np.stack([blk_cols(aw[i]).reshape(80, 128, KC * 128) for i in range(2)]))
    awo = g("attn_w_out")
    shared["attwo"] = np.ascontiguousarray(np.stack([blk_cols(awo[i]).reshape(8, 128, KC * 128) for i in range(2)]))
    for nm, key in (("qnw", "attn_q_norm_w"), ("knw", "attn_k_norm_w")):
        w = g(key)
        wp = np.concatenate([w[:, 64:], w[:, :64]], axis=1)
        shared[nm] = np.ascontiguousarray(np.stack([w.T, wp.T], axis=2))
    ew = g("even_w_in")
    shared["evw"] = np.ascontiguousarray(np.stack([blk_cols(ew[i]).reshape(20, 128, KC * 128) for i in range(2)]))
    ewo = g("even_w_out")
    shared["evwo"] = np.ascontiguousarray(np.stack([blk_cols(ewo[i]).reshape(8, 128, KC * 128) for i in range(2)]))
    gw = g("ssm_glu_w")
    shared["gluw"] = np.ascontiguousarray(np.stack([gw[i].reshape(4, 128, 512).transpose(1, 0, 2) for i in range(2)]))
    cw = g("conv_dw_w")
    shared["convw"] = np.ascontiguousarray(np.stack([cw[i].T.reshape(4, 128, 31).transpose(1, 0, 2) for i in range(2)], axis=1))
    vecs = []
    for i in range(2):
        vecs.append(np.stack([colvec(g(k)[i], 4) for k in ("conv_dw_b", "conv_ln_w", "conv_ln_b", "ssm_d", "ssm_glu_b")], axis=1))
    shared["cvec"] = np.ascontiguousarray(np.stack(vecs, axis=1))

    def l1(a):
        t = a.reshape(4, 8, 64).transpose(1, 0, 2)
        return np.repeat(t[:, None], 16, axis=1).reshape(128, 4, 64)

    def l1b(b):
        return b.reshape(4, 8, 64, 16).transpose(1, 3, 0, 2).reshape(128, 4, 64)

    def l2(a):
        return a.reshape(16, 2, 64).transpose(1, 2, 0).reshape(128, 16)

    s5l1, s5l2a, s5l2b = [], [], []
    for i in range(2):
        ldt = np.broadcast_to(g("ssm_log_dt")[i][:, None], (32, 64))
        s5l1.append(np.stack([l1(g("ssm_lam_re")[i]), l1(g("ssm_lam_im")[i]), l1(ldt),
                              l1b(g("ssm_b_re")[i]), l1b(g("ssm_b_im")[i])], axis=1).reshape(128, 5, 256))
        s5l2a.append(np.stack([l2(g("ssm_lam_re")[i]), l2(g("ssm_lam_im")[i]), l2(ldt)], axis=1))
        bb = [g(k)[i].reshape(16, 2, 64, 16).transpose(1, 2, 0, 3).reshape(128, 256) for k in ("ssm_b_re", "ssm_b_im")]
        cc = [g(k)[i].reshape(16, 2, 16, 64).transpose(1, 3, 0, 2).reshape(128, 256) for k in ("ssm_c_re", "ssm_c_im")]
        s5l2b.append(np.stack(bb + cc, axis=1))
    shared["s5l1"] = np.ascontiguousarray(np.stack(s5l1, axis=1))
    shared["s5l2a"] = np.ascontiguousarray(np.stack(s5l2a, axis=1))
    shared["s5l2b"] = np.ascontiguousarray(np.stack(s5l2b, axis=1))
    maps = []
    for ci in range(nb):
        m = dict(shared)
        m["x"] = np.ascontiguousarray(x[2 * ci:2 * ci + 2])
        cc = c[2 * ci:2 * ci + 2]
        m["cT"] = np.ascontiguousarray(cc.reshape(2, KC, 128).transpose(2, 1, 0))
        m["pos"] = np.ascontiguousarray(pos[2 * ci:2 * ci + 2])
        maps.append(m)
    return maps


def kernel(**inputs):
    stages = ("xpose", 0, 1, 2, 3, "final")
    if stages not in _CACHE:
        _CACHE[stages] = build(stages)
    nc = _CACHE[stages]
    maps = make_in_maps(inputs)
    res = run_bass_kernel_spmd(nc, maps, core_ids=list(range(NCORES)))
    return np.concatenate([r["y"] for r in res.results], axis=0).astype(np.float32)
```
